# Optimizing a Trainium2 kernel written in Bass

```python
import jax
import jax.numpy as jnp
from jax import lax
import numpy as np


D_MODEL = 1024
BATCH = 32
SEQ = 2048
DEPTH = 4

GRID_W = 64
CTX_LEN = 256
CONV_W = 256
GMLP_W = 256
GMLP_GROUPS = 4
CHUNK = 128
N_Q_HEADS = 8
N_KV_HEADS = 2
HEAD_DIM = 64
Q_PER_KV = N_Q_HEADS // N_KV_HEADS
ATTN_W = N_Q_HEADS * HEAD_DIM
KV_W = N_KV_HEADS * HEAD_DIM
WINDOW = 128
BLOCK = 128
ROPE_BASE = 10000.0
N_BRANCH = 3
D_FF = 2816
N_MOD = 9
EPS = 1e-6
NEG_INF = -1e30

OFF_CB = 0
OFF_CC = OFF_CB + CONV_W
OFF_CH = OFF_CC + CONV_W
OFF_GU = OFF_CH + CONV_W
OFF_GV = OFF_GU + GMLP_W
OFF_Q = OFF_GV + GMLP_W
OFF_K = OFF_Q + ATTN_W
OFF_V = OFF_K + KV_W
OFF_GATE = OFF_V + KV_W
PROJ_W = OFF_GATE + N_BRANCH * D_MODEL

kernel_name = 'hybrid_gated_dit_trunk'


def rmsnorm(x, g):
    xf = x.astype(jnp.float32)
    y = xf * lax.rsqrt(jnp.mean(xf * xf, axis=-1, keepdims=True) + EPS)
    return (y * g.astype(jnp.float32)).astype(x.dtype)


def layernorm(x, g, b):
    xf = x.astype(jnp.float32)
    mu = jnp.mean(xf, axis=-1, keepdims=True)
    var = jnp.mean(jnp.square(xf - mu), axis=-1, keepdims=True)
    y = (xf - mu) * lax.rsqrt(var + EPS)
    return (y * g.astype(jnp.float32) + b.astype(jnp.float32)).astype(x.dtype)


def ada_in(t, g, shift, scale):
    return rmsnorm(t, g) * (1 + scale) + shift


def swiglu(z, w_in, w_out):
    a, u = jnp.split(z @ w_in, 2, axis=-1)
    return (jax.nn.silu(a) * u) @ w_out


def short_conv(z, w):
    zp = jnp.pad(z, ((0, 0), (1, 1), (0, 0)))
    return zp[:, :-2] * w[0] + zp[:, 1:-1] * w[1] + zp[:, 2:] * w[2]


def chunk_mlp(u, v, ln_g, ln_b, ws, bs):
    b, n, _ = u.shape
    u = jax.nn.gelu(u)
    v = layernorm(jax.nn.gelu(v), ln_g, ln_b)
    vc = v.reshape(b, n // CHUNK, CHUNK, GMLP_GROUPS, GMLP_W // GMLP_GROUPS)
    s = jnp.einsum('gpq,bcqgd->bcpgd', ws, vc) + bs.T[None, None, :, :, None]
    return u * s.reshape(b, n, GMLP_W)


def axial_rope(n, dtype):
    rows = n // GRID_W
    r, col = jnp.meshgrid(jnp.arange(rows), jnp.arange(GRID_W), indexing='ij')
    r = r.reshape(-1).astype(jnp.float32)
    col = col.reshape(-1).astype(jnp.float32)
    half = HEAD_DIM // 2
    inv = ROPE_BASE ** (-jnp.arange(0, half, 2, dtype=jnp.float32) / half)
    ang_r = r[:, None] * inv[None, :]
    ang_c = col[:, None] * inv[None, :]
    ang = jnp.concatenate([ang_r, ang_r, ang_c, ang_c], axis=-1)
    return jnp.cos(ang).astype(dtype), jnp.sin(ang).astype(dtype)


def apply_rope(x, cos, sin):
    xs = x.reshape(x.shape[:-1] + (2, 2, HEAD_DIM // 4))
    rot = jnp.stack([-xs[..., 1, :], xs[..., 0, :]], axis=-2).reshape(x.shape)
    return x * cos[None, :, None, :] + rot * sin[None, :, None, :]


def head_q(proj, q_g):
    q = proj[..., OFF_Q:OFF_K].reshape(proj.shape[:2] + (N_Q_HEADS, HEAD_DIM))
    return rmsnorm(q, q_g)


def head_kv(pkv, k_g):
    shp = pkv.shape[:2] + (N_KV_HEADS, HEAD_DIM)
    k = rmsnorm(pkv[..., :KV_W].reshape(shp), k_g)
    v = pkv[..., KV_W:].reshape(shp)
    return k, v


def windowed_attention(q, k, v, kc, vc, sink):
    b, n = q.shape[:2]
    nb = n // BLOCK
    scale = HEAD_DIM ** -0.5
    qb = q.reshape(b, nb, BLOCK, N_KV_HEADS, Q_PER_KV, HEAD_DIM).transpose(1, 0, 2, 3, 4, 5)
    pad = ((0, 0), (BLOCK, BLOCK), (0, 0), (0, 0))
    kp = jnp.pad(k, pad).reshape(b, nb + 2, BLOCK, N_KV_HEADS, HEAD_DIM)
    vp = jnp.pad(v, pad).reshape(b, nb + 2, BLOCK, N_KV_HEADS, HEAD_DIM)

    def band(t):
        return jnp.concatenate([t[:, :-2], t[:, 1:-1], t[:, 2:]], axis=2).transpose(1, 0, 2, 3, 4)

    ks, vs = band(kp), band(vp)
    blk = jnp.arange(nb)[:, None, None]
    qpos = blk * BLOCK + jnp.arange(BLOCK)[None, :, None]
    kpos = blk * BLOCK - BLOCK + jnp.arange(3 * BLOCK)[None, None, :]
    mask = (jnp.abs(kpos - qpos) <= WINDOW) & (kpos >= 0) & (kpos < n)
    sink_l = sink.astype(jnp.float32).reshape(1, N_KV_HEADS, Q_PER_KV, 1, 1)
    n_band = 3 * BLOCK

    def one_block(args):
        qblk, kblk, vblk, m = args
        s_lat = jnp.einsum('bqhgd,bkhd->bhgqk', qblk, kblk).astype(jnp.float32) * scale
        s_lat = jnp.where(m[None, None, None], s_lat, NEG_INF)
        s_ctx = jnp.einsum('bqhgd,bkhd->bhgqk', qblk, kc).astype(jnp.float32) * scale
        s_snk = jnp.broadcast_to(sink_l, s_lat.shape[:-1] + (1,))
        p = jax.nn.softmax(jnp.concatenate([s_lat, s_ctx, s_snk], axis=-1), axis=-1)
        p_lat = p[..., :n_band].astype(vblk.dtype)
        p_ctx = p[..., n_band:-1].astype(vblk.dtype)
        return (jnp.einsum('bhgqk,bkhd->bqhgd', p_lat, vblk)
                + jnp.einsum('bhgqk,bkhd->bqhgd', p_ctx, vc))

    o = lax.map(one_block, (qb, ks, vs, mask))
    return o.transpose(1, 0, 2, 3, 4, 5).reshape(b, n, ATTN_W)


def context_attention(q, k, v, sink):
    b, n = q.shape[:2]
    scale = HEAD_DIM ** -0.5
    qg = q.reshape(b, n, N_KV_HEADS, Q_PER_KV, HEAD_DIM)
    s = jnp.einsum('bqhgd,bkhd->bhgqk', qg, k).astype(jnp.float32) * scale
    s_snk = jnp.broadcast_to(sink.astype(jnp.float32).reshape(1, N_KV_HEADS, Q_PER_KV, 1, 1), s.shape[:-1] + (1,))
    p = jax.nn.softmax(jnp.concatenate([s, s_snk], axis=-1), axis=-1)[..., :-1].astype(v.dtype)
    return jnp.einsum('bhgqk,bkhd->bqhgd', p, v).reshape(b, n, ATTN_W)


def mixer_merge(proj, y_attn, conv_w, ln_g, ln_b, ws, bs, b_gate, w_bc, w_bg, w_ba, w_o):
    y_conv = proj[..., OFF_CB:OFF_CC] * short_conv(proj[..., OFF_CC:OFF_CH] * proj[..., OFF_CH:OFF_GU], conv_w)
    y_gmlp = chunk_mlp(proj[..., OFF_GU:OFF_GV], proj[..., OFF_GV:OFF_Q], ln_g, ln_b, ws, bs)
    gates = jax.nn.sigmoid(proj[..., OFF_GATE:].reshape(proj.shape[:2] + (N_BRANCH, D_MODEL)) + b_gate)
    merged = (gates[..., 0, :] * (y_conv @ w_bc)
              + gates[..., 1, :] * (y_gmlp @ w_bg)
              + gates[..., 2, :] * (y_attn @ w_ba))
    return merged @ w_o


def setup_inputs(seed: int = 0) -> dict:
    key = jax.random.key(seed)
    ks = jax.random.split(key, 24)

    def nrm(k, shape, scale):
        return jax.random.normal(k, shape, jnp.float32) * scale

    L = DEPTH
    return {
        'x': nrm(ks[0], (BATCH, SEQ, D_MODEL), 1.0),
        'c': nrm(ks[1], (BATCH, D_MODEL), 1.0),
        'ctx': nrm(ks[2], (BATCH, CTX_LEN, D_MODEL), 1.0),
        'c_ctx': nrm(ks[3], (D_MODEL,), 1.0),
        'w_mod': nrm(ks[4], (L, D_MODEL, N_MOD * D_MODEL), 0.5 * D_MODEL ** -0.5),
        'b_mod': nrm(ks[5], (L, N_MOD * D_MODEL), 0.02),
        'norm_g': 1.0 + nrm(ks[6], (L, 3, D_MODEL), 0.02),
        'ffn_w_in': nrm(ks[7], (L, 2, D_MODEL, 2 * D_FF), D_MODEL ** -0.5),
        'ffn_w_out': nrm(ks[8], (L, 2, D_FF, D_MODEL), D_FF ** -0.5),
        'w_in': nrm(ks[9], (L, D_MODEL, PROJ_W), D_MODEL ** -0.5),
        'b_gate': nrm(ks[10], (L, N_BRANCH, D_MODEL), 0.02),
        'conv_w': nrm(ks[11], (L, 3, CONV_W), 3 ** -0.5),
        'gmlp_ln_g': 1.0 + nrm(ks[12], (L, GMLP_W), 0.02),
        'gmlp_ln_b': nrm(ks[13], (L, GMLP_W), 0.02),
        'gmlp_ws': nrm(ks[14], (L, GMLP_GROUPS, CHUNK, CHUNK), CHUNK ** -0.5),
        'gmlp_bs': 1.0 + nrm(ks[15], (L, GMLP_GROUPS, CHUNK), 0.02),
        'q_norm_g': 1.0 + nrm(ks[16], (L, HEAD_DIM), 0.02),
        'k_norm_g': 1.0 + nrm(ks[17], (L, HEAD_DIM), 0.02),
        'attn_sink': nrm(ks[18], (L, N_Q_HEADS), 0.5),
        'w_branch_conv': nrm(ks[19], (L, CONV_W, D_MODEL), CONV_W ** -0.5),
        'w_branch_gmlp': nrm(ks[20], (L, GMLP_W, D_MODEL), GMLP_W ** -0.5),
        'w_branch_attn': nrm(ks[21], (L, ATTN_W, D_MODEL), ATTN_W ** -0.5),
        'w_out': nrm(ks[22], (L, D_MODEL, D_MODEL), D_MODEL ** -0.5),
    }


def reference(x, c, ctx, c_ctx, w_mod, b_mod, norm_g, ffn_w_in, ffn_w_out, w_in, b_gate, conv_w,
              gmlp_ln_g, gmlp_ln_b, gmlp_ws, gmlp_bs, q_norm_g, k_norm_g, attn_sink,
              w_branch_conv, w_branch_gmlp, w_branch_attn, w_out):
    bsz, n, _ = x.shape
    cos, sin = axial_rope(n, x.dtype)
    sc = jax.nn.silu(c)
    scc = jax.nn.silu(c_ctx)
    h, hc = x, ctx
    for l in range(DEPTH):
        last = l == DEPTH - 1
        mx = (sc @ w_mod[l] + b_mod[l]).reshape(bsz, 1, N_MOD, D_MODEL)
        mc = (scc @ w_mod[l] + b_mod[l]).reshape(N_MOD, D_MODEL)
        branch = (conv_w[l], gmlp_ln_g[l], gmlp_ln_b[l], gmlp_ws[l], gmlp_bs[l], b_gate[l],
                  w_branch_conv[l], w_branch_gmlp[l], w_branch_attn[l], w_out[l])

        h = h + 0.5 * mx[:, :, 2] * swiglu(ada_in(h, norm_g[l, 0], mx[:, :, 0], mx[:, :, 1]), ffn_w_in[l, 0], ffn_w_out[l, 0])
        hc = hc + 0.5 * mc[2] * swiglu(ada_in(hc, norm_g[l, 0], mc[0], mc[1]), ffn_w_in[l, 0], ffn_w_out[l, 0])

        zx = ada_in(h, norm_g[l, 1], mx[:, :, 3], mx[:, :, 4])
        zc = ada_in(hc, norm_g[l, 1], mc[3], mc[4])
        px = zx @ w_in[l]
        qx = apply_rope(head_q(px, q_norm_g[l]), cos, sin)
        kx, vx = head_kv(px[..., OFF_K:OFF_GATE], k_norm_g[l])
        kx = apply_rope(kx, cos, sin)
        if last:
            kc, vc = head_kv(zc @ w_in[l][:, OFF_K:OFF_GATE], k_norm_g[l])
        else:
            pc = zc @ w_in[l]
            kc, vc = head_kv(pc[..., OFF_K:OFF_GATE], k_norm_g[l])
        ya = windowed_attention(qx, kx, vx, kc, vc, attn_sink[l])
        h = h + mx[:, :, 5] * mixer_merge(px, ya, *branch)
        if not last:
            yc = context_attention(head_q(pc, q_norm_g[l]), kc, vc, attn_sink[l])
            hc = hc + mc[5] * mixer_merge(pc, yc, *branch)

        h = h + 0.5 * mx[:, :, 8] * swiglu(ada_in(h, norm_g[l, 2], mx[:, :, 6], mx[:, :, 7]), ffn_w_in[l, 1], ffn_w_out[l, 1])
        if not last:
            hc = hc + 0.5 * mc[8] * swiglu(ada_in(hc, norm_g[l, 2], mc[6], mc[7]), ffn_w_in[l, 1], ffn_w_out[l, 1])
    return h
```

```python
import numpy as np
from contextlib import ExitStack
import concourse.bass as bass
import concourse.mybir as mybir
from concourse.bass_utils import run_bass_kernel_spmd

F32 = mybir.dt.float32
BF16 = mybir.dt.bfloat16
AF = mybir.ActivationFunctionType
ALU = mybir.AluOpType

D = 1024
NCH = 8
SEQ = 2048
CTX = 256
T = SEQ + CTX
DEPTH = 4
DFF = 2816
NFF = 22
NMOD = 9
PROJ_W = 5120
EPS = 1e-6
NCORES = 8
NB = 4
OFF_CB, OFF_CC, OFF_CH, OFF_GU, OFF_GV, OFF_Q, OFF_K, OFF_V, OFF_GATE = 0, 256, 512, 768, 1024, 1280, 1792, 1920, 2048

TILES = [(0, 512, False), (512, 512, False), (1024, 512, False), (1536, 512, False), (2048, 256, True)]


class TT:
    __slots__ = ("w", "r", "psum")

    def __init__(self, psum=False):
        self.w = None
        self.r = []
        self.psum = psum


class DmaSem:
    def __init__(self, sem):
        self.sem = sem
        self.issued = 0


class Op:
    __slots__ = ("eng", "fn", "waits", "signal", "seq", "dma")

    def __init__(self, eng, fn, seq, dma=None):
        self.eng = eng
        self.fn = fn
        self.waits = []
        self.signal = False
        self.seq = seq
        self.dma = dma


class Sched:
    CENG = ("pe", "act", "dve", "pool")

    def __init__(self, nc, stack):
        self.nc = nc
        self.E = {"pe": nc.tensor, "act": nc.scalar, "dve": nc.vector, "pool": nc.gpsimd, "sp": nc.sync}
        self.sem = {e: stack.enter_context(nc.semaphore("sem_" + e)) for e in self.CENG}
        self.count = {e: 0 for e in self.CENG}
        self.nops = {e: 0 for e in self.CENG}
        self.ops = {e: {} for e in self.CENG}
        self.sigval = {e: {} for e in self.CENG}
        self.pending = []
        self.waited = {e: {s: -1 for s in self.CENG} for e in self.E}
        self.waited_dma = {e: {} for e in self.E}
        self.dsems = []
        self.stack = stack
        self.n_instr = 0

    def dma_sem(self, name):
        d = DmaSem(self.stack.enter_context(self.nc.semaphore(name)))
        self.dsems.append(d)
        return d

    def _add_dep(self, op, ev):
        e = op.eng
        if ev[0] == "c":
            _, src, s = ev
            if src == e and e == "pe":
                return
            if s <= self.waited[e][src]:
                return
            self.waited[e][src] = s
            op.waits.append(ev)
            self.ops[src][s].signal = True
        else:
            d = ev[1]
            val = d.issued
            if self.waited_dma[e].get(d, 0) >= val:
                return
            self.waited_dma[e][d] = val
            op.waits.append(("d", d, val))

    def op(self, eng, fn, reads=(), writes=(), dma=None):
        if dma is None:
            seq = self.nops[eng]
            self.nops[eng] += 1
            op = Op(eng, fn, seq)
            self.ops[eng][seq] = op
            ev = ("c", eng, seq)
        else:
            op = Op(eng, fn, -1, dma)
            ev = ("d", dma)
        deps = []
        for t in reads:
            if t.w is not None:
                deps.append(t.w)
            if t.psum:
                deps.extend(r for r in t.r if r[0] == "c" and r[1] != eng)
        for t in writes:
            if t.w is not None:
                deps.append(t.w)
            deps.extend(t.r)
        for dv in deps:
            self._add_dep(op, dv)
        if dma is not None:
            dma.issued += 16
        for t in reads:
            t.r.append(ev)
        for t in writes:
            t.w = ev
            t.r = []
        self.pending.append(op)
        return op

    def flush(self):
        for op in self.pending:
            eng = self.E[op.eng]
            for w in op.waits:
                if w[0] == "c":
                    eng.wait_ge(self.sem[w[1]], self.sigval[w[1]][w[2]])
                else:
                    eng.wait_ge(w[1].sem, w[2])
            ins = op.fn()
            self.n_instr += 1
            if op.dma is not None:
                ins.then_inc(op.dma.sem, 16)
            elif op.signal:
                self.count[op.eng] += 1
                self.sigval[op.eng][op.seq] = self.count[op.eng]
                ins.then_inc(self.sem[op.eng], 1)
        self.pending = []
        for e in self.CENG:
            self.ops[e] = {}

    def barrier(self):
        for e in self.CENG:
            if self.nops[e] > 0:
                last = self.nops[e] - 1
                if last in self.ops[e]:
                    self.ops[e][last].signal = True
        self.flush()
        for e, eng in self.E.items():
            for s in self.CENG:
                if self.count[s] > 0 and not (s == e and e == "pe"):
                    eng.wait_ge(self.sem[s], self.count[s])
                if self.nops[s] > 0:
                    self.waited[e][s] = self.nops[s] - 1
            for d in self.dsems:
                if d.issued > self.waited_dma[e].get(d, 0):
                    eng.wait_ge(d.sem, d.issued)
                    self.waited_dma[e][d] = d.issued


def ffn_groups():
    return [(0, 4), (4, 4), (8, 4), (12, 4), (16, 4), (20, 2)]


def build_nc(cfg):
    nb = cfg.get("nb", NB)
    layers = cfg.get("layers", list(range(DEPTH)))
    subs = cfg.get("subs", ("ffn1", "mix", "ffn2"))
    nc = bass.Bass("TRN2", target_bir_lowering=False)

    def din(name, shape, dt=F32):
        return nc.dram_tensor(name, list(shape), dt, kind="ExternalInput").ap()

    xT = din("xT", [nb, D, SEQ])
    ctxT = din("ctxT", [nb, D, CTX])
    cT = din("cT", [128, NCH * 5])
    bmodT = din("bmodT", [128, DEPTH * 72])
    ngT = din("ngT", [128, DEPTH * 3 * NCH])
    w_mod = din("w_mod", [DEPTH, D, NMOD * D])
    ffn_w_in = din("ffn_w_in", [DEPTH, 2, D, 2 * DFF])
    ffn_w_out = din("ffn_w_out", [DEPTH, 2, DFF, D])
    w_in = din("w_in", [DEPTH, D, PROJ_W])
    w_bc = din("w_bc", [DEPTH, 256, D])
    w_bg = din("w_bg", [DEPTH, 256, D])
    w_ba = din("w_ba", [DEPTH, 512, D])
    w_o = din("w_o", [DEPTH, D, D])
    bgT = din("bgT", [128, DEPTH * 3 * NCH])
    cwT = din("cwT", [128, DEPTH * 2 * 3])
    lnB_d = din("lnB", [DEPTH, 128, 512])
    wsT_d = din("wsT", [DEPTH, 128, 512])
    bsB_d = din("bsB", [DEPTH, 128, 256])
    qkg_d = din("qkg", [DEPTH, 128, 640])
    sinkB_d = din("sinkB", [DEPTH, 128, 8])
    ropeC_d = din("ropeC", [128, 16 * 64])
    ropeS_d = din("ropeS", [128, 16 * 64])
    maskP_d = din("maskP", [128, 128])
    maskN_d = din("maskN", [128, 128])
    ident_d = din("ident", [128, 128])
    yT = nc.dram_tensor("yT", [nb, D, SEQ], F32, kind="ExternalOutput").ap()
    dumped = []
    if cfg.get("dump_y"):
        dbgY = nc.dram_tensor("dbgY", [128, NCH * T], F32, kind="ExternalOutput").ap()
        dbgZ = nc.dram_tensor("dbgZ", [128, NCH * T], F32, kind="ExternalOutput").ap()

    _uid = [0]

    def uname(name):
        _uid[0] += 1
        return "s%d_%s" % (_uid[0], name)

    with ExitStack() as st:
        S = Sched(nc, st)

        def sb(name, shape, dt):
            return st.enter_context(nc.sbuf_tensor(uname(name), list(shape), dt))

        H = sb("H", [128, NCH * T], F32)
        Hv = H[:].rearrange("p (c t) -> p c t", c=NCH)
        modT = sb("modT", [128, DEPTH * 72 * 5], F32)
        modv = modT[:].rearrange("p (l m c j) -> p l m c j", l=DEPTH, m=NMOD, c=NCH)
        ps = [st.enter_context(nc.psum_tensor("ps%d" % i, [128, 512], F32)) for i in range(8)]
        pst = [TT(psum=True) for _ in range(8)]
        hT = [[TT() for _ in TILES] for _ in range(NCH)]
        t_mod = TT()
        t_const = TT()
        d_w = [S.dma_sem("dw%d" % i) for i in range(2)]
        d_misc = S.dma_sem("dmisc")
        d_x = S.dma_sem("dx")
        d_out = S.dma_sem("dout")

        d_misc2 = S.dma_sem("dmisc2")
        ones_b = sb("ones_b", [128, 128], BF16)
        S.op("dve", lambda: nc.vector.memset(ones_b[:], 1.0), writes=[t_const])
        ident = sb("ident", [128, 128], BF16)
        maskP = sb("maskP", [128, 128], BF16)
        maskN = sb("maskN", [128, 128], BF16)
        bg_s = sb("bg_s", [128, DEPTH * 3 * NCH], F32)
        cw_s = sb("cw_s", [128, DEPTH * 2 * 3], F32)
        S.op("pool", lambda: nc.gpsimd.dma_start(out=ident[:], in_=ident_d), writes=[t_const], dma=d_misc2)
        S.op("pool", lambda: nc.gpsimd.dma_start(out=maskP[:], in_=maskP_d), writes=[t_const], dma=d_misc2)
        S.op("pool", lambda: nc.gpsimd.dma_start(out=maskN[:], in_=maskN_d), writes=[t_const], dma=d_misc2)
        S.op("sp", lambda: nc.sync.dma_start(out=bg_s[:], in_=bgT), writes=[t_const], dma=d_misc)
        S.op("sp", lambda: nc.sync.dma_start(out=cw_s[:], in_=cwT), writes=[t_const], dma=d_misc)

        with ExitStack() as ps_st:
            def psb(name, shape, dt):
                return ps_st.enter_context(nc.sbuf_tensor(uname(name), list(shape), dt))
            cT_s = psb("cT_s", [128, NCH * 5], F32)
            scT = psb("scT", [128, NCH * 5], BF16)
            bm_s = psb("bm_s", [128, DEPTH * 72], F32)
            ng_s = psb("ng_s", [128, DEPTH * 3 * NCH], F32)
            wm = [psb("wm%d" % i, [128, NCH * 1024], BF16) for i in range(2)]
            t_c, t_sc, t_bm, t_ng = TT(), TT(), TT(), TT()
            t_wm = [TT(), TT()]
            S.op("sp", lambda: nc.sync.dma_start(out=cT_s[:], in_=cT), writes=[t_c], dma=d_misc)
            S.op("sp", lambda: nc.sync.dma_start(out=bm_s[:], in_=bmodT), writes=[t_bm], dma=d_misc)
            S.op("sp", lambda: nc.sync.dma_start(out=ng_s[:], in_=ngT), writes=[t_ng], dma=d_misc)
            S.op("act", lambda: nc.scalar.activation(out=scT[:], in_=cT_s[:], func=AF.Silu), reads=[t_c], writes=[t_sc])
            scv = scT[:].rearrange("p (k j) -> p k j", k=NCH)
            blk = 0
            for l in range(DEPTH):
                wsrc = w_mod[l].rearrange("(k p) n -> p k n", p=128)
                for cb in range(9):
                    s = blk % 2
                    blk += 1
                    wv = wm[s][:].rearrange("p (k n) -> p k n", k=NCH)
                    S.op("pool", (lambda wv=wv, wsrc=wsrc, cb=cb: nc.gpsimd.dma_start(out=wv, in_=wsrc[:, :, cb * 1024:(cb + 1) * 1024])),
                         writes=[t_wm[s]], dma=d_w[s])
                    pt = ps[cb % 2]
                    for oc in range(8):
                        for k in range(NCH):
                            S.op("pe", (lambda pt=pt, wv=wv, oc=oc, k=k: nc.tensor.matmul(
                                pt[:, oc * 5:(oc + 1) * 5], lhsT=wv[:, k, oc * 128:(oc + 1) * 128], rhs=scv[:, k, :],
                                start=(k == 0), stop=(k == NCH - 1))),
                                reads=[t_wm[s], t_sc], writes=[pst[cb % 2]])
                    bsl = bm_s[:, l * 72 + cb * 8: l * 72 + cb * 8 + 8]
                    S.op("dve", (lambda pt=pt, l=l, cb=cb, bsl=bsl: nc.vector.tensor_tensor(
                        out=modv[:, l, cb, :, :], in0=pt[:, 0:40].rearrange("p (c j) -> p c j", c=NCH),
                        in1=bsl.unsqueeze(2).broadcast_to([128, NCH, 5]), op=ALU.add)),
                        reads=[pst[cb % 2], t_bm], writes=[t_mod])
                for i in range(3):
                    ngs = ng_s[:, (l * 3 + i) * NCH:(l * 3 + i + 1) * NCH]
                    S.op("dve", (lambda l=l, i=i, ngs=ngs: nc.vector.scalar_tensor_tensor(
                        out=modv[:, l, 3 * i + 1, :, :], in0=modv[:, l, 3 * i + 1, :, :], scalar=1.0,
                        in1=ngs.unsqueeze(2).broadcast_to([128, NCH, 5]), op0=ALU.add, op1=ALU.mult)),
                        reads=[t_ng, t_mod], writes=[t_mod])
                for i in (0, 2):
                    S.op("dve", (lambda l=l, i=i: nc.vector.tensor_scalar(
                        out=modv[:, l, 3 * i + 2, :, :], in0=modv[:, l, 3 * i + 2, :, :], scalar1=0.5, scalar2=None,
                        op0=ALU.mult)), reads=[t_mod], writes=[t_mod])
            S.barrier()

        def mcol(l, m, c, j):
            return modv[:, l, m, c, j:j + 1]

        def emit_norm(l, i, j_of_tile, tiles, zv, zT, tmp, zdst=None, pbf=None):
            if zdst is None:
                zdst = lambda c, ti: (zv[:, c, TILES[ti][0]:TILES[ti][0] + TILES[ti][1]], zT[c][ti])
            for ti in tiles:
                t0, w, isctx = TILES[ti]
                j = j_of_tile(ti)
                pb = (4 + (ti % 2)) if pbf is None else pbf(ti)
                for c in range(NCH):
                    sq, tsq = tmp["sq"][c % 2]
                    S.op("act", (lambda sq=sq, c=c, t0=t0, w=w: nc.scalar.activation(
                        out=sq[:, 0:w], in_=Hv[:, c, t0:t0 + w], func=AF.Square)), reads=[hT[c][ti]], writes=[tsq])
                    S.op("pe", (lambda sq=sq, c=c, w=w, pb=pb: nc.tensor.matmul(
                        ps[pb][:, 0:w], lhsT=ones_b[:], rhs=sq[:, 0:w], start=(c == 0), stop=(c == NCH - 1))),
                        reads=[tsq, t_const], writes=[pst[pb]])
                sd, tsd = tmp["sd"]
                rs, trs = tmp["rstd"][ti % 2]
                S.op("act", (lambda sd=sd, w=w, pb=pb: nc.scalar.activation(
                    out=sd[:, 0:w], in_=ps[pb][:, 0:w], func=AF.Ln, bias=tmp["eps"][:, 0:1], scale=1.0 / D)),
                    reads=[pst[pb], t_const], writes=[tsd])
                S.op("act", (lambda sd=sd, rs=rs, w=w: nc.scalar.activation(out=rs[:, 0:w], in_=sd[:, 0:w], func=AF.Exp, scale=-0.5)),
                     reads=[tsd], writes=[trs])
                for c in range(NCH):
                    tb, ttb = tmp["t"][c % 2]
                    S.op("dve", (lambda tb=tb, rs=rs, c=c, t0=t0, w=w: nc.vector.tensor_tensor(
                        out=tb[:, 0:w], in0=Hv[:, c, t0:t0 + w], in1=rs[:, 0:w], op=ALU.mult)),
                        reads=[hT[c][ti], trs], writes=[ttb])
                    zap, ztt = zdst(c, ti)
                    S.op("act", (lambda tb=tb, c=c, w=w, j=j, zap=zap: nc.scalar.activation(
                        out=zap, in_=tb[:, 0:w], func=AF.Identity,
                        bias=mcol(l, 3 * i, c, j), scale=mcol(l, 3 * i + 1, c, j))),
                        reads=[ttb, t_mod], writes=[ztt])

        def emit_ffn(l, fi, b, tiles):
            i = 0 if fi == 0 else 2
            with ExitStack() as fs:
                def fsb(name, shape, dt):
                    return fs.enter_context(nc.sbuf_tensor(uname(name), list(shape), dt))
                Z = fsb("Z", [128, NCH * T], BF16)
                zv = Z[:].rearrange("p (c t) -> p c t", c=NCH)
                zT = [[TT() for _ in TILES] for _ in range(NCH)]
                W = [fsb("W%d" % s, [128, 3 * 4096], BF16) for s in range(2)]
                tW = [TT(), TT()]
                G = [[(fsb("g%d_%d" % (p, j), [128, 512], BF16), TT()) for j in range(4)] for p in range(2)]
                SA = [(fsb("sa%d" % k, [128, 512], F32), TT()) for k in range(2)]
                epsb = fsb("epsb", [128, 1], F32)
                tmp = {
                    "sq": [(fsb("sq%d" % k, [128, 512], BF16), TT()) for k in range(2)],
                    "t": [(fsb("tt%d" % k, [128, 512], F32), TT()) for k in range(2)],
                    "sd": (fsb("sd", [128, 512], F32), TT()),
                    "rstd": [(fsb("rstd%d" % k, [128, 512], F32), TT()) for k in range(2)],
                    "eps": epsb,
                }
                S.op("dve", lambda: nc.vector.memset(epsb[:], EPS), writes=[t_const])
                jf = (lambda ti: 4 if TILES[ti][2] else b)
                win = ffn_w_in[l, fi].rearrange("(k p) n -> p k n", p=128)
                wout = ffn_w_out[l, fi].rearrange("(f p) n -> p f n", p=128)
                groups = ffn_groups()

                def load_group(gi):
                    g0, G_ = groups[gi]
                    s = gi % 2
                    n = G_ * 128
                    wa = W[s][:, 0:NCH * n].rearrange("p (k n) -> p k n", k=NCH)
                    wu = W[s][:, 4096:4096 + NCH * n].rearrange("p (k n) -> p k n", k=NCH)
                    wo = W[s][:, 8192:8192 + G_ * 1024].rearrange("p (f n) -> p f n", f=G_)
                    S.op("pool", (lambda: nc.gpsimd.dma_start(out=wa, in_=win[:, :, g0 * 128:g0 * 128 + n])), writes=[tW[s]], dma=d_w[s])
                    S.op("pool", (lambda: nc.gpsimd.dma_start(out=wu, in_=win[:, :, DFF + g0 * 128:DFF + g0 * 128 + n])), writes=[tW[s]], dma=d_w[s])
                    S.op("pool", (lambda: nc.gpsimd.dma_start(out=wo, in_=wout[:, g0:g0 + G_, :])), writes=[tW[s]], dma=d_w[s])
                    return wa, wu, wo

                wviews = {0: load_group(0)}
                emit_norm(l, i, jf, tiles[0:1], zv, zT, tmp)

                def emit_au(gi, ti, par):
                    g0, G_ = groups[gi]
                    s = gi % 2
                    wa, wu, wo = wviews[gi]
                    t0, w, isctx = TILES[ti]
                    for jj in range(G_):
                        pa, pu = jj % 2, 2 + jj % 2
                        for k in range(NCH):
                            S.op("pe", (lambda k=k, jj=jj, pa=pa: nc.tensor.matmul(
                                ps[pa][:, 0:w], lhsT=wa[:, k, jj * 128:(jj + 1) * 128], rhs=zv[:, k, t0:t0 + w],
                                start=(k == 0), stop=(k == NCH - 1))), reads=[tW[s], zT[k][ti]], writes=[pst[pa]])
                        for k in range(NCH):
                            S.op("pe", (lambda k=k, jj=jj, pu=pu: nc.tensor.matmul(
                                ps[pu][:, 0:w], lhsT=wu[:, k, jj * 128:(jj + 1) * 128], rhs=zv[:, k, t0:t0 + w],
                                start=(k == 0), stop=(k == NCH - 1))), reads=[tW[s], zT[k][ti]], writes=[pst[pu]])
                        sa, tsa = SA[jj % 2]
                        gb, tgb = G[par][jj]
                        S.op("act", (lambda sa=sa, pa=pa: nc.scalar.activation(out=sa[:, 0:w], in_=ps[pa][:, 0:w], func=AF.Silu)),
                             reads=[pst[pa]], writes=[tsa])
                        S.op("dve", (lambda sa=sa, gb=gb, pu=pu: nc.vector.tensor_tensor(
                            out=gb[:, 0:w], in0=sa[:, 0:w], in1=ps[pu][:, 0:w], op=ALU.mult)),
                            reads=[tsa, pst[pu]], writes=[tgb])

                def emit_o(gi, ti, par):
                    g0, G_ = groups[gi]
                    s = gi % 2
                    wa, wu, wo = wviews[gi]
                    t0, w, isctx = TILES[ti]
                    j = jf(ti)
                    for c in range(NCH):
                        po = 4 + c % 4
                        for jj in range(G_):
                            gb, tgb = G[par][jj]
                            S.op("pe", (lambda c=c, jj=jj, gb=gb, po=po: nc.tensor.matmul(
                                ps[po][:, 0:w], lhsT=wo[:, jj, c * 128:(c + 1) * 128], rhs=gb[:, 0:w],
                                start=(jj == 0), stop=(jj == G_ - 1))), reads=[tW[s], tgb], writes=[pst[po]])
                        S.op("dve", (lambda c=c, po=po: nc.vector.scalar_tensor_tensor(
                            out=Hv[:, c, t0:t0 + w], in0=ps[po][:, 0:w], scalar=mcol(l, 3 * i + 2, c, j),
                            in1=Hv[:, c, t0:t0 + w], op0=ALU.mult, op1=ALU.add)),
                            reads=[pst[po], hT[c][ti], t_mod], writes=[hT[c][ti]])

                steps = [(gi, ti) for gi in range(len(groups)) for ti in tiles]
                prev = None
                for n, (gi, ti) in enumerate(steps):
                    if gi == 0 and n + 1 < len(tiles):
                        emit_norm(l, i, jf, tiles[n + 1:n + 2], zv, zT, tmp)
                    emit_au(gi, ti, n % 2)
                    if prev is not None:
                        emit_o(prev[0], prev[1], prev[2])
                    if ti == tiles[0] and gi + 1 < len(groups):
                        wviews[gi + 1] = load_group(gi + 1)
                    prev = (gi, ti, n % 2)
                emit_o(prev[0], prev[1], prev[2])
                S.barrier()

        def emit_mixer(l, b, last):
            jf = (lambda ti: 4 if TILES[ti][2] else b)
            tiles1 = [4, 0, 1, 2, 3]
            tiles2 = [0, 1, 2, 3] if last else [4, 0, 1, 2, 3]
            with ExitStack() as ms:
                def msb(name, shape, dt):
                    return ms.enter_context(nc.sbuf_tensor(uname(name), list(shape), dt))
                Y = msb("Y", [128, NCH * T], BF16)
                yv = Y[:].rearrange("p (c t) -> p c t", c=NCH)
                NBLK = T // 128
                yT = [[TT() for _ in range(NBLK)] for _ in range(NCH)]
                epsb = msb("epsb_m", [128, 1], F32)
                S.op("dve", lambda: nc.vector.memset(epsb[:], EPS), writes=[t_const])
                qkg = msb("qkg", [128, 640], F32)
                esk = msb("esk", [128, 8], F32)
                t_lp = TT()
                S.op("sp", lambda: nc.sync.dma_start(out=qkg[:], in_=qkg_d[l]), writes=[t_lp], dma=d_misc)
                S.op("sp", lambda: nc.sync.dma_start(out=esk[:], in_=sinkB_d[l]), writes=[t_lp], dma=d_misc)
                S.op("act", lambda: nc.scalar.activation(out=esk[:], in_=esk[:], func=AF.Exp), reads=[t_lp], writes=[t_lp])
                qkgv = qkg[:].rearrange("p (h d) -> p h d", h=10)

                def ycols(tb):
                    return slice(tb * 128, (tb + 1) * 128)

                with ExitStack() as p1:
                    def sb1(name, shape, dt):
                        return p1.enter_context(nc.sbuf_tensor(uname(name), list(shape), dt))
                    ropeC = sb1("ropeC", [128, 16 * 64], F32)
                    ropeS = sb1("ropeS", [128, 16 * 64], F32)
                    t_rope = TT()
                    S.op("sp", lambda: nc.sync.dma_start(out=ropeC[:], in_=ropeC_d), writes=[t_rope], dma=d_misc)
                    S.op("sp", lambda: nc.sync.dma_start(out=ropeS[:], in_=ropeS_d), writes=[t_rope], dma=d_misc)
                    kT = sb1("kT", [128, T], BF16)
                    Vt = sb1("Vt", [128, NBLK * 128], BF16)
                    Vv = Vt[:].rearrange("p (b n) -> p b n", b=NBLK)
                    kT_t = [TT() for _ in range(NBLK)]
                    V_t = [TT() for _ in range(NBLK)]
                    W2 = sb1("W2", [128, NCH * 768], BF16)
                    W2v = W2[:].rearrange("p (k n) -> p k n", k=NCH)
                    t_W2 = TT()
                    S.op("pool", lambda: nc.gpsimd.dma_start(out=W2v, in_=w_in[l].rearrange("(k p) n -> p k n", p=128)[:, :, OFF_Q:OFF_GATE]),
                         writes=[t_W2], dma=d_w[0])
                    ZT = [sb1("zt%d" % k, [128, NCH * 512], BF16) for k in range(2)]
                    ZTv = [z[:].rearrange("p (c t) -> p c t", c=NCH) for z in ZT]
                    ZT_t = [[TT() for _ in range(NCH)] for _ in range(2)]
                    tmp = {
                        "sq": [(sb1("sq%d" % k, [128, 512], BF16), TT()) for k in range(2)],
                        "t": [(sb1("tt%d" % k, [128, 512], F32), TT()) for k in range(2)],
                        "sd": (sb1("sd", [128, 512], F32), TT()),
                        "rstd": [(sb1("rstd0", [128, 512], F32), TT())] * 2,
                        "eps": epsb,
                    }
                    CH = []
                    for p_ in range(2):
                        CH.append({
                            "qk": (sb1("qk%d" % p_, [128, 640], F32), TT()),
                            "t2": (sb1("t2%d" % p_, [128, 640], F32), TT()),
                            "qrot": (sb1("qrot%d" % p_, [128, 640], BF16), TT()),
                            "ssq": (sb1("ssq%d" % p_, [128, 10], F32), TT()),
                            "sdq": (sb1("sdq%d" % p_, [128, 10], F32), TT()),
                            "rsq": (sb1("rsq%d" % p_, [128, 10], F32), TT()),
                        })
                    QT = [(sb1("qT%d" % k, [128, 512], BF16), TT()) for k in range(4)]
                    PT = [(sb1("PT%d" % k, [128, 512], BF16), TT()) for k in range(10)]
                    DS = [(sb1("dsum%d" % k, [128, 512], F32), TT()) for k in range(2)]
                    RD = [(None, None)]
                    ps2b = ps[2][:, :].bitcast(BF16)

                    CHE = cfg.get("chain_eng", "pool")
                    CHN = nc.vector if CHE == "dve" else nc.gpsimd

                    def proj_block(ti, sbk, tb, need_q, par):
                        zs = tiles1.index(ti) % 2
                        isctx = TILES[ti][2]
                        B_ = CH[par]
                        qk, t_qk = B_["qk"]; qn, t_qn = B_["qk"]; t2, t_t2 = B_["t2"]; qrot, t_qrot = B_["qrot"]
                        ssq, t_ssq = B_["ssq"]; sdq, t_sdq = B_["sdq"]; rsq, t_rsq = B_["rsq"]
                        if need_q:
                            for k in range(NCH):
                                S.op("pe", (lambda k=k: nc.tensor.matmul(ps[0][:, 0:512], lhsT=ZTv[zs][:, k, sbk * 128:(sbk + 1) * 128],
                                                                         rhs=W2v[:, k, 0:512], start=(k == 0), stop=(k == NCH - 1))),
                                     reads=[ZT_t[zs][k], t_W2], writes=[pst[0]])
                        for k in range(NCH):
                            S.op("pe", (lambda k=k: nc.tensor.matmul(ps[1][:, 0:256], lhsT=ZTv[zs][:, k, sbk * 128:(sbk + 1) * 128],
                                                                     rhs=W2v[:, k, 512:768], start=(k == 0), stop=(k == NCH - 1))),
                                 reads=[ZT_t[zs][k], t_W2], writes=[pst[1]])
                        if need_q:
                            S.op("act", lambda: nc.scalar.copy(out=qk[:, 0:512], in_=ps[0][:, 0:512]), reads=[pst[0]], writes=[t_qk])
                        S.op("act", lambda: nc.scalar.copy(out=qk[:, 512:640], in_=ps[1][:, 0:128]), reads=[pst[1]], writes=[t_qk])
                        S.op("act", lambda: nc.scalar.copy(out=Vv[:, tb, :], in_=ps[1][:, 128:256]), reads=[pst[1]], writes=[V_t[tb]])
                        lo = 0 if need_q else 512
                        h0 = lo // 64
                        nh = (640 - lo) // 64
                        S.op("dve", lambda: nc.vector.tensor_tensor(out=t2[:, lo:640], in0=qk[:, lo:640], in1=qk[:, lo:640], op=ALU.mult),
                             reads=[t_qk], writes=[t_t2])
                        S.op("dve", lambda: nc.vector.reduce_sum(out=ssq[:, h0:10], in_=t2[:, lo:640].rearrange("p (h d) -> p h d", d=64),
                                                                 axis=mybir.AxisListType.X), reads=[t_t2], writes=[t_ssq])
                        S.op("act", lambda: nc.scalar.activation(out=sdq[:, h0:10], in_=ssq[:, h0:10], func=AF.Ln, bias=epsb[:, 0:1], scale=1.0 / 64),
                             reads=[t_ssq, t_const], writes=[t_sdq])
                        S.op("act", lambda: nc.scalar.activation(out=rsq[:, h0:10], in_=sdq[:, h0:10], func=AF.Exp, scale=-0.5), reads=[t_sdq], writes=[t_rsq])
                        qkv3 = qk[:, lo:640].rearrange("p (h d) -> p h d", d=64)
                        qn3 = qn[:, lo:640].rearrange("p (h d) -> p h d", d=64)
                        S.op(CHE, lambda: CHN.tensor_tensor(out=qn3, in0=qkv3, in1=rsq[:, h0:10].unsqueeze(2).broadcast_to([128, nh, 64]), op=ALU.mult),
                             reads=[t_qk, t_rsq], writes=[t_qn])
                        S.op(CHE, lambda: CHN.tensor_tensor(out=qn3, in0=qn3, in1=qkgv[:, h0:10, :], op=ALU.mult),
                             reads=[t_qn, t_lp], writes=[t_qn])
                        qr_q = qrot[:, 0:512].rearrange("p (hh kv d) -> p kv hh d", hh=4, kv=2)
                        if isctx:
                            if need_q:
                                S.op(CHE, lambda: CHN.tensor_copy(out=qr_q, in_=qn[:, 0:512].rearrange("p (kv hh d) -> p kv hh d", kv=2, hh=4)),
                                     reads=[t_qn], writes=[t_qrot])
                            S.op(CHE, lambda: CHN.tensor_copy(out=qrot[:, 512:640], in_=qn[:, 512:640]), reads=[t_qn], writes=[t_qrot])
                        else:
                            cosb = ropeC[:, tb * 64:(tb + 1) * 64]
                            sinb = ropeS[:, tb * 64:(tb + 1) * 64]
                            qn4 = qn[:, lo:640].rearrange("p (h s a d) -> p h s a d", s=2, a=2, d=16)
                            t24 = t2[:, lo:640].rearrange("p (h s a d) -> p h s a d", s=2, a=2, d=16)
                            sin4 = sinb.rearrange("p (s a d) -> p s a d", s=2, a=2)
                            for a in range(2):
                                S.op(CHE, (lambda a=a: CHN.tensor_tensor(
                                    out=t24[:, :, :, a, :], in0=qn4[:, :, :, 1 - a, :],
                                    in1=sin4[:, :, a, :].unsqueeze(1).broadcast_to([128, nh, 2, 16]), op=ALU.mult)),
                                    reads=[t_qn, t_rope, t_ssq], writes=[t_t2])
                            S.op(CHE, lambda: CHN.tensor_tensor(out=qn3, in0=qn3, in1=cosb.unsqueeze(1).broadcast_to([128, nh, 64]), op=ALU.mult),
                                 reads=[t_qn, t_rope, t_t2], writes=[t_qn])
                            if need_q:
                                S.op(CHE, lambda: CHN.tensor_tensor(
                                    out=qr_q, in0=qn[:, 0:512].rearrange("p (kv hh d) -> p kv hh d", kv=2, hh=4),
                                    in1=t2[:, 0:512].rearrange("p (kv hh d) -> p kv hh d", kv=2, hh=4), op=ALU.add),
                                    reads=[t_qn, t_t2], writes=[t_qrot])
                            S.op(CHE, lambda: CHN.tensor_tensor(out=qrot[:, 512:640], in0=qn[:, 512:640], in1=t2[:, 512:640], op=ALU.add),
                                 reads=[t_qn, t_t2], writes=[t_qrot])

                    def trans_block(tb, need_q, par):
                        qrot, t_qrot = CH[par]["qrot"]
                        if need_q:
                            for hh in range(4):
                                S.op("pe", (lambda hh=hh: nc.tensor.transpose(out=ps2b[:, hh * 128:(hh + 1) * 128], in_=qrot[:, hh * 128:(hh + 1) * 128], identity=ident[:])),
                                     reads=[t_qrot, t_const], writes=[pst[2]])
                        S.op("pe", lambda: nc.tensor.transpose(out=ps2b[:, 512:640], in_=qrot[:, 512:640], identity=ident[:]),
                             reads=[t_qrot, t_const], writes=[pst[2]])
                        if need_q:
                            qt, tqt = QT[tb % 4]
                            S.op("act", lambda: nc.scalar.copy(out=qt[:], in_=ps2b[:, 0:512]), reads=[pst[2]], writes=[tqt])
                        S.op("act", lambda: nc.scalar.copy(out=kT[:, tb * 128:(tb + 1) * 128], in_=ps2b[:, 512:640]), reads=[pst[2]], writes=[kT_t[tb]])

                    def attn_block(tb):
                        isctx = tb >= 16
                        if isctx:
                            keys = [(16, None), (17, None)]
                        else:
                            keys = []
                            if tb > 0:
                                keys.append((tb - 1, maskP))
                            keys.append((tb, None))
                            if tb < 15:
                                keys.append((tb + 1, maskN))
                            keys += [(16, None), (17, None)]
                        nk = len(keys)
                        qt, tqt = QT[tb % 4]

                        SB = [4, 5, 3]

                        def s_exp(kv, idx):
                            rows = slice(kv * 64, (kv + 1) * 64)
                            kb, mask = keys[idx]
                            sp_ = SB[idx % 3]
                            pt, tpt = PT[kv * 5 + idx]
                            S.op("pe", lambda: nc.tensor.matmul(ps[sp_][:, 0:512], lhsT=kT[rows, kb * 128:(kb + 1) * 128], rhs=qt[rows, :], start=True, stop=True),
                                 reads=[kT_t[kb], tqt], writes=[pst[sp_]])
                            S.op("act", lambda: nc.scalar.activation(out=pt[:], in_=ps[sp_][:, 0:512], func=AF.Exp, scale=0.125), reads=[pst[sp_]], writes=[tpt])
                            if mask is not None:
                                S.op("dve", lambda: nc.vector.tensor_tensor(
                                    out=pt[:].rearrange("p (h q) -> p h q", h=4), in0=pt[:].rearrange("p (h q) -> p h q", h=4),
                                    in1=mask[:].unsqueeze(1).broadcast_to([128, 4, 128]), op=ALU.mult), reads=[tpt, t_const], writes=[tpt])

                        def den_mm(kv, idx):
                            pt, tpt = PT[kv * 5 + idx]
                            S.op("pe", lambda: nc.tensor.matmul(ps[6][:, 0:512], lhsT=ones_b[:], rhs=pt[:], start=(idx == 0), stop=(idx == nk - 1)),
                                 reads=[tpt, t_const], writes=[pst[6]])

                        def o_mm(kv):
                            for hh in range(4):
                                j = 2 * kv + hh // 2
                                half = slice((hh % 2) * 64, (hh % 2) * 64 + 64)
                                for idx, (kb, mask) in enumerate(keys):
                                    pt, tpt = PT[kv * 5 + idx]
                                    S.op("pe", (lambda kb=kb, pt=pt, idx=idx, j=j, half=half, hh=hh: nc.tensor.matmul(
                                        ps[7][half, j * 128:(j + 1) * 128], lhsT=Vv[:, kb, kv * 64:(kv + 1) * 64], rhs=pt[:, hh * 128:(hh + 1) * 128],
                                        start=(idx == 0), stop=(idx == nk - 1))),
                                        reads=[V_t[kb], tpt], writes=[pst[7]])

                        def den_fin(kv):
                            ds, tds = DS[kv]
                            rd, trd = RD[0]
                            S.op("dve", lambda: nc.vector.tensor_tensor(
                                out=ds[:].rearrange("p (h q) -> p h q", h=4), in0=ps[6][:, 0:512].rearrange("p (h q) -> p h q", h=4),
                                in1=esk[:, 4 * kv:4 * kv + 4].unsqueeze(2).broadcast_to([128, 4, 128]), op=ALU.add),
                                reads=[pst[6], t_lp], writes=[tds])
                            S.op("act", lambda: nc.scalar.activation(out=ds[:], in_=ds[:], func=AF.Ln), reads=[tds], writes=[tds])
                            S.op("act", lambda: nc.scalar.activation(out=ds[:], in_=ds[:], func=AF.Exp, scale=-1.0), reads=[tds], writes=[tds])

                        def o_fin(kv):
                            ds, tds = DS[kv]
                            rd4 = ds[:].rearrange("p (a b q) -> p a b q", a=2, b=2)
                            for hb in range(2):
                                half = slice(hb * 64, hb * 64 + 64)
                                S.op("dve", (lambda hb=hb, half=half: nc.vector.tensor_tensor(
                                    out=yv[half, 4 + 2 * kv:4 + 2 * kv + 2, tb * 128:(tb + 1) * 128],
                                    in0=ps[7][half, 2 * kv * 128:(2 * kv + 2) * 128].rearrange("p (a q) -> p a q", a=2),
                                    in1=rd4[half, :, hb, :], op=ALU.mult)),
                                    reads=[pst[7], tds], writes=[yT[4 + 2 * kv][tb], yT[4 + 2 * kv + 1][tb]])

                        def s_phase(kv, first, last_):
                            for idx in range(first, last_):
                                s_exp(kv, idx)

                        LA = 2
                        for idx in range(min(LA, nk)):
                            s_exp(0, idx)
                        for idx in range(nk):
                            if idx + LA < nk:
                                s_exp(0, idx + LA)
                            den_mm(0, idx)
                        den_fin(0)
                        for idx in range(min(LA, nk)):
                            s_exp(1, idx)
                        o_mm(0)
                        for idx in range(nk):
                            if idx + LA < nk:
                                s_exp(1, idx + LA)
                            den_mm(1, idx)
                        den_fin(1)
                        o_fin(0)
                        o_mm(1)
                        o_fin(1)

                    seq = []
                    for ti in tiles1:
                        t0, w, isctx = TILES[ti]
                        for sbk in range(w // 128):
                            seq.append((ti, sbk, t0 // 128 + sbk))
                    normed = set()

                    def do_norm(ti):
                        if ti in normed:
                            return
                        normed.add(ti)
                        zs = tiles1.index(ti) % 2
                        w = TILES[ti][1]
                        emit_norm(l, 1, jf, [ti], None, None, tmp,
                                  zdst=(lambda c, ti_, zs=zs, w=w: (ZTv[zs][:, c, 0:w], ZT_t[zs][c])), pbf=(lambda ti_: 3))

                    projected = set()
                    done_attn = set()
                    attn_wanted = set(range(16)) | (set() if last else {16, 17})

                    def ready_blocks():
                        out = []
                        for tb in sorted(attn_wanted - done_attn, key=lambda x: (x < 16, x)):
                            need = {16, 17} if tb >= 16 else ({tb, 16, 17} | ({tb - 1} if tb > 0 else set()) | ({tb + 1} if tb < 15 else set()))
                            if need <= projected:
                                out.append(tb)
                        return out

                    for n, (ti, sbk, tb) in enumerate(seq):
                        do_norm(ti)
                        if sbk == 1 or TILES[ti][1] == 256:
                            nxt = tiles1.index(ti) + 1
                            if nxt < len(tiles1) and sbk >= (1 if TILES[ti][1] > 256 else 1):
                                do_norm(tiles1[nxt])
                        need_q = not (TILES[ti][2] and last)
                        proj_block(ti, sbk, tb, need_q, n % 2)
                        for rb in ready_blocks()[:2]:
                            attn_block(rb)
                            done_attn.add(rb)
                        trans_block(tb, need_q, n % 2)
                        projected.add(tb)
                    for rb in ready_blocks():
                        attn_block(rb)
                        done_attn.add(rb)
                    assert done_attn == attn_wanted
                    S.barrier()

                with ExitStack() as p2:
                    cur2 = [p2]

                    def sb2(name, shape, dt):
                        return cur2[0].enter_context(nc.sbuf_tensor(uname(name), list(shape), dt))
                    lnB = sb2("lnB", [128, 512], F32)
                    wsT = sb2("wsT", [128, 512], BF16)
                    bsB = sb2("bsB", [128, 256], F32)
                    t_lp2 = TT()
                    S.op("sp", lambda: nc.sync.dma_start(out=lnB[:], in_=lnB_d[l]), writes=[t_lp2], dma=d_misc)
                    S.op("pool", lambda: nc.gpsimd.dma_start(out=wsT[:], in_=wsT_d[l]), writes=[t_lp2], dma=d_misc2)
                    S.op("sp", lambda: nc.sync.dma_start(out=bsB[:], in_=bsB_d[l]), writes=[t_lp2], dma=d_misc)
                    Z = sb2("Zm", [128, NCH * T], BF16)
                    zv = Z[:].rearrange("p (c t) -> p c t", c=NCH)
                    zT = [[TT() for _ in TILES] for _ in range(NCH)]
                    pw = ExitStack()
                    cur2[0] = pw
                    W1 = sb2("W1", [128, NCH * 1280], BF16)
                    W1v = W1[:].rearrange("p (k n) -> p k n", k=NCH)
                    t_W1 = TT()
                    S.op("pool", lambda: nc.gpsimd.dma_start(out=W1v, in_=w_in[l].rearrange("(k p) n -> p k n", p=128)[:, :, 0:1280]),
                         writes=[t_W1], dma=d_w[1])
                    pn = ExitStack()
                    cur2[0] = pw
                    tmp = {
                        "sq": [(sb2("sq%d" % k, [128, 512], BF16), TT()) for k in range(2)],
                        "t": [(sb2("tt%d" % k, [128, 512], F32), TT()) for k in range(2)],
                        "sd": (sb2("sd", [128, 512], F32), TT()),
                        "rstd": [(sb2("rstd0", [128, 512], F32), TT())] * 2,
                        "eps": epsb,
                    }
                    UU = [(sb2("U%d" % k, [128, 2 * 512], BF16), TT()) for k in range(2)]
                    GB = []
                    for p_ in range(2):
                        GB.append({
                            "gv": (sb2("gv%d" % p_, [128, 256], F32), TT()),
                            "bst": (sb2("bst%d" % p_, [128, 6], F32), TT()),
                            "mv": (sb2("mvv%d" % p_, [128, 2], F32), TT()),
                            "sdl": (sb2("sdl%d" % p_, [128, 1], F32), TT()),
                            "rsl": (sb2("rsl%d" % p_, [128, 1], F32), TT()),
                            "vn": (sb2("vn%d" % p_, [128, 256], F32), TT()),
                            "vnb": (sb2("vnb%d" % p_, [128, 256], BF16), TT()),
                            "stmp": (sb2("stmp%d" % p_, [128, 256], F32), TT()),
                        })
                    hal = sb2("hal", [128, 8], F32); t_hal = TT()
                    tbuf = sb2("tbuf", [128, 2 * 514], F32); tbv = tbuf[:].rearrange("p (c t) -> p c t", c=2); t_tb = TT()
                    acc = sb2("acc", [128, 512], F32); t_acc = TT()
                    cwv = cw_s[:].rearrange("p (l c k) -> p l c k", l=DEPTH, c=2)

                    def gu_tile(ti, up):
                        t0, w, isctx = TILES[ti]
                        U, t_U = UU[up]
                        Uv = U[:].rearrange("p (c t) -> p c t", c=2)
                        for cc in range(2):
                            for k in range(NCH):
                                S.op("pe", (lambda k=k, cc=cc: nc.tensor.matmul(ps[cc][:, 0:w], lhsT=W1v[:, k, OFF_GU + cc * 128:OFF_GU + (cc + 1) * 128],
                                                                               rhs=zv[:, k, t0:t0 + w], start=(k == 0), stop=(k == NCH - 1))),
                                     reads=[t_W1, zT[k][ti]], writes=[pst[cc]])
                            S.op("act", (lambda cc=cc: nc.scalar.activation(out=Uv[:, cc, 0:w], in_=ps[cc][:, 0:w], func=AF.Gelu_apprx_tanh)),
                                 reads=[pst[cc]], writes=[t_U])

                    def stage1(ti, sbk, par):
                        t0, w, isctx = TILES[ti]
                        tok = slice(t0 + sbk * 128, t0 + (sbk + 1) * 128)
                        B_ = GB[par]
                        gv, t_gv = B_["gv"]; bst, t_bst = B_["bst"]; mvv, t_mv = B_["mv"]; sdl, t_sdl = B_["sdl"]; rsl, t_rsl = B_["rsl"]
                        vn, t_vn = B_["vn"]; vnb, t_vnb = B_["vnb"]
                        for k in range(NCH):
                            S.op("pe", (lambda k=k: nc.tensor.matmul(ps[2][:, 0:256], lhsT=zv[:, k, tok], rhs=W1v[:, k, OFF_GV:OFF_GV + 256],
                                                                     start=(k == 0), stop=(k == NCH - 1))),
                                 reads=[t_W1, zT[k][ti]], writes=[pst[2]])
                        S.op("act", lambda: nc.scalar.activation(out=gv[:], in_=ps[2][:, 0:256], func=AF.Gelu_apprx_tanh), reads=[pst[2]], writes=[t_gv])
                        S.op("dve", lambda: nc.vector.bn_stats(out=bst[:], in_=gv[:]), reads=[t_gv], writes=[t_bst])
                        S.op("dve", lambda: nc.vector.bn_aggr(out=mvv[:], in_=bst[:]), reads=[t_bst], writes=[t_mv])
                        S.op("act", lambda: nc.scalar.activation(out=sdl[:], in_=mvv[:, 1:2], func=AF.Ln, bias=epsb[:, 0:1], scale=1.0),
                             reads=[t_mv, t_const], writes=[t_sdl])
                        S.op("act", lambda: nc.scalar.activation(out=rsl[:], in_=sdl[:], func=AF.Exp, scale=-0.5), reads=[t_sdl], writes=[t_rsl])
                        S.op("dve", lambda: nc.vector.tensor_scalar(out=vn[:], in0=gv[:], scalar1=mvv[:, 0:1], scalar2=rsl[:, 0:1], op0=ALU.subtract, op1=ALU.mult),
                             reads=[t_gv, t_mv, t_rsl], writes=[t_vn])
                        S.op("pool", lambda: nc.gpsimd.tensor_tensor(out=vn[:], in0=vn[:], in1=lnB[:, 0:256], op=ALU.mult), reads=[t_vn, t_lp2], writes=[t_vn])
                        S.op("pool", lambda: nc.gpsimd.tensor_tensor(out=vnb[:], in0=vn[:], in1=lnB[:, 256:512], op=ALU.add), reads=[t_vn, t_lp2], writes=[t_vnb])

                    def stage2(ti, sbk, par, up):
                        t0, w, isctx = TILES[ti]
                        tb = t0 // 128 + sbk
                        tok = slice(t0 + sbk * 128, t0 + (sbk + 1) * 128)
                        B_ = GB[par]
                        vnb, t_vnb = B_["vnb"]; stmp, t_stmp = B_["stmp"]
                        U, t_U = UU[up]
                        Uv = U[:].rearrange("p (c t) -> p c t", c=2)
                        for g in range(4):
                            half = slice((g % 2) * 64, (g % 2) * 64 + 64)
                            S.op("pe", (lambda g=g, half=half: nc.tensor.matmul(ps[3][half, (g // 2) * 128:(g // 2 + 1) * 128], lhsT=vnb[:, g * 64:(g + 1) * 64],
                                                                                 rhs=wsT[:, g * 128:(g + 1) * 128], start=True, stop=True)),
                                 reads=[t_vnb, t_lp2], writes=[pst[3]])
                        S.op("dve", lambda: nc.vector.tensor_tensor(out=stmp[:], in0=ps[3][:, 0:256], in1=bsB[:], op=ALU.add), reads=[pst[3], t_lp2], writes=[t_stmp])
                        S.op("dve", lambda: nc.vector.tensor_tensor(
                            out=yv[:, 2:4, tok], in0=stmp[:].rearrange("p (c t) -> p c t", c=2), in1=Uv[:, :, sbk * 128:(sbk + 1) * 128], op=ALU.mult),
                            reads=[t_stmp, t_U], writes=[yT[2][tb], yT[3][tb]])

                    def conv_mm1(ti):
                        t0, w, isctx = TILES[ti]
                        has_l = t0 not in (0, SEQ)
                        has_r = (t0 + w) not in (SEQ, T)
                        for cc in range(2):
                            for k in range(NCH):
                                S.op("pe", (lambda k=k, cc=cc: nc.tensor.matmul(ps[4 + cc][:, 0:w], lhsT=W1v[:, k, OFF_CC + cc * 128:OFF_CC + (cc + 1) * 128],
                                                                               rhs=zv[:, k, t0:t0 + w], start=(k == 0), stop=(k == NCH - 1))),
                                     reads=[t_W1, zT[k][ti]], writes=[pst[4 + cc]])
                            S.op("act", (lambda cc=cc: nc.scalar.copy(out=tbv[:, cc, 1:w + 1], in_=ps[4 + cc][:, 0:w])), reads=[pst[4 + cc]], writes=[t_tb])
                        for cc in range(2):
                            for k in range(NCH):
                                S.op("pe", (lambda k=k, cc=cc: nc.tensor.matmul(ps[6 + cc][:, 0:w], lhsT=W1v[:, k, OFF_CH + cc * 128:OFF_CH + (cc + 1) * 128],
                                                                               rhs=zv[:, k, t0:t0 + w], start=(k == 0), stop=(k == NCH - 1))),
                                     reads=[t_W1, zT[k][ti]], writes=[pst[6 + cc]])
                            S.op("dve", (lambda cc=cc: nc.vector.tensor_tensor(out=tbv[:, cc, 1:w + 1], in0=tbv[:, cc, 1:w + 1], in1=ps[6 + cc][:, 0:w], op=ALU.mult)),
                                 reads=[t_tb, pst[6 + cc]], writes=[t_tb])
                        sides = []
                        if has_l:
                            sides.append((0, t0 - 1, ti - 1))
                        if has_r:
                            sides.append((1, t0 + w, ti + 1))
                        for side, col, nti in sides:
                            for which, off in ((0, OFF_CC), (1, OFF_CH)):
                                for cc in range(2):
                                    pc = side * 4 + which * 2 + cc
                                    for k in range(NCH):
                                        S.op("pe", (lambda k=k, cc=cc, off=off, col=col, pc=pc: nc.tensor.matmul(
                                            ps[2][:, 256 + pc:256 + pc + 1], lhsT=W1v[:, k, off + cc * 128:off + (cc + 1) * 128], rhs=zv[:, k, col:col + 1],
                                            start=(k == 0), stop=(k == NCH - 1))),
                                            reads=[t_W1, zT[k][nti]], writes=[pst[2]])
                        if sides:
                            S.op("act", lambda: nc.scalar.copy(out=hal[:], in_=ps[2][:, 256:264]), reads=[pst[2]], writes=[t_hal])
                        for side in (0, 1):
                            present = any(s_[0] == side for s_ in sides)
                            colt = 0 if side == 0 else w + 1
                            if present:
                                S.op("dve", (lambda side=side, colt=colt: nc.vector.tensor_tensor(
                                    out=tbv[:, :, colt], in0=hal[:, side * 4:side * 4 + 2], in1=hal[:, side * 4 + 2:side * 4 + 4], op=ALU.mult)),
                                    reads=[t_hal], writes=[t_tb])
                            else:
                                S.op("dve", (lambda colt=colt: nc.vector.memset(tbv[:, :, colt:colt + 1], 0.0)), writes=[t_tb])

                    def conv_mm2(ti):
                        t0, w, isctx = TILES[ti]
                        for cc in range(2):
                            for k in range(NCH):
                                S.op("pe", (lambda k=k, cc=cc: nc.tensor.matmul(ps[4 + cc][:, 0:w], lhsT=W1v[:, k, OFF_CB + cc * 128:OFF_CB + (cc + 1) * 128],
                                                                               rhs=zv[:, k, t0:t0 + w], start=(k == 0), stop=(k == NCH - 1))),
                                     reads=[t_W1, zT[k][ti]], writes=[pst[4 + cc]])
                            S.op("dve", (lambda cc=cc: nc.vector.tensor_scalar(out=acc[:, 0:w], in0=tbv[:, cc, 1:w + 1], scalar1=cwv[:, l, cc, 1:2], scalar2=None, op0=ALU.mult)),
                                 reads=[t_tb, t_const], writes=[t_acc])
                            S.op("dve", (lambda cc=cc: nc.vector.scalar_tensor_tensor(out=acc[:, 0:w], in0=tbv[:, cc, 0:w], scalar=cwv[:, l, cc, 0:1], in1=acc[:, 0:w],
                                                                                      op0=ALU.mult, op1=ALU.add)), reads=[t_tb, t_acc, t_const], writes=[t_acc])
                            S.op("dve", (lambda cc=cc: nc.vector.scalar_tensor_tensor(out=acc[:, 0:w], in0=tbv[:, cc, 2:w + 2], scalar=cwv[:, l, cc, 2:3], in1=acc[:, 0:w],
                                                                                      op0=ALU.mult, op1=ALU.add)), reads=[t_tb, t_acc, t_const], writes=[t_acc])
                            S.op("dve", (lambda cc=cc: nc.vector.tensor_tensor(out=yv[:, cc, t0:t0 + w], in0=acc[:, 0:w], in1=ps[4 + cc][:, 0:w], op=ALU.mult)),
                                 reads=[t_acc, pst[4 + cc]], writes=[yT[cc][tb_] for tb_ in range(t0 // 128, (t0 + w) // 128)])

                    norm_done = []

                    def norm_tile(idx_):
                        if idx_ < len(tiles2) and idx_ not in norm_done:
                            norm_done.append(idx_)
                            emit_norm(l, 1, jf, [tiles2[idx_]], zv, zT, tmp, pbf=(lambda ti_: 3))

                    nblk = 0
                    for pos, ti in enumerate(tiles2):
                        norm_tile(pos)
                        norm_tile(pos + 1)
                        for nb_ in (ti - 1, ti + 1):
                            if nb_ in tiles2:
                                norm_tile(tiles2.index(nb_))
                        up = pos % 2
                        gu_tile(ti, up)
                        nb = TILES[ti][1] // 128
                        pend = None
                        conv_steps = [conv_mm1, conv_mm2]
                        for sbk in range(nb):
                            stage1(ti, sbk, nblk % 2)
                            if pend is not None:
                                stage2(*pend)
                            if sbk < len(conv_steps):
                                conv_steps[sbk](ti)
                            pend = (ti, sbk, nblk % 2, up)
                            nblk += 1
                        stage2(*pend)
                    if cfg.get("dump_y") and not dumped:
                        dumped.append(1)
                        S.op("pool", lambda: nc.gpsimd.dma_start(out=dbgY, in_=Y[:]), reads=[t for row in yT for t in row], dma=d_out)
                        S.op("pool", lambda: nc.gpsimd.dma_start(out=dbgZ, in_=Z[:]), reads=[t for row in zT for t in row], dma=d_out)
                    S.barrier()
                    pw.close()
                    cur2[0] = p2
                    WM = [sb2("WM%d" % s_, [128, 5120], BF16) for s_ in range(2)]
                    t_WM = [TT(), TT()]
                    SG = [(sb2("sg%d" % k, [128, 512], F32), TT()) for k in range(3)]
                    M1 = sb2("m1", [128, 512], F32); t_m1 = TT()
                    M2 = sb2("m2", [128, 512], F32); t_m2 = TT()
                    M3 = sb2("m3", [128, 512], F32); t_m3 = TT()
                    MG = [(sb2("mg%d" % k, [128, 512], BF16), TT()) for k in range(2)]
                    bgv = bg_s[:].rearrange("p (l r c) -> p l r c", l=DEPTH, r=3)
                    win_v = w_in[l].rearrange("(k p) n -> p k n", p=128)

                    def load_c(c):
                        s_ = c % 2
                        wg = WM[s_][:, 0:3072].rearrange("p (k r n) -> p k r n", k=NCH, r=3)
                        wb = WM[s_][:, 3072:4096].rearrange("p (k n) -> p k n", k=8)
                        wo = WM[s_][:, 4096:5120]
                        for r in range(3):
                            S.op("pool", (lambda r=r: nc.gpsimd.dma_start(out=wg[:, :, r, :], in_=win_v[:, :, OFF_GATE + r * D + c * 128:OFF_GATE + r * D + (c + 1) * 128])),
                                 writes=[t_WM[s_]], dma=d_w[s_])
                        S.op("pool", lambda: nc.gpsimd.dma_start(out=wb[:, 0:2, :], in_=w_bc[l].rearrange("(k p) n -> p k n", p=128)[:, :, c * 128:(c + 1) * 128]),
                             writes=[t_WM[s_]], dma=d_w[s_])
                        S.op("pool", lambda: nc.gpsimd.dma_start(out=wb[:, 2:4, :], in_=w_bg[l].rearrange("(k p) n -> p k n", p=128)[:, :, c * 128:(c + 1) * 128]),
                             writes=[t_WM[s_]], dma=d_w[s_])
                        S.op("pool", lambda: nc.gpsimd.dma_start(out=wb[:, 4:8, :], in_=w_ba[l].rearrange("(k p) n -> p k n", p=128)[:, :, c * 128:(c + 1) * 128]),
                             writes=[t_WM[s_]], dma=d_w[s_])
                        S.op("pool", lambda: nc.gpsimd.dma_start(out=wo, in_=w_o[l][c * 128:(c + 1) * 128, :]), writes=[t_WM[s_]], dma=d_w[s_])
                        return wg, wb, wo

                    wv_ = {0: load_c(0)}

                    def emit_gate(c, ti, par):
                        s_ = c % 2
                        wg, wb, wo = wv_[c]
                        t0, w, isctx = TILES[ti]
                        blks = range(t0 // 128, (t0 + w) // 128)
                        for r in range(3):
                            for k in range(NCH):
                                S.op("pe", (lambda r=r, k=k: nc.tensor.matmul(ps[r % 2][:, 0:w], lhsT=wg[:, k, r, :], rhs=zv[:, k, t0:t0 + w], start=(k == 0), stop=(k == NCH - 1))),
                                     reads=[t_WM[s_], zT[k][ti]], writes=[pst[r % 2]])
                            sg, tsg = SG[r]
                            S.op("act", (lambda r=r, sg=sg: nc.scalar.activation(out=sg[:, 0:w], in_=ps[r % 2][:, 0:w], func=AF.Sigmoid, bias=bgv[:, l, r, c:c + 1], scale=1.0)),
                                 reads=[pst[r % 2], t_const], writes=[tsg])

                    def emit_branch(c, ti, par):
                        s_ = c % 2
                        wg, wb, wo = wv_[c]
                        t0, w, isctx = TILES[ti]
                        blks = range(t0 // 128, (t0 + w) // 128)
                        kr = [(0, 2), (2, 4), (4, 8)]
                        MM = [(M1, t_m1), (M2, t_m2), (M3, t_m3)]
                        for r in range(3):
                            k0, k1 = kr[r]
                            pb_ = 2 + r % 2
                            for kk in range(k0, k1):
                                S.op("pe", (lambda kk=kk, k0=k0, k1=k1, pb_=pb_: nc.tensor.matmul(ps[pb_][:, 0:w], lhsT=wb[:, kk, :], rhs=yv[:, kk, t0:t0 + w],
                                                                                             start=(kk == k0), stop=(kk == k1 - 1))),
                                     reads=[t_WM[s_]] + [yT[kk][tb_] for tb_ in blks], writes=[pst[pb_]])
                            mm_, tmm_ = MM[r]
                            S.op("dve", (lambda r=r, pb_=pb_, mm_=mm_: nc.vector.tensor_tensor(out=mm_[:, 0:w], in0=SG[r][0][:, 0:w], in1=ps[pb_][:, 0:w], op=ALU.mult)),
                                 reads=[SG[r][1], pst[pb_]], writes=[tmm_])
                        mg, tmg = MG[par]
                        S.op("pool", lambda: nc.gpsimd.tensor_tensor(out=M1[:, 0:w], in0=M1[:, 0:w], in1=M2[:, 0:w], op=ALU.add), reads=[t_m1, t_m2], writes=[t_m1])
                        S.op("pool", lambda: nc.gpsimd.tensor_tensor(out=mg[:, 0:w], in0=M1[:, 0:w], in1=M3[:, 0:w], op=ALU.add), reads=[t_m1, t_m3], writes=[tmg])

                    def emit_out(c, ti, par):
                        s_ = c % 2
                        wg, wb, wo = wv_[c]
                        t0, w, isctx = TILES[ti]
                        j = jf(ti)
                        mg, tmg = MG[par]
                        for c2 in range(NCH):
                            po = 4 + c2 % 4
                            S.op("pe", (lambda c2=c2, po=po: nc.tensor.matmul(ps[po][:, 0:w], lhsT=wo[:, c2 * 128:(c2 + 1) * 128], rhs=mg[:, 0:w], start=True, stop=True)),
                                 reads=[t_WM[s_], tmg], writes=[pst[po]])
                            S.op("dve", (lambda c2=c2, po=po: nc.vector.scalar_tensor_tensor(
                                out=Hv[:, c2, t0:t0 + w], in0=ps[po][:, 0:w], scalar=mcol(l, 5, c2, j), in1=Hv[:, c2, t0:t0 + w], op0=ALU.mult, op1=ALU.add)),
                                reads=[pst[po], hT[c2][ti], t_mod], writes=[hT[c2][ti]])

                    steps = [(c, ti) for c in range(NCH) for ti in tiles2]
                    prev = None
                    for n, (c, ti) in enumerate(steps):
                        emit_gate(c, ti, n % 2)
                        if prev is not None:
                            emit_out(*prev)
                        emit_branch(c, ti, n % 2)
                        if ti == tiles2[0] and c + 1 < NCH:
                            wv_[c + 1] = load_c(c + 1)
                        prev = (c, ti, n % 2)
                    emit_out(*prev)
                    S.barrier()

        for b in range(nb):
            for c in range(NCH):
                S.op("sp", (lambda c=c, b=b: nc.sync.dma_start(out=Hv[:, c, 0:SEQ], in_=xT[b, c * 128:(c + 1) * 128, :])),
                     writes=[hT[c][ti] for ti in range(4)], dma=d_x)
                S.op("sp", (lambda c=c, b=b: nc.sync.dma_start(out=Hv[:, c, SEQ:T], in_=ctxT[b, c * 128:(c + 1) * 128, :])),
                     writes=[hT[c][4]], dma=d_x)
            for l in layers:
                last = (l == DEPTH - 1)
                if "ffn1" in subs:
                    emit_ffn(l, 0, b, [4, 0, 1, 2, 3])
                if "mix" in subs:
                    emit_mixer(l, b, last)
                if "ffn2" in subs:
                    emit_ffn(l, 1, b, [0, 1, 2, 3] if last else [4, 0, 1, 2, 3])
            for c in range(NCH):
                S.op("sp", (lambda c=c, b=b: nc.sync.dma_start(out=yT[b, c * 128:(c + 1) * 128, :], in_=Hv[:, c, 0:SEQ])),
                     reads=[hT[c][ti] for ti in range(4)], dma=d_out)
            S.barrier()
        S.barrier()
        print("instructions:", S.n_instr)
    return nc


def prep_inputs(inp, nb=NB, ncores=NCORES):
    f = np.float32
    x = np.asarray(inp["x"], f)
    ctx = np.asarray(inp["ctx"], f)
    c = np.asarray(inp["c"], f)
    c_ctx = np.asarray(inp["c_ctx"], f)
    shared = {
        "bmodT": np.ascontiguousarray(np.asarray(inp["b_mod"], f).reshape(DEPTH, 72, 128).transpose(2, 0, 1).reshape(128, DEPTH * 72)),
        "ngT": np.ascontiguousarray(np.asarray(inp["norm_g"], f).reshape(DEPTH, 3, NCH, 128).transpose(3, 0, 1, 2).reshape(128, -1)),
        "w_mod": np.ascontiguousarray(np.asarray(inp["w_mod"], f)),
        "ffn_w_in": np.ascontiguousarray(np.asarray(inp["ffn_w_in"], f)),
        "ffn_w_out": np.ascontiguousarray(np.asarray(inp["ffn_w_out"], f)),
    }
    L = DEPTH
    shared["w_in"] = np.ascontiguousarray(np.asarray(inp["w_in"], f))
    shared["w_bc"] = np.ascontiguousarray(np.asarray(inp["w_branch_conv"], f))
    shared["w_bg"] = np.ascontiguousarray(np.asarray(inp["w_branch_gmlp"], f))
    shared["w_ba"] = np.ascontiguousarray(np.asarray(inp["w_branch_attn"], f))
    shared["w_o"] = np.ascontiguousarray(np.asarray(inp["w_out"], f))
    shared["bgT"] = np.ascontiguousarray(np.asarray(inp["b_gate"], f).reshape(L, 3, NCH, 128).transpose(3, 0, 1, 2).reshape(128, -1))
    shared["cwT"] = np.ascontiguousarray(np.asarray(inp["conv_w"], f).reshape(L, 3, 2, 128).transpose(3, 0, 2, 1).reshape(128, -1))
    lng = np.asarray(inp["gmlp_ln_g"], f)
    lnb = np.asarray(inp["gmlp_ln_b"], f)
    shared["lnB"] = np.ascontiguousarray(np.broadcast_to(np.concatenate([lng, lnb], axis=1)[:, None, :], (L, 128, 512)))
    ws = np.asarray(inp["gmlp_ws"], f)
    shared["wsT"] = np.ascontiguousarray(ws.transpose(0, 3, 1, 2).reshape(L, 128, 512))
    bs = np.asarray(inp["gmlp_bs"], f)
    shared["bsB"] = np.ascontiguousarray(np.repeat(bs.reshape(L, 2, 2, 1, 128), 64, axis=3).transpose(0, 2, 3, 1, 4).reshape(L, 128, 256))
    qg = np.asarray(inp["q_norm_g"], f)
    kg = np.asarray(inp["k_norm_g"], f)
    qkg = np.concatenate([np.tile(qg, (1, 8)), np.tile(kg, (1, 2))], axis=1)
    shared["qkg"] = np.ascontiguousarray(np.broadcast_to(qkg[:, None, :], (L, 128, 640)))
    shared["sinkB"] = np.ascontiguousarray(np.broadcast_to(np.asarray(inp["attn_sink"], f)[:, None, :], (L, 128, 8)))
    shared.update(_const_tables())
    maps = []
    for core in range(ncores):
        bs = slice(core * nb, (core + 1) * nb)
        cc = np.concatenate([c[bs], c_ctx[None, :]], axis=0)
        if nb < 4:
            cc = np.concatenate([c[bs], np.zeros((4 - nb, D), f), c_ctx[None, :]], axis=0)
        cT = np.ascontiguousarray(cc.reshape(5, NCH, 128).transpose(2, 1, 0).reshape(128, NCH * 5))
        m = dict(shared)
        m["xT"] = np.ascontiguousarray(x[bs].transpose(0, 2, 1))
        m["ctxT"] = np.ascontiguousarray(ctx[bs].transpose(0, 2, 1))
        m["cT"] = cT
        maps.append(m)
    return maps


def _const_tables():
    f = np.float32
    pos = np.arange(SEQ)
    r = (pos // 64).astype(f)
    col = (pos % 64).astype(f)
    half = 32
    inv = (np.float32(10000.0) ** (-np.arange(0, half, 2, dtype=f) / half)).astype(f)
    ang_r = r[:, None] * inv[None, :]
    ang_c = col[:, None] * inv[None, :]
    ang = np.concatenate([ang_r, ang_r, ang_c, ang_c], axis=-1).astype(f)
    cos = np.cos(ang).astype(f)
    sin = np.sin(ang).astype(f)
    sgn = np.tile(np.concatenate([-np.ones(16, f), np.ones(16, f)]), 2)
    sin_s = sin * sgn[None, :]
    ropeC = cos.reshape(16, 128, 64).transpose(1, 0, 2).reshape(128, 16 * 64)
    ropeS = sin_s.reshape(16, 128, 64).transpose(1, 0, 2).reshape(128, 16 * 64)
    j = np.arange(128)[:, None]
    i = np.arange(128)[None, :]
    return {
        "ropeC": np.ascontiguousarray(ropeC), "ropeS": np.ascontiguousarray(ropeS),
        "maskP": (j >= i).astype(f), "maskN": (j <= i).astype(f), "ident": np.eye(128, dtype=f),
    }


_NC_CACHE = {}


def kernel(**inputs):
    cfg = {"nb": NB}
    key = "full"
    if key not in _NC_CACHE:
        _NC_CACHE[key] = build_nc(cfg)
    nc = _NC_CACHE[key]
    maps = prep_inputs(inputs)
    res = run_bass_kernel_spmd(nc, maps, core_ids=list(range(NCORES)))
    outs = [np.asarray(r["yT"]).transpose(0, 2, 1) for r in res.results]
    return np.ascontiguousarray(np.concatenate(outs, axis=0).astype(np.float32))
```

```python
import numpy as np
from contextlib import ExitStack
import concourse.bass as bass
import concourse.mybir as mybir
from concourse.bass_utils import run_bass_kernel_spmd

F32 = mybir.dt.float32
BF16 = mybir.dt.bfloat16
AF = mybir.ActivationFunctionType
ALU = mybir.AluOpType

D = 1024
NCH = 8
SEQ = 2048
CTX = 256
T = SEQ + CTX
DEPTH = 4
DFF = 2816
NFF = 22
NMOD = 9
PROJ_W = 5120
EPS = 1e-6
NCORES = 8
NB = 4
OFF_CB, OFF_CC, OFF_CH, OFF_GU, OFF_GV, OFF_Q, OFF_K, OFF_V, OFF_GATE = 0, 256, 512, 768, 1024, 1280, 1792, 1920, 2048

TILES = [(0, 512, False), (512, 512, False), (1024, 512, False), (1536, 512, False), (2048, 256, True)]


class TT:
    __slots__ = ("w", "r", "psum")

    def __init__(self, psum=False):
        self.w = None
        self.r = []
        self.psum = psum


class DmaSem:
    def __init__(self, sem):
        self.sem = sem
        self.issued = 0


class Op:
    __slots__ = ("eng", "fn", "waits", "signal", "seq", "dma")

    def __init__(self, eng, fn, seq, dma=None):
        self.eng = eng
        self.fn = fn
        self.waits = []
        self.signal = False
        self.seq = seq
        self.dma = dma


class Sched:
    CENG = ("pe", "act", "dve", "pool")

    def __init__(self, nc, stack):
        self.nc = nc
        self.E = {"pe": nc.tensor, "act": nc.scalar, "dve": nc.vector, "pool": nc.gpsimd, "sp": nc.sync}
        self.sem = {e: stack.enter_context(nc.semaphore("sem_" + e)) for e in self.CENG}
        self.count = {e: 0 for e in self.CENG}
        self.nops = {e: 0 for e in self.CENG}
        self.ops = {e: {} for e in self.CENG}
        self.sigval = {e: {} for e in self.CENG}
        self.pending = []
        self.waited = {e: {s: -1 for s in self.CENG} for e in self.E}
        self.waited_dma = {e: {} for e in self.E}
        self.dsems = []
        self.stack = stack
        self.n_instr = 0

    def dma_sem(self, name):
        d = DmaSem(self.stack.enter_context(self.nc.semaphore(name)))
        self.dsems.append(d)
        return d

    def _add_dep(self, op, ev):
        e = op.eng
        if ev[0] == "c":
            _, src, s = ev
            if src == e and e == "pe":
                return
            if s <= self.waited[e][src]:
                return
            self.waited[e][src] = s
            op.waits.append(ev)
            self.ops[src][s].signal = True
        else:
            d = ev[1]
            val = d.issued
            if self.waited_dma[e].get(d, 0) >= val:
                return
            self.waited_dma[e][d] = val
            op.waits.append(("d", d, val))

    def op(self, eng, fn, reads=(), writes=(), dma=None):
        if dma is None:
            seq = self.nops[eng]
            self.nops[eng] += 1
            op = Op(eng, fn, seq)
            self.ops[eng][seq] = op
            ev = ("c", eng, seq)
        else:
            op = Op(eng, fn, -1, dma)
            ev = ("d", dma)
        deps = []
        for t in reads:
            if t.w is not None:
                deps.append(t.w)
            if t.psum:
                deps.extend(r for r in t.r if r[0] == "c" and r[1] != eng)
        for t in writes:
            if t.w is not None:
                deps.append(t.w)
            deps.extend(t.r)
        for dv in deps:
            self._add_dep(op, dv)
        if dma is not None:
            dma.issued += 16
        for t in reads:
            t.r.append(ev)
        for t in writes:
            t.w = ev
            t.r = []
        self.pending.append(op)
        return op

    def flush(self):
        for op in self.pending:
            eng = self.E[op.eng]
            for w in op.waits:
                if w[0] == "c":
                    eng.wait_ge(self.sem[w[1]], self.sigval[w[1]][w[2]])
                else:
                    eng.wait_ge(w[1].sem, w[2])
            ins = op.fn()
            self.n_instr += 1
            if op.dma is not None:
                ins.then_inc(op.dma.sem, 16)
            elif op.signal:
                self.count[op.eng] += 1
                self.sigval[op.eng][op.seq] = self.count[op.eng]
                ins.then_inc(self.sem[op.eng], 1)
        self.pending = []
        for e in self.CENG:
            self.ops[e] = {}

    def barrier(self):
        for e in self.CENG:
            if self.nops[e] > 0:
                last = self.nops[e] - 1
                if last in self.ops[e]:
                    self.ops[e][last].signal = True
        self.flush()
        for e, eng in self.E.items():
            for s in self.CENG:
                if self.count[s] > 0 and not (s == e and e == "pe"):
                    eng.wait_ge(self.sem[s], self.count[s])
                if self.nops[s] > 0:
                    self.waited[e][s] = self.nops[s] - 1
            for d in self.dsems:
                if d.issued > self.waited_dma[e].get(d, 0):
                    eng.wait_ge(d.sem, d.issued)
                    self.waited_dma[e][d] = d.issued


def ffn_groups():
    return [(0, 4), (4, 4), (8, 4), (12, 4), (16, 4), (20, 2)]


def build_nc(cfg):
    nb = cfg.get("nb", NB)
    layers = cfg.get("layers", list(range(DEPTH)))
    subs = cfg.get("subs", ("ffn1", "mix", "ffn2"))
    nc = bass.Bass("TRN2", target_bir_lowering=False)

    def din(name, shape, dt=F32):
        return nc.dram_tensor(name, list(shape), dt, kind="ExternalInput").ap()

    xT = din("xT", [nb, D, SEQ])
    ctxT = din("ctxT", [nb, D, CTX])
    cT = din("cT", [128, NCH * 5])
    bmodT = din("bmodT", [128, DEPTH * 72])
    ngT = din("ngT", [128, DEPTH * 3 * NCH])
    w_mod = din("w_mod", [DEPTH, D, NMOD * D])
    ffn_w_in = din("ffn_w_in", [DEPTH, 2, D, 2 * DFF])
    ffn_w_out = din("ffn_w_out", [DEPTH, 2, DFF, D])
    w_in = din("w_in", [DEPTH, D, PROJ_W])
    w_bc = din("w_bc", [DEPTH, 256, D])
    w_bg = din("w_bg", [DEPTH, 256, D])
    w_ba = din("w_ba", [DEPTH, 512, D])
    w_o = din("w_o", [DEPTH, D, D])
    bgT = din("bgT", [128, DEPTH * 3 * NCH])
    cwT = din("cwT", [128, DEPTH * 2 * 3])
    lnB_d = din("lnB", [DEPTH, 128, 512])
    wsT_d = din("wsT", [DEPTH, 128, 512])
    bsB_d = din("bsB", [DEPTH, 128, 256])
    qkg_d = din("qkg", [DEPTH, 128, 640])
    sinkB_d = din("sinkB", [DEPTH, 128, 8])
    ropeC_d = din("ropeC", [128, 16 * 64])
    ropeS_d = din("ropeS", [128, 16 * 64])
    maskP_d = din("maskP", [128, 128])
    maskN_d = din("maskN", [128, 128])
    ident_d = din("ident", [128, 128])
    yT = nc.dram_tensor("yT", [nb, D, SEQ], F32, kind="ExternalOutput").ap()
    dumped = []
    if cfg.get("dump_y"):
        dbgY = nc.dram_tensor("dbgY", [128, NCH * T], F32, kind="ExternalOutput").ap()
        dbgZ = nc.dram_tensor("dbgZ", [128, NCH * T], F32, kind="ExternalOutput").ap()

    _uid = [0]

    def uname(name):
        _uid[0] += 1
        return "s%d_%s" % (_uid[0], name)

    with ExitStack() as st:
        S = Sched(nc, st)

        def sb(name, shape, dt):
            return st.enter_context(nc.sbuf_tensor(uname(name), list(shape), dt))

        H = sb("H", [128, NCH * T], F32)
        Hv = H[:].rearrange("p (c t) -> p c t", c=NCH)
        modT = sb("modT", [128, DEPTH * 72 * 5], F32)
        modv = modT[:].rearrange("p (l m c j) -> p l m c j", l=DEPTH, m=NMOD, c=NCH)
        ps = [st.enter_context(nc.psum_tensor("ps%d" % i, [128, 512], F32)) for i in range(8)]
        pst = [TT(psum=True) for _ in range(8)]
        hT = [[TT() for _ in TILES] for _ in range(NCH)]
        t_mod = TT()
        t_const = TT()
        d_w = [S.dma_sem("dw%d" % i) for i in range(2)]
        d_misc = S.dma_sem("dmisc")
        d_x = S.dma_sem("dx")
        d_out = S.dma_sem("dout")

        d_misc2 = S.dma_sem("dmisc2")
        d_w2 = [S.dma_sem("dw%d" % i) for i in range(2, 4)]
        ones_b = sb("ones_b", [128, 128], BF16)
        S.op("dve", lambda: nc.vector.memset(ones_b[:], 1.0), writes=[t_const])
        ident = sb("ident", [128, 128], BF16)
        maskP = sb("maskP", [128, 128], BF16)
        maskN = sb("maskN", [128, 128], BF16)
        bg_s = sb("bg_s", [128, DEPTH * 3 * NCH], F32)
        cw_s = sb("cw_s", [128, DEPTH * 2 * 3], F32)
        S.op("pool", lambda: nc.gpsimd.dma_start(out=ident[:], in_=ident_d), writes=[t_const], dma=d_misc2)
        S.op("pool", lambda: nc.gpsimd.dma_start(out=maskP[:], in_=maskP_d), writes=[t_const], dma=d_misc2)
        S.op("pool", lambda: nc.gpsimd.dma_start(out=maskN[:], in_=maskN_d), writes=[t_const], dma=d_misc2)
        S.op("sp", lambda: nc.sync.dma_start(out=bg_s[:], in_=bgT), writes=[t_const], dma=d_misc)
        S.op("sp", lambda: nc.sync.dma_start(out=cw_s[:], in_=cwT), writes=[t_const], dma=d_misc)

        with ExitStack() as ps_st:
            def psb(name, shape, dt):
                return ps_st.enter_context(nc.sbuf_tensor(uname(name), list(shape), dt))
            cT_s = psb("cT_s", [128, NCH * 5], F32)
            scT = psb("scT", [128, NCH * 5], BF16)
            bm_s = psb("bm_s", [128, DEPTH * 72], F32)
            ng_s = psb("ng_s", [128, DEPTH * 3 * NCH], F32)
            wm = [psb("wm%d" % i, [128, NCH * 1024], BF16) for i in range(2)]
            t_c, t_sc, t_bm, t_ng = TT(), TT(), TT(), TT()
            t_wm = [TT(), TT()]
            S.op("sp", lambda: nc.sync.dma_start(out=cT_s[:], in_=cT), writes=[t_c], dma=d_misc)
            S.op("sp", lambda: nc.sync.dma_start(out=bm_s[:], in_=bmodT), writes=[t_bm], dma=d_misc)
            S.op("sp", lambda: nc.sync.dma_start(out=ng_s[:], in_=ngT), writes=[t_ng], dma=d_misc)
            S.op("act", lambda: nc.scalar.activation(out=scT[:], in_=cT_s[:], func=AF.Silu), reads=[t_c], writes=[t_sc])
            scv = scT[:].rearrange("p (k j) -> p k j", k=NCH)
            blk = 0
            for l in range(DEPTH):
                wsrc = w_mod[l].rearrange("(k p) n -> p k n", p=128)
                for cb in range(9):
                    s = blk % 2
                    blk += 1
                    wv = wm[s][:].rearrange("p (k n) -> p k n", k=NCH)
                    S.op("pool", (lambda wv=wv, wsrc=wsrc, cb=cb: nc.gpsimd.dma_start(out=wv, in_=wsrc[:, :, cb * 1024:(cb + 1) * 1024])),
                         writes=[t_wm[s]], dma=d_w[s])
                    pt = ps[cb % 2]
                    for oc in range(8):
                        for k in range(NCH):
                            S.op("pe", (lambda pt=pt, wv=wv, oc=oc, k=k: nc.tensor.matmul(
                                pt[:, oc * 5:(oc + 1) * 5], lhsT=wv[:, k, oc * 128:(oc + 1) * 128], rhs=scv[:, k, :],
                                start=(k == 0), stop=(k == NCH - 1))),
                                reads=[t_wm[s], t_sc], writes=[pst[cb % 2]])
                    bsl = bm_s[:, l * 72 + cb * 8: l * 72 + cb * 8 + 8]
                    S.op("dve", (lambda pt=pt, l=l, cb=cb, bsl=bsl: nc.vector.tensor_tensor(
                        out=modv[:, l, cb, :, :], in0=pt[:, 0:40].rearrange("p (c j) -> p c j", c=NCH),
                        in1=bsl.unsqueeze(2).broadcast_to([128, NCH, 5]), op=ALU.add)),
                        reads=[pst[cb % 2], t_bm], writes=[t_mod])
                for i in range(3):
                    ngs = ng_s[:, (l * 3 + i) * NCH:(l * 3 + i + 1) * NCH]
                    S.op("dve", (lambda l=l, i=i, ngs=ngs: nc.vector.scalar_tensor_tensor(
                        out=modv[:, l, 3 * i + 1, :, :], in0=modv[:, l, 3 * i + 1, :, :], scalar=1.0,
                        in1=ngs.unsqueeze(2).broadcast_to([128, NCH, 5]), op0=ALU.add, op1=ALU.mult)),
                        reads=[t_ng, t_mod], writes=[t_mod])
                for i in (0, 2):
                    S.op("dve", (lambda l=l, i=i: nc.vector.tensor_scalar(
                        out=modv[:, l, 3 * i + 2, :, :], in0=modv[:, l, 3 * i + 2, :, :], scalar1=0.5, scalar2=None,
                        op0=ALU.mult)), reads=[t_mod], writes=[t_mod])
            S.barrier()

        def mcol(l, m, c, j):
            return modv[:, l, m, c, j:j + 1]

        def emit_norm(l, i, j_of_tile, tiles, zv, zT, tmp, zdst=None, pbf=None):
            if zdst is None:
                zdst = lambda c, ti: (zv[:, c, TILES[ti][0]:TILES[ti][0] + TILES[ti][1]], zT[c][ti])
            for ti in tiles:
                t0, w, isctx = TILES[ti]
                j = j_of_tile(ti)
                pb = (4 + (ti % 2)) if pbf is None else pbf(ti)
                for c in range(NCH):
                    sq, tsq = tmp["sq"][c % 2]
                    S.op("act", (lambda sq=sq, c=c, t0=t0, w=w: nc.scalar.activation(
                        out=sq[:, 0:w], in_=Hv[:, c, t0:t0 + w], func=AF.Square)), reads=[hT[c][ti]], writes=[tsq])
                    S.op("pe", (lambda sq=sq, c=c, w=w, pb=pb: nc.tensor.matmul(
                        ps[pb][:, 0:w], lhsT=ones_b[:], rhs=sq[:, 0:w], start=(c == 0), stop=(c == NCH - 1))),
                        reads=[tsq, t_const], writes=[pst[pb]])
                sd, tsd = tmp["sd"]
                rs, trs = tmp["rstd"][ti % 2]
                S.op("act", (lambda sd=sd, w=w, pb=pb: nc.scalar.activation(
                    out=sd[:, 0:w], in_=ps[pb][:, 0:w], func=AF.Ln, bias=tmp["eps"][:, 0:1], scale=1.0 / D)),
                    reads=[pst[pb], t_const], writes=[tsd])
                S.op("act", (lambda sd=sd, rs=rs, w=w: nc.scalar.activation(out=rs[:, 0:w], in_=sd[:, 0:w], func=AF.Exp, scale=-0.5)),
                     reads=[tsd], writes=[trs])
                for c in range(NCH):
                    tb, ttb = tmp["t"][c % 2]
                    S.op("dve", (lambda tb=tb, rs=rs, c=c, t0=t0, w=w: nc.vector.tensor_tensor(
                        out=tb[:, 0:w], in0=Hv[:, c, t0:t0 + w], in1=rs[:, 0:w], op=ALU.mult)),
                        reads=[hT[c][ti], trs], writes=[ttb])
                    zap, ztt = zdst(c, ti)
                    S.op("act", (lambda tb=tb, c=c, w=w, j=j, zap=zap: nc.scalar.activation(
                        out=zap, in_=tb[:, 0:w], func=AF.Identity,
                        bias=mcol(l, 3 * i, c, j), scale=mcol(l, 3 * i + 1, c, j))),
                        reads=[ttb, t_mod], writes=[ztt])

        def emit_ffn(l, fi, b, tiles):
            i = 0 if fi == 0 else 2
            with ExitStack() as fs:
                def fsb(name, shape, dt):
                    return fs.enter_context(nc.sbuf_tensor(uname(name), list(shape), dt))
                Z = fsb("Z", [128, NCH * T], BF16)
                zv = Z[:].rearrange("p (c t) -> p c t", c=NCH)
                zT = [[TT() for _ in TILES] for _ in range(NCH)]
                W = [fsb("W%d" % s, [128, 3 * 4096], BF16) for s in range(2)]
                tW = [TT(), TT()]
                G = [[(fsb("g%d_%d" % (p, j), [128, 512], BF16), TT()) for j in range(4)] for p in range(2)]
                SA = [(fsb("sa%d" % k, [128, 512], F32), TT()) for k in range(2)]
                epsb = fsb("epsb", [128, 1], F32)
                tmp = {
                    "sq": [(fsb("sq%d" % k, [128, 512], BF16), TT()) for k in range(2)],
                    "t": [(fsb("tt%d" % k, [128, 512], F32), TT()) for k in range(2)],
                    "sd": (fsb("sd", [128, 512], F32), TT()),
                    "rstd": [(fsb("rstd%d" % k, [128, 512], F32), TT()) for k in range(2)],
                    "eps": epsb,
                }
                S.op("dve", lambda: nc.vector.memset(epsb[:], EPS), writes=[t_const])
                jf = (lambda ti: 4 if TILES[ti][2] else b)
                win = ffn_w_in[l, fi].rearrange("(k p) n -> p k n", p=128)
                wout = ffn_w_out[l, fi].rearrange("(f p) n -> p f n", p=128)
                groups = ffn_groups()

                def load_group(gi):
                    g0, G_ = groups[gi]
                    s = gi % 2
                    n = G_ * 128
                    wa = W[s][:, 0:NCH * n].rearrange("p (k n) -> p k n", k=NCH)
                    wu = W[s][:, 4096:4096 + NCH * n].rearrange("p (k n) -> p k n", k=NCH)
                    wo = W[s][:, 8192:8192 + G_ * 1024].rearrange("p (f n) -> p f n", f=G_)
                    S.op("pool", (lambda: nc.gpsimd.dma_start(out=wa, in_=win[:, :, g0 * 128:g0 * 128 + n])), writes=[tW[s]], dma=d_w[s])
                    S.op("pool", (lambda: nc.gpsimd.dma_start(out=wu, in_=win[:, :, DFF + g0 * 128:DFF + g0 * 128 + n])), writes=[tW[s]], dma=d_w[s])
                    S.op("pool", (lambda: nc.gpsimd.dma_start(out=wo, in_=wout[:, g0:g0 + G_, :])), writes=[tW[s]], dma=d_w[s])
                    return wa, wu, wo

                wviews = {0: load_group(0)}
                emit_norm(l, i, jf, tiles[0:1], zv, zT, tmp)

                def emit_au(gi, ti, par):
                    g0, G_ = groups[gi]
                    s = gi % 2
                    wa, wu, wo = wviews[gi]
                    t0, w, isctx = TILES[ti]
                    for jj in range(G_):
                        pa, pu = jj % 2, 2 + jj % 2
                        for k in range(NCH):
                            S.op("pe", (lambda k=k, jj=jj, pa=pa: nc.tensor.matmul(
                                ps[pa][:, 0:w], lhsT=wa[:, k, jj * 128:(jj + 1) * 128], rhs=zv[:, k, t0:t0 + w],
                                start=(k == 0), stop=(k == NCH - 1))), reads=[tW[s], zT[k][ti]], writes=[pst[pa]])
                        for k in range(NCH):
                            S.op("pe", (lambda k=k, jj=jj, pu=pu: nc.tensor.matmul(
                                ps[pu][:, 0:w], lhsT=wu[:, k, jj * 128:(jj + 1) * 128], rhs=zv[:, k, t0:t0 + w],
                                start=(k == 0), stop=(k == NCH - 1))), reads=[tW[s], zT[k][ti]], writes=[pst[pu]])
                        sa, tsa = SA[jj % 2]
                        gb, tgb = G[par][jj]
                        S.op("act", (lambda sa=sa, pa=pa: nc.scalar.activation(out=sa[:, 0:w], in_=ps[pa][:, 0:w], func=AF.Silu)),
                             reads=[pst[pa]], writes=[tsa])
                        S.op("dve", (lambda sa=sa, gb=gb, pu=pu: nc.vector.tensor_tensor(
                            out=gb[:, 0:w], in0=sa[:, 0:w], in1=ps[pu][:, 0:w], op=ALU.mult)),
                            reads=[tsa, pst[pu]], writes=[tgb])

                def emit_o(gi, ti, par):
                    g0, G_ = groups[gi]
                    s = gi % 2
                    wa, wu, wo = wviews[gi]
                    t0, w, isctx = TILES[ti]
                    j = jf(ti)
                    for c in range(NCH):
                        po = 4 + c % 4
                        for jj in range(G_):
                            gb, tgb = G[par][jj]
                            S.op("pe", (lambda c=c, jj=jj, gb=gb, po=po: nc.tensor.matmul(
                                ps[po][:, 0:w], lhsT=wo[:, jj, c * 128:(c + 1) * 128], rhs=gb[:, 0:w],
                                start=(jj == 0), stop=(jj == G_ - 1))), reads=[tW[s], tgb], writes=[pst[po]])
                        S.op("dve", (lambda c=c, po=po: nc.vector.scalar_tensor_tensor(
                            out=Hv[:, c, t0:t0 + w], in0=ps[po][:, 0:w], scalar=mcol(l, 3 * i + 2, c, j),
                            in1=Hv[:, c, t0:t0 + w], op0=ALU.mult, op1=ALU.add)),
                            reads=[pst[po], hT[c][ti], t_mod], writes=[hT[c][ti]])

                steps = [(gi, ti) for gi in range(len(groups)) for ti in tiles]
                prev = None
                for n, (gi, ti) in enumerate(steps):
                    if gi == 0 and n + 1 < len(tiles):
                        emit_norm(l, i, jf, tiles[n + 1:n + 2], zv, zT, tmp)
                    emit_au(gi, ti, n % 2)
                    if prev is not None:
                        emit_o(prev[0], prev[1], prev[2])
                    if ti == tiles[0] and gi + 1 < len(groups):
                        wviews[gi + 1] = load_group(gi + 1)
                    prev = (gi, ti, n % 2)
                emit_o(prev[0], prev[1], prev[2])
                S.barrier()

        def emit_mixer(l, b, last):
            jf = (lambda ti: 4 if TILES[ti][2] else b)
            tiles1 = [4, 0, 1, 2, 3]
            tiles2 = [0, 1, 2, 3] if last else [4, 0, 1, 2, 3]
            with ExitStack() as ms:
                def msb(name, shape, dt):
                    return ms.enter_context(nc.sbuf_tensor(uname(name), list(shape), dt))
                Y = msb("Y", [128, NCH * T], BF16)
                yv = Y[:].rearrange("p (c t) -> p c t", c=NCH)
                NBLK = T // 128
                yT = [[TT() for _ in range(NBLK)] for _ in range(NCH)]
                epsb = msb("epsb_m", [128, 1], F32)
                S.op("dve", lambda: nc.vector.memset(epsb[:], EPS), writes=[t_const])
                t_lp = TT()

                def ycols(tb):
                    return slice(tb * 128, (tb + 1) * 128)

                with ExitStack() as p1:
                    def sb1(name, shape, dt):
                        return p1.enter_context(nc.sbuf_tensor(uname(name), list(shape), dt))
                    qkg = sb1("qkg", [128, 640], F32)
                    esk = sb1("esk", [128, 8], F32)
                    S.op("sp", lambda: nc.sync.dma_start(out=qkg[:], in_=qkg_d[l]), writes=[t_lp], dma=d_misc)
                    S.op("sp", lambda: nc.sync.dma_start(out=esk[:], in_=sinkB_d[l]), writes=[t_lp], dma=d_misc)
                    S.op("act", lambda: nc.scalar.activation(out=esk[:], in_=esk[:], func=AF.Exp), reads=[t_lp], writes=[t_lp])
                    qkgv = qkg[:].rearrange("p (h d) -> p h d", h=10)
                    ropeC = sb1("ropeC", [128, 16 * 64], F32)
                    ropeS = sb1("ropeS", [128, 16 * 64], F32)
                    t_rope = TT()
                    S.op("sp", lambda: nc.sync.dma_start(out=ropeC[:], in_=ropeC_d), writes=[t_rope], dma=d_misc)
                    S.op("sp", lambda: nc.sync.dma_start(out=ropeS[:], in_=ropeS_d), writes=[t_rope], dma=d_misc)
                    kT = sb1("kT", [128, T], BF16)
                    Vt = sb1("Vt", [128, NBLK * 128], BF16)
                    Vv = Vt[:].rearrange("p (b n) -> p b n", b=NBLK)
                    kT_t = [TT() for _ in range(NBLK)]
                    V_t = [TT() for _ in range(NBLK)]
                    W2 = sb1("W2", [128, NCH * 768], BF16)
                    W2v = W2[:].rearrange("p (k n) -> p k n", k=NCH)
                    t_W2 = TT()
                    S.op("pool", lambda: nc.gpsimd.dma_start(out=W2v, in_=w_in[l].rearrange("(k p) n -> p k n", p=128)[:, :, OFF_Q:OFF_GATE]),
                         writes=[t_W2], dma=d_w[0])
                    ZT = [sb1("zt%d" % k, [128, NCH * 512], BF16) for k in range(2)]
                    ZTv = [z[:].rearrange("p (c t) -> p c t", c=NCH) for z in ZT]
                    ZT_t = [[TT() for _ in range(NCH)] for _ in range(2)]
                    tmp = {
                        "sq": [(sb1("sq%d" % k, [128, 512], BF16), TT()) for k in range(2)],
                        "t": [(sb1("tt%d" % k, [128, 512], F32), TT()) for k in range(2)],
                        "sd": (sb1("sd", [128, 512], F32), TT()),
                        "rstd": [(sb1("rstd0", [128, 512], F32), TT())] * 2,
                        "eps": epsb,
                    }
                    CH = []
                    for p_ in range(2):
                        CH.append({
                            "qk": (sb1("qk%d" % p_, [128, 640], F32), TT()),
                            "t2": (sb1("t2%d" % p_, [128, 640], F32), TT()),
                            "qrot": (sb1("qrot%d" % p_, [128, 640], BF16), TT()),
                            "ssq": (sb1("ssq%d" % p_, [128, 10], F32), TT()),
                            "sdq": (sb1("sdq%d" % p_, [128, 10], F32), TT()),
                            "rsq": (sb1("rsq%d" % p_, [128, 10], F32), TT()),
                        })
                    QT = [(sb1("qT%d" % k, [128, 512], BF16), TT()) for k in range(4)]
                    PT = [(sb1("PT%d" % k, [128, 512], BF16), TT()) for k in range(10)]
                    DS = [(sb1("dsum%d" % k, [128, 512], F32), TT()) for k in range(2)]
                    RD = [(None, None)]
                    ps2b = ps[2][:, :].bitcast(BF16)

                    CHE = cfg.get("chain_eng", "pool")
                    CHN = nc.vector if CHE == "dve" else nc.gpsimd

                    def proj_block(ti, sbk, tb, need_q, par):
                        zs = tiles1.index(ti) % 2
                        isctx = TILES[ti][2]
                        B_ = CH[par]
                        qk, t_qk = B_["qk"]; qn, t_qn = B_["qk"]; t2, t_t2 = B_["t2"]; qrot, t_qrot = B_["qrot"]
                        ssq, t_ssq = B_["ssq"]; sdq, t_sdq = B_["sdq"]; rsq, t_rsq = B_["rsq"]
                        if need_q:
                            for k in range(NCH):
                                S.op("pe", (lambda k=k: nc.tensor.matmul(ps[0][:, 0:512], lhsT=ZTv[zs][:, k, sbk * 128:(sbk + 1) * 128],
                                                                         rhs=W2v[:, k, 0:512], start=(k == 0), stop=(k == NCH - 1))),
                                     reads=[ZT_t[zs][k], t_W2], writes=[pst[0]])
                        for k in range(NCH):
                            S.op("pe", (lambda k=k: nc.tensor.matmul(ps[1][:, 0:256], lhsT=ZTv[zs][:, k, sbk * 128:(sbk + 1) * 128],
                                                                     rhs=W2v[:, k, 512:768], start=(k == 0), stop=(k == NCH - 1))),
                                 reads=[ZT_t[zs][k], t_W2], writes=[pst[1]])
                        if need_q:
                            S.op("act", lambda: nc.scalar.copy(out=qk[:, 0:512], in_=ps[0][:, 0:512]), reads=[pst[0]], writes=[t_qk])
                        S.op("act", lambda: nc.scalar.copy(out=qk[:, 512:640], in_=ps[1][:, 0:128]), reads=[pst[1]], writes=[t_qk])
                        S.op("act", lambda: nc.scalar.copy(out=Vv[:, tb, :], in_=ps[1][:, 128:256]), reads=[pst[1]], writes=[V_t[tb]])
                        lo = 0 if need_q else 512
                        h0 = lo // 64
                        nh = (640 - lo) // 64
                        S.op("dve", lambda: nc.vector.tensor_tensor(out=t2[:, lo:640], in0=qk[:, lo:640], in1=qk[:, lo:640], op=ALU.mult),
                             reads=[t_qk], writes=[t_t2])
                        S.op("dve", lambda: nc.vector.reduce_sum(out=ssq[:, h0:10], in_=t2[:, lo:640].rearrange("p (h d) -> p h d", d=64),
                                                                 axis=mybir.AxisListType.X), reads=[t_t2], writes=[t_ssq])
                        S.op("act", lambda: nc.scalar.activation(out=sdq[:, h0:10], in_=ssq[:, h0:10], func=AF.Ln, bias=epsb[:, 0:1], scale=1.0 / 64),
                             reads=[t_ssq, t_const], writes=[t_sdq])
                        S.op("act", lambda: nc.scalar.activation(out=rsq[:, h0:10], in_=sdq[:, h0:10], func=AF.Exp, scale=-0.5), reads=[t_sdq], writes=[t_rsq])
                        qkv3 = qk[:, lo:640].rearrange("p (h d) -> p h d", d=64)
                        qn3 = qn[:, lo:640].rearrange("p (h d) -> p h d", d=64)
                        S.op(CHE, lambda: CHN.tensor_tensor(out=qn3, in0=qkv3, in1=rsq[:, h0:10].unsqueeze(2).broadcast_to([128, nh, 64]), op=ALU.mult),
                             reads=[t_qk, t_rsq], writes=[t_qn])
                        S.op(CHE, lambda: CHN.tensor_tensor(out=qn3, in0=qn3, in1=qkgv[:, h0:10, :], op=ALU.mult),
                             reads=[t_qn, t_lp], writes=[t_qn])
                        qr_q = qrot[:, 0:512].rearrange("p (hh kv d) -> p kv hh d", hh=4, kv=2)
                        if isctx:
                            if need_q:
                                S.op(CHE, lambda: CHN.tensor_copy(out=qr_q, in_=qn[:, 0:512].rearrange("p (kv hh d) -> p kv hh d", kv=2, hh=4)),
                                     reads=[t_qn], writes=[t_qrot])
                            S.op(CHE, lambda: CHN.tensor_copy(out=qrot[:, 512:640], in_=qn[:, 512:640]), reads=[t_qn], writes=[t_qrot])
                        else:
                            cosb = ropeC[:, tb * 64:(tb + 1) * 64]
                            sinb = ropeS[:, tb * 64:(tb + 1) * 64]
                            qn4 = qn[:, lo:640].rearrange("p (h s a d) -> p h s a d", s=2, a=2, d=16)
                            t24 = t2[:, lo:640].rearrange("p (h s a d) -> p h s a d", s=2, a=2, d=16)
                            sin4 = sinb.rearrange("p (s a d) -> p s a d", s=2, a=2)
                            for a in range(2):
                                S.op(CHE, (lambda a=a: CHN.tensor_tensor(
                                    out=t24[:, :, :, a, :], in0=qn4[:, :, :, 1 - a, :],
                                    in1=sin4[:, :, a, :].unsqueeze(1).broadcast_to([128, nh, 2, 16]), op=ALU.mult)),
                                    reads=[t_qn, t_rope, t_ssq], writes=[t_t2])
                            S.op(CHE, lambda: CHN.tensor_tensor(out=qn3, in0=qn3, in1=cosb.unsqueeze(1).broadcast_to([128, nh, 64]), op=ALU.mult),
                                 reads=[t_qn, t_rope, t_t2], writes=[t_qn])
                            if need_q:
                                S.op(CHE, lambda: CHN.tensor_tensor(
                                    out=qr_q, in0=qn[:, 0:512].rearrange("p (kv hh d) -> p kv hh d", kv=2, hh=4),
                                    in1=t2[:, 0:512].rearrange("p (kv hh d) -> p kv hh d", kv=2, hh=4), op=ALU.add),
                                    reads=[t_qn, t_t2], writes=[t_qrot])
                            S.op(CHE, lambda: CHN.tensor_tensor(out=qrot[:, 512:640], in0=qn[:, 512:640], in1=t2[:, 512:640], op=ALU.add),
                                 reads=[t_qn, t_t2], writes=[t_qrot])

                    def trans_block(tb, need_q, par):
                        qrot, t_qrot = CH[par]["qrot"]
                        if need_q:
                            for hh in range(4):
                                S.op("pe", (lambda hh=hh: nc.tensor.transpose(out=ps2b[:, hh * 128:(hh + 1) * 128], in_=qrot[:, hh * 128:(hh + 1) * 128], identity=ident[:])),
                                     reads=[t_qrot, t_const], writes=[pst[2]])
                        S.op("pe", lambda: nc.tensor.transpose(out=ps2b[:, 512:640], in_=qrot[:, 512:640], identity=ident[:]),
                             reads=[t_qrot, t_const], writes=[pst[2]])
                        if need_q:
                            qt, tqt = QT[tb % 4]
                            S.op("act", lambda: nc.scalar.copy(out=qt[:], in_=ps2b[:, 0:512]), reads=[pst[2]], writes=[tqt])
                        S.op("act", lambda: nc.scalar.copy(out=kT[:, tb * 128:(tb + 1) * 128], in_=ps2b[:, 512:640]), reads=[pst[2]], writes=[kT_t[tb]])

                    def attn_block(tb):
                        isctx = tb >= 16
                        if isctx:
                            keys = [(16, None), (17, None)]
                        else:
                            keys = []
                            if tb > 0:
                                keys.append((tb - 1, maskP))
                            keys.append((tb, None))
                            if tb < 15:
                                keys.append((tb + 1, maskN))
                            keys += [(16, None), (17, None)]
                        nk = len(keys)
                        qt, tqt = QT[tb % 4]

                        SB = [4, 5, 3]

                        def s_exp(kv, idx):
                            rows = slice(kv * 64, (kv + 1) * 64)
                            kb, mask = keys[idx]
                            sp_ = SB[idx % 3]
                            pt, tpt = PT[kv * 5 + idx]
                            S.op("pe", lambda: nc.tensor.matmul(ps[sp_][:, 0:512], lhsT=kT[rows, kb * 128:(kb + 1) * 128], rhs=qt[rows, :], start=True, stop=True),
                                 reads=[kT_t[kb], tqt], writes=[pst[sp_]])
                            S.op("act", lambda: nc.scalar.activation(out=pt[:], in_=ps[sp_][:, 0:512], func=AF.Exp, scale=0.125), reads=[pst[sp_]], writes=[tpt])
                            if mask is not None:
                                S.op("dve", lambda: nc.vector.tensor_tensor(
                                    out=pt[:].rearrange("p (h q) -> p h q", h=4), in0=pt[:].rearrange("p (h q) -> p h q", h=4),
                                    in1=mask[:].unsqueeze(1).broadcast_to([128, 4, 128]), op=ALU.mult), reads=[tpt, t_const], writes=[tpt])

                        def den_mm(kv, idx):
                            pt, tpt = PT[kv * 5 + idx]
                            S.op("pe", lambda: nc.tensor.matmul(ps[6][:, 0:512], lhsT=ones_b[:], rhs=pt[:], start=(idx == 0), stop=(idx == nk - 1)),
                                 reads=[tpt, t_const], writes=[pst[6]])

                        def o_mm(kv):
                            for hh in range(4):
                                j = 2 * kv + hh // 2
                                half = slice((hh % 2) * 64, (hh % 2) * 64 + 64)
                                for idx, (kb, mask) in enumerate(keys):
                                    pt, tpt = PT[kv * 5 + idx]
                                    S.op("pe", (lambda kb=kb, pt=pt, idx=idx, j=j, half=half, hh=hh: nc.tensor.matmul(
                                        ps[7][half, j * 128:(j + 1) * 128], lhsT=Vv[:, kb, kv * 64:(kv + 1) * 64], rhs=pt[:, hh * 128:(hh + 1) * 128],
                                        start=(idx == 0), stop=(idx == nk - 1))),
                                        reads=[V_t[kb], tpt], writes=[pst[7]])

                        def den_fin(kv):
                            ds, tds = DS[kv]
                            rd, trd = RD[0]
                            S.op("dve", lambda: nc.vector.tensor_tensor(
                                out=ds[:].rearrange("p (h q) -> p h q", h=4), in0=ps[6][:, 0:512].rearrange("p (h q) -> p h q", h=4),
                                in1=esk[:, 4 * kv:4 * kv + 4].unsqueeze(2).broadcast_to([128, 4, 128]), op=ALU.add),
                                reads=[pst[6], t_lp], writes=[tds])
                            S.op("act", lambda: nc.scalar.activation(out=ds[:], in_=ds[:], func=AF.Ln), reads=[tds], writes=[tds])
                            S.op("act", lambda: nc.scalar.activation(out=ds[:], in_=ds[:], func=AF.Exp, scale=-1.0), reads=[tds], writes=[tds])

                        def o_fin(kv):
                            ds, tds = DS[kv]
                            rd4 = ds[:].rearrange("p (a b q) -> p a b q", a=2, b=2)
                            for hb in range(2):
                                half = slice(hb * 64, hb * 64 + 64)
                                S.op("dve", (lambda hb=hb, half=half: nc.vector.tensor_tensor(
                                    out=yv[half, 4 + 2 * kv:4 + 2 * kv + 2, tb * 128:(tb + 1) * 128],
                                    in0=ps[7][half, 2 * kv * 128:(2 * kv + 2) * 128].rearrange("p (a q) -> p a q", a=2),
                                    in1=rd4[half, :, hb, :], op=ALU.mult)),
                                    reads=[pst[7], tds], writes=[yT[4 + 2 * kv][tb], yT[4 + 2 * kv + 1][tb]])

                        def s_phase(kv, first, last_):
                            for idx in range(first, last_):
                                s_exp(kv, idx)

                        LA = 2
                        for idx in range(min(LA, nk)):
                            s_exp(0, idx)
                        for idx in range(nk):
                            if idx + LA < nk:
                                s_exp(0, idx + LA)
                            den_mm(0, idx)
                        den_fin(0)
                        for idx in range(min(LA, nk)):
                            s_exp(1, idx)
                        o_mm(0)
                        for idx in range(nk):
                            if idx + LA < nk:
                                s_exp(1, idx + LA)
                            den_mm(1, idx)
                        den_fin(1)
                        o_fin(0)
                        o_mm(1)
                        o_fin(1)

                    seq = []
                    for ti in tiles1:
                        t0, w, isctx = TILES[ti]
                        for sbk in range(w // 128):
                            seq.append((ti, sbk, t0 // 128 + sbk))
                    normed = set()

                    def do_norm(ti):
                        if ti in normed:
                            return
                        normed.add(ti)
                        zs = tiles1.index(ti) % 2
                        w = TILES[ti][1]
                        emit_norm(l, 1, jf, [ti], None, None, tmp,
                                  zdst=(lambda c, ti_, zs=zs, w=w: (ZTv[zs][:, c, 0:w], ZT_t[zs][c])), pbf=(lambda ti_: 3))

                    projected = set()
                    done_attn = set()
                    attn_wanted = set(range(16)) | (set() if last else {16, 17})

                    def ready_blocks():
                        out = []
                        for tb in sorted(attn_wanted - done_attn, key=lambda x: (x < 16, x)):
                            need = {16, 17} if tb >= 16 else ({tb, 16, 17} | ({tb - 1} if tb > 0 else set()) | ({tb + 1} if tb < 15 else set()))
                            if need <= projected:
                                out.append(tb)
                        return out

                    for n, (ti, sbk, tb) in enumerate(seq):
                        do_norm(ti)
                        if sbk == 1 or TILES[ti][1] == 256:
                            nxt = tiles1.index(ti) + 1
                            if nxt < len(tiles1) and sbk >= (1 if TILES[ti][1] > 256 else 1):
                                do_norm(tiles1[nxt])
                        need_q = not (TILES[ti][2] and last)
                        proj_block(ti, sbk, tb, need_q, n % 2)
                        for rb in ready_blocks()[:2]:
                            attn_block(rb)
                            done_attn.add(rb)
                        trans_block(tb, need_q, n % 2)
                        projected.add(tb)
                    for rb in ready_blocks():
                        attn_block(rb)
                        done_attn.add(rb)
                    assert done_attn == attn_wanted
                    S.barrier()

                with ExitStack() as p2:
                    cur2 = [p2]

                    def sb2(name, shape, dt):
                        return cur2[0].enter_context(nc.sbuf_tensor(uname(name), list(shape), dt))
                    Z = sb2("Zm", [128, NCH * T], BF16)
                    zv = Z[:].rearrange("p (c t) -> p c t", c=NCH)
                    zT = [[TT() for _ in TILES] for _ in range(NCH)]
                    pw = ExitStack()
                    cur2[0] = pw
                    lnB = sb2("lnB", [128, 512], F32)
                    wsT = sb2("wsT", [128, 512], BF16)
                    bsB = sb2("bsB", [128, 256], F32)
                    t_lp2 = TT()
                    S.op("sp", lambda: nc.sync.dma_start(out=lnB[:], in_=lnB_d[l]), writes=[t_lp2], dma=d_misc)
                    S.op("pool", lambda: nc.gpsimd.dma_start(out=wsT[:], in_=wsT_d[l]), writes=[t_lp2], dma=d_misc2)
                    S.op("sp", lambda: nc.sync.dma_start(out=bsB[:], in_=bsB_d[l]), writes=[t_lp2], dma=d_misc)
                    W1 = sb2("W1", [128, NCH * 1280], BF16)
                    W1v = W1[:].rearrange("p (k n) -> p k n", k=NCH)
                    t_W1 = TT()
                    S.op("pool", lambda: nc.gpsimd.dma_start(out=W1v, in_=w_in[l].rearrange("(k p) n -> p k n", p=128)[:, :, 0:1280]),
                         writes=[t_W1], dma=d_w[1])
                    pn = ExitStack()
                    cur2[0] = pw
                    tmp = {
                        "sq": [(sb2("sq%d" % k, [128, 512], BF16), TT()) for k in range(2)],
                        "t": [(sb2("tt%d" % k, [128, 512], F32), TT()) for k in range(2)],
                        "sd": (sb2("sd", [128, 512], F32), TT()),
                        "rstd": [(sb2("rstd0", [128, 512], F32), TT())] * 2,
                        "eps": epsb,
                    }
                    UU = [(sb2("U%d" % k, [128, 2 * 512], BF16), TT()) for k in range(2)]
                    GB = []
                    for p_ in range(2):
                        GB.append({
                            "gv": (sb2("gv%d" % p_, [128, 256], F32), TT()),
                            "bst": (sb2("bst%d" % p_, [128, 6], F32), TT()),
                            "mv": (sb2("mvv%d" % p_, [128, 2], F32), TT()),
                            "sdl": (sb2("sdl%d" % p_, [128, 1], F32), TT()),
                            "rsl": (sb2("rsl%d" % p_, [128, 1], F32), TT()),
                            "vn": (sb2("vn%d" % p_, [128, 256], F32), TT()),
                            "vnb": (sb2("vnb%d" % p_, [128, 256], BF16), TT()),
                            "stmp": (sb2("stmp%d" % p_, [128, 256], F32), TT()),
                        })
                    hal = sb2("hal", [128, 8], F32); t_hal = TT()
                    tbuf = sb2("tbuf", [128, 2 * 514], F32); tbv = tbuf[:].rearrange("p (c t) -> p c t", c=2); t_tb = TT()
                    acc = sb2("acc", [128, 512], F32); t_acc = TT()
                    cwv = cw_s[:].rearrange("p (l c k) -> p l c k", l=DEPTH, c=2)

                    def gu_tile(ti, up):
                        t0, w, isctx = TILES[ti]
                        U, t_U = UU[up]
                        Uv = U[:].rearrange("p (c t) -> p c t", c=2)
                        for cc in range(2):
                            for k in range(NCH):
                                S.op("pe", (lambda k=k, cc=cc: nc.tensor.matmul(ps[cc][:, 0:w], lhsT=W1v[:, k, OFF_GU + cc * 128:OFF_GU + (cc + 1) * 128],
                                                                               rhs=zv[:, k, t0:t0 + w], start=(k == 0), stop=(k == NCH - 1))),
                                     reads=[t_W1, zT[k][ti]], writes=[pst[cc]])
                            S.op("act", (lambda cc=cc: nc.scalar.activation(out=Uv[:, cc, 0:w], in_=ps[cc][:, 0:w], func=AF.Gelu_apprx_tanh)),
                                 reads=[pst[cc]], writes=[t_U])

                    def stage1(ti, sbk, par):
                        t0, w, isctx = TILES[ti]
                        tok = slice(t0 + sbk * 128, t0 + (sbk + 1) * 128)
                        B_ = GB[par]
                        gv, t_gv = B_["gv"]; bst, t_bst = B_["bst"]; mvv, t_mv = B_["mv"]; sdl, t_sdl = B_["sdl"]; rsl, t_rsl = B_["rsl"]
                        vn, t_vn = B_["vn"]; vnb, t_vnb = B_["vnb"]
                        for k in range(NCH):
                            S.op("pe", (lambda k=k: nc.tensor.matmul(ps[2][:, 0:256], lhsT=zv[:, k, tok], rhs=W1v[:, k, OFF_GV:OFF_GV + 256],
                                                                     start=(k == 0), stop=(k == NCH - 1))),
                                 reads=[t_W1, zT[k][ti]], writes=[pst[2]])
                        S.op("act", lambda: nc.scalar.activation(out=gv[:], in_=ps[2][:, 0:256], func=AF.Gelu_apprx_tanh), reads=[pst[2]], writes=[t_gv])
                        S.op("dve", lambda: nc.vector.bn_stats(out=bst[:], in_=gv[:]), reads=[t_gv], writes=[t_bst])
                        S.op("dve", lambda: nc.vector.bn_aggr(out=mvv[:], in_=bst[:]), reads=[t_bst], writes=[t_mv])
                        S.op("act", lambda: nc.scalar.activation(out=sdl[:], in_=mvv[:, 1:2], func=AF.Ln, bias=epsb[:, 0:1], scale=1.0),
                             reads=[t_mv, t_const], writes=[t_sdl])
                        S.op("act", lambda: nc.scalar.activation(out=rsl[:], in_=sdl[:], func=AF.Exp, scale=-0.5), reads=[t_sdl], writes=[t_rsl])
                        S.op("dve", lambda: nc.vector.tensor_scalar(out=vn[:], in0=gv[:], scalar1=mvv[:, 0:1], scalar2=rsl[:, 0:1], op0=ALU.subtract, op1=ALU.mult),
                             reads=[t_gv, t_mv, t_rsl], writes=[t_vn])
                        S.op("pool", lambda: nc.gpsimd.tensor_tensor(out=vn[:], in0=vn[:], in1=lnB[:, 0:256], op=ALU.mult), reads=[t_vn, t_lp2], writes=[t_vn])
                        S.op("pool", lambda: nc.gpsimd.tensor_tensor(out=vnb[:], in0=vn[:], in1=lnB[:, 256:512], op=ALU.add), reads=[t_vn, t_lp2], writes=[t_vnb])

                    def stage2(ti, sbk, par, up):
                        t0, w, isctx = TILES[ti]
                        tb = t0 // 128 + sbk
                        tok = slice(t0 + sbk * 128, t0 + (sbk + 1) * 128)
                        B_ = GB[par]
                        vnb, t_vnb = B_["vnb"]; stmp, t_stmp = B_["stmp"]
                        U, t_U = UU[up]
                        Uv = U[:].rearrange("p (c t) -> p c t", c=2)
                        for g in range(4):
                            half = slice((g % 2) * 64, (g % 2) * 64 + 64)
                            S.op("pe", (lambda g=g, half=half: nc.tensor.matmul(ps[3][half, (g // 2) * 128:(g // 2 + 1) * 128], lhsT=vnb[:, g * 64:(g + 1) * 64],
                                                                                 rhs=wsT[:, g * 128:(g + 1) * 128], start=True, stop=True)),
                                 reads=[t_vnb, t_lp2], writes=[pst[3]])
                        S.op("dve", lambda: nc.vector.tensor_tensor(out=stmp[:], in0=ps[3][:, 0:256], in1=bsB[:], op=ALU.add), reads=[pst[3], t_lp2], writes=[t_stmp])
                        S.op("dve", lambda: nc.vector.tensor_tensor(
                            out=yv[:, 2:4, tok], in0=stmp[:].rearrange("p (c t) -> p c t", c=2), in1=Uv[:, :, sbk * 128:(sbk + 1) * 128], op=ALU.mult),
                            reads=[t_stmp, t_U], writes=[yT[2][tb], yT[3][tb]])

                    def conv_mm1(ti):
                        t0, w, isctx = TILES[ti]
                        has_l = t0 not in (0, SEQ)
                        has_r = (t0 + w) not in (SEQ, T)
                        for cc in range(2):
                            for k in range(NCH):
                                S.op("pe", (lambda k=k, cc=cc: nc.tensor.matmul(ps[4 + cc][:, 0:w], lhsT=W1v[:, k, OFF_CC + cc * 128:OFF_CC + (cc + 1) * 128],
                                                                               rhs=zv[:, k, t0:t0 + w], start=(k == 0), stop=(k == NCH - 1))),
                                     reads=[t_W1, zT[k][ti]], writes=[pst[4 + cc]])
                            S.op("act", (lambda cc=cc: nc.scalar.copy(out=tbv[:, cc, 1:w + 1], in_=ps[4 + cc][:, 0:w])), reads=[pst[4 + cc]], writes=[t_tb])
                        for cc in range(2):
                            for k in range(NCH):
                                S.op("pe", (lambda k=k, cc=cc: nc.tensor.matmul(ps[6 + cc][:, 0:w], lhsT=W1v[:, k, OFF_CH + cc * 128:OFF_CH + (cc + 1) * 128],
                                                                               rhs=zv[:, k, t0:t0 + w], start=(k == 0), stop=(k == NCH - 1))),
                                     reads=[t_W1, zT[k][ti]], writes=[pst[6 + cc]])
                            S.op("dve", (lambda cc=cc: nc.vector.tensor_tensor(out=tbv[:, cc, 1:w + 1], in0=tbv[:, cc, 1:w + 1], in1=ps[6 + cc][:, 0:w], op=ALU.mult)),
                                 reads=[t_tb, pst[6 + cc]], writes=[t_tb])
                        sides = []
                        if has_l:
                            sides.append((0, t0 - 1, ti - 1))
                        if has_r:
                            sides.append((1, t0 + w, ti + 1))
                        for side, col, nti in sides:
                            for which, off in ((0, OFF_CC), (1, OFF_CH)):
                                for cc in range(2):
                                    pc = side * 4 + which * 2 + cc
                                    for k in range(NCH):
                                        S.op("pe", (lambda k=k, cc=cc, off=off, col=col, pc=pc: nc.tensor.matmul(
                                            ps[2][:, 256 + pc:256 + pc + 1], lhsT=W1v[:, k, off + cc * 128:off + (cc + 1) * 128], rhs=zv[:, k, col:col + 1],
                                            start=(k == 0), stop=(k == NCH - 1))),
                                            reads=[t_W1, zT[k][nti]], writes=[pst[2]])
                        if sides:
                            S.op("act", lambda: nc.scalar.copy(out=hal[:], in_=ps[2][:, 256:264]), reads=[pst[2]], writes=[t_hal])
                        for side in (0, 1):
                            present = any(s_[0] == side for s_ in sides)
                            colt = 0 if side == 0 else w + 1
                            if present:
                                S.op("dve", (lambda side=side, colt=colt: nc.vector.tensor_tensor(
                                    out=tbv[:, :, colt], in0=hal[:, side * 4:side * 4 + 2], in1=hal[:, side * 4 + 2:side * 4 + 4], op=ALU.mult)),
                                    reads=[t_hal], writes=[t_tb])
                            else:
                                S.op("dve", (lambda colt=colt: nc.vector.memset(tbv[:, :, colt:colt + 1], 0.0)), writes=[t_tb])

                    def conv_mm2(ti):
                        t0, w, isctx = TILES[ti]
                        for cc in range(2):
                            for k in range(NCH):
                                S.op("pe", (lambda k=k, cc=cc: nc.tensor.matmul(ps[4 + cc][:, 0:w], lhsT=W1v[:, k, OFF_CB + cc * 128:OFF_CB + (cc + 1) * 128],
                                                                               rhs=zv[:, k, t0:t0 + w], start=(k == 0), stop=(k == NCH - 1))),
                                     reads=[t_W1, zT[k][ti]], writes=[pst[4 + cc]])
                            S.op("dve", (lambda cc=cc: nc.vector.tensor_scalar(out=acc[:, 0:w], in0=tbv[:, cc, 1:w + 1], scalar1=cwv[:, l, cc, 1:2], scalar2=None, op0=ALU.mult)),
                                 reads=[t_tb, t_const], writes=[t_acc])
                            S.op("dve", (lambda cc=cc: nc.vector.scalar_tensor_tensor(out=acc[:, 0:w], in0=tbv[:, cc, 0:w], scalar=cwv[:, l, cc, 0:1], in1=acc[:, 0:w],
                                                                                      op0=ALU.mult, op1=ALU.add)), reads=[t_tb, t_acc, t_const], writes=[t_acc])
                            S.op("dve", (lambda cc=cc: nc.vector.scalar_tensor_tensor(out=acc[:, 0:w], in0=tbv[:, cc, 2:w + 2], scalar=cwv[:, l, cc, 2:3], in1=acc[:, 0:w],
                                                                                      op0=ALU.mult, op1=ALU.add)), reads=[t_tb, t_acc, t_const], writes=[t_acc])
                            S.op("dve", (lambda cc=cc: nc.vector.tensor_tensor(out=yv[:, cc, t0:t0 + w], in0=acc[:, 0:w], in1=ps[4 + cc][:, 0:w], op=ALU.mult)),
                                 reads=[t_acc, pst[4 + cc]], writes=[yT[cc][tb_] for tb_ in range(t0 // 128, (t0 + w) // 128)])

                    norm_done = []

                    def norm_tile(idx_):
                        if idx_ < len(tiles2) and idx_ not in norm_done:
                            norm_done.append(idx_)
                            emit_norm(l, 1, jf, [tiles2[idx_]], zv, zT, tmp, pbf=(lambda ti_: 3))

                    nblk = 0
                    for pos, ti in enumerate(tiles2):
                        norm_tile(pos)
                        norm_tile(pos + 1)
                        for nb_ in (ti - 1, ti + 1):
                            if nb_ in tiles2:
                                norm_tile(tiles2.index(nb_))
                        up = pos % 2
                        gu_tile(ti, up)
                        nb = TILES[ti][1] // 128
                        pend = None
                        conv_steps = [conv_mm1, conv_mm2]
                        for sbk in range(nb):
                            stage1(ti, sbk, nblk % 2)
                            if pend is not None:
                                stage2(*pend)
                            if sbk < len(conv_steps):
                                conv_steps[sbk](ti)
                            pend = (ti, sbk, nblk % 2, up)
                            nblk += 1
                        stage2(*pend)
                    if cfg.get("dump_y") and not dumped:
                        dumped.append(1)
                        S.op("pool", lambda: nc.gpsimd.dma_start(out=dbgY, in_=Y[:]), reads=[t for row in yT for t in row], dma=d_out)
                        S.op("pool", lambda: nc.gpsimd.dma_start(out=dbgZ, in_=Z[:]), reads=[t for row in zT for t in row], dma=d_out)
                    S.barrier()
                    pw.close()
                    cur2[0] = p2
                    WM = [sb2("WM%d" % s_, [128, 5120], BF16) for s_ in range(4)]
                    t_WM = [TT(), TT(), TT(), TT()]
                    d_wm = d_w + d_w2
                    SG = [(sb2("sg%d" % k, [128, 512], F32), TT()) for k in range(3)]
                    M1 = sb2("m1", [128, 512], F32); t_m1 = TT()
                    M2 = sb2("m2", [128, 512], F32); t_m2 = TT()
                    M3 = sb2("m3", [128, 512], F32); t_m3 = TT()
                    MG = [(sb2("mg%d" % k, [128, 512], BF16), TT()) for k in range(4)]
                    bgv = bg_s[:].rearrange("p (l r c) -> p l r c", l=DEPTH, r=3)
                    win_v = w_in[l].rearrange("(k p) n -> p k n", p=128)

                    def load_c(c):
                        s_ = c % 4
                        wg = WM[s_][:, 0:3072].rearrange("p (k r n) -> p k r n", k=NCH, r=3)
                        wb = WM[s_][:, 3072:4096].rearrange("p (k n) -> p k n", k=8)
                        wo = WM[s_][:, 4096:5120]
                        for r in range(3):
                            S.op("pool", (lambda r=r: nc.gpsimd.dma_start(out=wg[:, :, r, :], in_=win_v[:, :, OFF_GATE + r * D + c * 128:OFF_GATE + r * D + (c + 1) * 128])),
                                 writes=[t_WM[s_]], dma=d_wm[s_])
                        S.op("pool", lambda: nc.gpsimd.dma_start(out=wb[:, 0:2, :], in_=w_bc[l].rearrange("(k p) n -> p k n", p=128)[:, :, c * 128:(c + 1) * 128]),
                             writes=[t_WM[s_]], dma=d_wm[s_])
                        S.op("pool", lambda: nc.gpsimd.dma_start(out=wb[:, 2:4, :], in_=w_bg[l].rearrange("(k p) n -> p k n", p=128)[:, :, c * 128:(c + 1) * 128]),
                             writes=[t_WM[s_]], dma=d_wm[s_])
                        S.op("pool", lambda: nc.gpsimd.dma_start(out=wb[:, 4:8, :], in_=w_ba[l].rearrange("(k p) n -> p k n", p=128)[:, :, c * 128:(c + 1) * 128]),
                             writes=[t_WM[s_]], dma=d_wm[s_])
                        S.op("pool", lambda: nc.gpsimd.dma_start(out=wo, in_=w_o[l][c * 128:(c + 1) * 128, :]), writes=[t_WM[s_]], dma=d_wm[s_])
                        return wg, wb, wo

                    wv_ = {0: load_c(0), 1: load_c(1)}

                    def emit_gate(c, ti, par):
                        s_ = c % 4
                        wg, wb, wo = wv_[c]
                        t0, w, isctx = TILES[ti]
                        blks = range(t0 // 128, (t0 + w) // 128)
                        for r in range(3):
                            for k in range(NCH):
                                S.op("pe", (lambda r=r, k=k: nc.tensor.matmul(ps[r % 2][:, 0:w], lhsT=wg[:, k, r, :], rhs=zv[:, k, t0:t0 + w], start=(k == 0), stop=(k == NCH - 1))),
                                     reads=[t_WM[s_], zT[k][ti]], writes=[pst[r % 2]])
                            sg, tsg = SG[r]
                            S.op("act", (lambda r=r, sg=sg: nc.scalar.activation(out=sg[:, 0:w], in_=ps[r % 2][:, 0:w], func=AF.Sigmoid, bias=bgv[:, l, r, c:c + 1], scale=1.0)),
                                 reads=[pst[r % 2], t_const], writes=[tsg])

                    def emit_branch(c, ti, par):
                        s_ = c % 4
                        wg, wb, wo = wv_[c]
                        t0, w, isctx = TILES[ti]
                        blks = range(t0 // 128, (t0 + w) // 128)
                        kr = [(0, 2), (2, 4), (4, 8)]
                        MM = [(M1, t_m1), (M2, t_m2), (M3, t_m3)]
                        for r in range(3):
                            k0, k1 = kr[r]
                            pb_ = 2 + r % 2
                            for kk in range(k0, k1):
                                S.op("pe", (lambda kk=kk, k0=k0, k1=k1, pb_=pb_: nc.tensor.matmul(ps[pb_][:, 0:w], lhsT=wb[:, kk, :], rhs=yv[:, kk, t0:t0 + w],
                                                                                             start=(kk == k0), stop=(kk == k1 - 1))),
                                     reads=[t_WM[s_]] + [yT[kk][tb_] for tb_ in blks], writes=[pst[pb_]])
                            mm_, tmm_ = MM[r]
                            S.op("dve", (lambda r=r, pb_=pb_, mm_=mm_: nc.vector.tensor_tensor(out=mm_[:, 0:w], in0=SG[r][0][:, 0:w], in1=ps[pb_][:, 0:w], op=ALU.mult)),
                                 reads=[SG[r][1], pst[pb_]], writes=[tmm_])
                        mg, tmg = MG[par * 2 + c % 2]
                        S.op("pool", lambda: nc.gpsimd.tensor_tensor(out=M1[:, 0:w], in0=M1[:, 0:w], in1=M2[:, 0:w], op=ALU.add), reads=[t_m1, t_m2], writes=[t_m1])
                        S.op("pool", lambda: nc.gpsimd.tensor_tensor(out=mg[:, 0:w], in0=M1[:, 0:w], in1=M3[:, 0:w], op=ALU.add), reads=[t_m1, t_m3], writes=[tmg])

                    def emit_out(c0, ti, par):
                        t0, w, isctx = TILES[ti]
                        j = jf(ti)
                        for c2 in range(NCH):
                            po = 4 + c2 % 4
                            for cc in range(2):
                                c = c0 + cc
                                s_ = c % 4
                                wo = wv_[c][2]
                                mg, tmg = MG[par * 2 + cc]
                                S.op("pe", (lambda c2=c2, po=po, wo=wo, mg=mg, cc=cc: nc.tensor.matmul(ps[po][:, 0:w], lhsT=wo[:, c2 * 128:(c2 + 1) * 128], rhs=mg[:, 0:w],
                                                                                               start=(cc == 0), stop=(cc == 1))),
                                     reads=[t_WM[s_], tmg], writes=[pst[po]])
                            S.op("dve", (lambda c2=c2, po=po: nc.vector.scalar_tensor_tensor(
                                out=Hv[:, c2, t0:t0 + w], in0=ps[po][:, 0:w], scalar=mcol(l, 5, c2, j), in1=Hv[:, c2, t0:t0 + w], op0=ALU.mult, op1=ALU.add)),
                                reads=[pst[po], hT[c2][ti], t_mod], writes=[hT[c2][ti]])

                    steps = [(c0, ti) for c0 in range(0, NCH, 2) for ti in tiles2]
                    prev = None
                    for n, (c0, ti) in enumerate(steps):
                        emit_gate(c0, ti, n % 2)
                        if prev is not None:
                            emit_out(*prev)
                        emit_branch(c0, ti, n % 2)
                        if ti == tiles2[0] and c0 + 2 < NCH:
                            wv_[c0 + 2] = load_c(c0 + 2)
                            wv_[c0 + 3] = load_c(c0 + 3)
                        emit_gate(c0 + 1, ti, n % 2)
                        emit_branch(c0 + 1, ti, n % 2)
                        prev = (c0, ti, n % 2)
                    emit_out(*prev)
                    S.barrier()

        for b in range(nb):
            for c in range(NCH):
                S.op("sp", (lambda c=c, b=b: nc.sync.dma_start(out=Hv[:, c, 0:SEQ], in_=xT[b, c * 128:(c + 1) * 128, :])),
                     writes=[hT[c][ti] for ti in range(4)], dma=d_x)
                S.op("sp", (lambda c=c, b=b: nc.sync.dma_start(out=Hv[:, c, SEQ:T], in_=ctxT[b, c * 128:(c + 1) * 128, :])),
                     writes=[hT[c][4]], dma=d_x)
            for l in layers:
                last = (l == DEPTH - 1)
                if "ffn1" in subs:
                    emit_ffn(l, 0, b, [4, 0, 1, 2, 3])
                if "mix" in subs:
                    emit_mixer(l, b, last)
                if "ffn2" in subs:
                    emit_ffn(l, 1, b, [0, 1, 2, 3] if last else [4, 0, 1, 2, 3])
            for c in range(NCH):
                S.op("sp", (lambda c=c, b=b: nc.sync.dma_start(out=yT[b, c * 128:(c + 1) * 128, :], in_=Hv[:, c, 0:SEQ])),
                     reads=[hT[c][ti] for ti in range(4)], dma=d_out)
            S.barrier()
        S.barrier()
        print("instructions:", S.n_instr)
    return nc


def prep_inputs(inp, nb=NB, ncores=NCORES):
    f = np.float32
    x = np.asarray(inp["x"], f)
    ctx = np.asarray(inp["ctx"], f)
    c = np.asarray(inp["c"], f)
    c_ctx = np.asarray(inp["c_ctx"], f)
    shared = {
        "bmodT": np.ascontiguousarray(np.asarray(inp["b_mod"], f).reshape(DEPTH, 72, 128).transpose(2, 0, 1).reshape(128, DEPTH * 72)),
        "ngT": np.ascontiguousarray(np.asarray(inp["norm_g"], f).reshape(DEPTH, 3, NCH, 128).transpose(3, 0, 1, 2).reshape(128, -1)),
        "w_mod": np.ascontiguousarray(np.asarray(inp["w_mod"], f)),
        "ffn_w_in": np.ascontiguousarray(np.asarray(inp["ffn_w_in"], f)),
        "ffn_w_out": np.ascontiguousarray(np.asarray(inp["ffn_w_out"], f)),
    }
    L = DEPTH
    shared["w_in"] = np.ascontiguousarray(np.asarray(inp["w_in"], f))
    shared["w_bc"] = np.ascontiguousarray(np.asarray(inp["w_branch_conv"], f))
    shared["w_bg"] = np.ascontiguousarray(np.asarray(inp["w_branch_gmlp"], f))
    shared["w_ba"] = np.ascontiguousarray(np.asarray(inp["w_branch_attn"], f))
    shared["w_o"] = np.ascontiguousarray(np.asarray(inp["w_out"], f))
    shared["bgT"] = np.ascontiguousarray(np.asarray(inp["b_gate"], f).reshape(L, 3, NCH, 128).transpose(3, 0, 1, 2).reshape(128, -1))
    shared["cwT"] = np.ascontiguousarray(np.asarray(inp["conv_w"], f).reshape(L, 3, 2, 128).transpose(3, 0, 2, 1).reshape(128, -1))
    lng = np.asarray(inp["gmlp_ln_g"], f)
    lnb = np.asarray(inp["gmlp_ln_b"], f)
    shared["lnB"] = np.ascontiguousarray(np.broadcast_to(np.concatenate([lng, lnb], axis=1)[:, None, :], (L, 128, 512)))
    ws = np.asarray(inp["gmlp_ws"], f)
    shared["wsT"] = np.ascontiguousarray(ws.transpose(0, 3, 1, 2).reshape(L, 128, 512))
    bs = np.asarray(inp["gmlp_bs"], f)
    shared["bsB"] = np.ascontiguousarray(np.repeat(bs.reshape(L, 2, 2, 1, 128), 64, axis=3).transpose(0, 2, 3, 1, 4).reshape(L, 128, 256))
    qg = np.asarray(inp["q_norm_g"], f)
    kg = np.asarray(inp["k_norm_g"], f)
    qkg = np.concatenate([np.tile(qg, (1, 8)), np.tile(kg, (1, 2))], axis=1)
    shared["qkg"] = np.ascontiguousarray(np.broadcast_to(qkg[:, None, :], (L, 128, 640)))
    shared["sinkB"] = np.ascontiguousarray(np.broadcast_to(np.asarray(inp["attn_sink"], f)[:, None, :], (L, 128, 8)))
    shared.update(_const_tables())
    maps = []
    for core in range(ncores):
        bs = slice(core * nb, (core + 1) * nb)
        cc = np.concatenate([c[bs], c_ctx[None, :]], axis=0)
        if nb < 4:
            cc = np.concatenate([c[bs], np.zeros((4 - nb, D), f), c_ctx[None, :]], axis=0)
        cT = np.ascontiguousarray(cc.reshape(5, NCH, 128).transpose(2, 1, 0).reshape(128, NCH * 5))
        m = dict(shared)
        m["xT"] = np.ascontiguousarray(x[bs].transpose(0, 2, 1))
        m["ctxT"] = np.ascontiguousarray(ctx[bs].transpose(0, 2, 1))
        m["cT"] = cT
        maps.append(m)
    return maps


def _const_tables():
    f = np.float32
    pos = np.arange(SEQ)
    r = (pos // 64).astype(f)
    col = (pos % 64).astype(f)
    half = 32
    inv = (np.float32(10000.0) ** (-np.arange(0, half, 2, dtype=f) / half)).astype(f)
    ang_r = r[:, None] * inv[None, :]
    ang_c = col[:, None] * inv[None, :]
    ang = np.concatenate([ang_r, ang_r, ang_c, ang_c], axis=-1).astype(f)
    cos = np.cos(ang).astype(f)
    sin = np.sin(ang).astype(f)
    sgn = np.tile(np.concatenate([-np.ones(16, f), np.ones(16, f)]), 2)
    sin_s = sin * sgn[None, :]
    ropeC = cos.reshape(16, 128, 64).transpose(1, 0, 2).reshape(128, 16 * 64)
    ropeS = sin_s.reshape(16, 128, 64).transpose(1, 0, 2).reshape(128, 16 * 64)
    j = np.arange(128)[:, None]
    i = np.arange(128)[None, :]
    return {
        "ropeC": np.ascontiguousarray(ropeC), "ropeS": np.ascontiguousarray(ropeS),
        "maskP": (j >= i).astype(f), "maskN": (j <= i).astype(f), "ident": np.eye(128, dtype=f),
    }


_NC_CACHE = {}


def kernel(**inputs):
    cfg = {"nb": NB}
    key = "full"
    if key not in _NC_CACHE:
        _NC_CACHE[key] = build_nc(cfg)
    nc = _NC_CACHE[key]
    maps = prep_inputs(inputs)
    res = run_bass_kernel_spmd(nc, maps, core_ids=list(range(NCORES)))
    outs = [np.asarray(r["yT"]).transpose(0, 2, 1) for r in res.results]
    return np.ascontiguousarray(np.concatenate(outs, axis=0).astype(np.float32))
```

```python
import numpy as np
from contextlib import ExitStack
import concourse.bass as bass
import concourse.mybir as mybir
from concourse.bass_utils import run_bass_kernel_spmd

F32 = mybir.dt.float32
BF16 = mybir.dt.bfloat16
AF = mybir.ActivationFunctionType
ALU = mybir.AluOpType

D = 1024
NCH = 8
SEQ = 2048
CTX = 256
T = SEQ + CTX
DEPTH = 4
DFF = 2816
NFF = 22
NMOD = 9
PROJ_W = 5120
EPS = 1e-6
NCORES = 8
NB = 4
OFF_CB, OFF_CC, OFF_CH, OFF_GU, OFF_GV, OFF_Q, OFF_K, OFF_V, OFF_GATE = 0, 256, 512, 768, 1024, 1280, 1792, 1920, 2048

TILES = [(0, 512, False), (512, 512, False), (1024, 512, False), (1536, 512, False), (2048, 256, True)]


class TT:
    __slots__ = ("w", "r", "psum")

    def __init__(self, psum=False):
        self.w = None
        self.r = []
        self.psum = psum


class DmaSem:
    def __init__(self, sem):
        self.sem = sem
        self.issued = 0


class Op:
    __slots__ = ("eng", "fn", "waits", "signal", "seq", "dma")

    def __init__(self, eng, fn, seq, dma=None):
        self.eng = eng
        self.fn = fn
        self.waits = []
        self.signal = False
        self.seq = seq
        self.dma = dma


class Sched:
    CENG = ("pe", "act", "dve", "pool")

    def __init__(self, nc, stack):
        self.nc = nc
        self.E = {"pe": nc.tensor, "act": nc.scalar, "dve": nc.vector, "pool": nc.gpsimd, "sp": nc.sync}
        self.sem = {e: stack.enter_context(nc.semaphore("sem_" + e)) for e in self.CENG}
        self.count = {e: 0 for e in self.CENG}
        self.nops = {e: 0 for e in self.CENG}
        self.ops = {e: {} for e in self.CENG}
        self.sigval = {e: {} for e in self.CENG}
        self.pending = []
        self.waited = {e: {s: -1 for s in self.CENG} for e in self.E}
        self.waited_dma = {e: {} for e in self.E}
        self.dsems = []
        self.stack = stack
        self.n_instr = 0

    def dma_sem(self, name):
        d = DmaSem(self.stack.enter_context(self.nc.semaphore(name)))
        self.dsems.append(d)
        return d

    def _add_dep(self, op, ev):
        e = op.eng
        if ev[0] == "c":
            _, src, s = ev
            if src == e and e == "pe":
                return
            if s <= self.waited[e][src]:
                return
            self.waited[e][src] = s
            op.waits.append(ev)
            self.ops[src][s].signal = True
        else:
            d = ev[1]
            val = d.issued
            if self.waited_dma[e].get(d, 0) >= val:
                return
            self.waited_dma[e][d] = val
            op.waits.append(("d", d, val))

    def op(self, eng, fn, reads=(), writes=(), dma=None):
        if dma is None:
            seq = self.nops[eng]
            self.nops[eng] += 1
            op = Op(eng, fn, seq)
            self.ops[eng][seq] = op
            ev = ("c", eng, seq)
        else:
            op = Op(eng, fn, -1, dma)
            ev = ("d", dma)
        deps = []
        for t in reads:
            if t.w is not None:
                deps.append(t.w)
            if t.psum:
                deps.extend(r for r in t.r if r[0] == "c" and r[1] != eng)
        for t in writes:
            if t.w is not None and not (dma is not None and t.w[0] == "d" and t.w[1] is dma):
                deps.append(t.w)
            deps.extend(t.r)
        for dv in deps:
            self._add_dep(op, dv)
        if dma is not None:
            dma.issued += 16
        for t in reads:
            t.r.append(ev)
        for t in writes:
            t.w = ev
            t.r = []
        self.pending.append(op)
        return op

    def flush(self):
        for op in self.pending:
            eng = self.E[op.eng]
            for w in op.waits:
                if w[0] == "c":
                    eng.wait_ge(self.sem[w[1]], self.sigval[w[1]][w[2]])
                else:
                    eng.wait_ge(w[1].sem, w[2])
            ins = op.fn()
            self.n_instr += 1
            if op.dma is not None:
                ins.then_inc(op.dma.sem, 16)
            elif op.signal:
                self.count[op.eng] += 1
                self.sigval[op.eng][op.seq] = self.count[op.eng]
                ins.then_inc(self.sem[op.eng], 1)
        self.pending = []
        for e in self.CENG:
            self.ops[e] = {}

    def barrier(self):
        for e in self.CENG:
            if self.nops[e] > 0:
                last = self.nops[e] - 1
                if last in self.ops[e]:
                    self.ops[e][last].signal = True
        self.flush()
        for e, eng in self.E.items():
            for s in self.CENG:
                if self.count[s] > 0 and not (s == e and e == "pe"):
                    eng.wait_ge(self.sem[s], self.count[s])
                if self.nops[s] > 0:
                    self.waited[e][s] = self.nops[s] - 1
            for d in self.dsems:
                if d.issued > self.waited_dma[e].get(d, 0):
                    eng.wait_ge(d.sem, d.issued)
                    self.waited_dma[e][d] = d.issued


def ffn_groups():
    return [(0, 4), (4, 4), (8, 4), (12, 4), (16, 4), (20, 2)]


def build_nc(cfg):
    nb = cfg.get("nb", NB)
    layers = cfg.get("layers", list(range(DEPTH)))
    subs = cfg.get("subs", ("ffn1", "mix", "ffn2"))
    nc = bass.Bass("TRN2", target_bir_lowering=False)

    def din(name, shape, dt=F32):
        return nc.dram_tensor(name, list(shape), dt, kind="ExternalInput").ap()

    xT = din("xT", [nb, D, SEQ])
    ctxT = din("ctxT", [nb, D, CTX])
    cT = din("cT", [128, NCH * 5])
    bmodT = din("bmodT", [128, DEPTH * 72])
    ngT = din("ngT", [128, DEPTH * 3 * NCH])
    w_mod = din("w_mod", [DEPTH, D, NMOD * D])
    ffn_w_in = din("ffn_w_in", [DEPTH, 2, D, 2 * DFF])
    ffn_w_out = din("ffn_w_out", [DEPTH, 2, DFF, D])
    w_in = din("w_in", [DEPTH, D, PROJ_W])
    w_bc = din("w_bc", [DEPTH, 256, D])
    w_bg = din("w_bg", [DEPTH, 256, D])
    w_ba = din("w_ba", [DEPTH, 512, D])
    w_o = din("w_o", [DEPTH, D, D])
    bgT = din("bgT", [128, DEPTH * 3 * NCH])
    cwT = din("cwT", [128, DEPTH * 2 * 3])
    lnB_d = din("lnB", [DEPTH, 128, 512])
    wsT_d = din("wsT", [DEPTH, 128, 512])
    bsB_d = din("bsB", [DEPTH, 128, 256])
    qkg_d = din("qkg", [DEPTH, 128, 640])
    sinkB_d = din("sinkB", [DEPTH, 128, 8])
    ropeC_d = din("ropeC", [128, 16 * 64])
    ropeS_d = din("ropeS", [128, 16 * 64])
    maskP_d = din("maskP", [128, 512])
    maskN_d = din("maskN", [128, 512])
    ident_d = din("ident", [128, 128])
    yT = nc.dram_tensor("yT", [nb, D, SEQ], F32, kind="ExternalOutput").ap()
    dumped = []
    if cfg.get("dump_y"):
        dbgY = nc.dram_tensor("dbgY", [128, NCH * T], F32, kind="ExternalOutput").ap()
        dbgZ = nc.dram_tensor("dbgZ", [128, NCH * T], F32, kind="ExternalOutput").ap()

    _uid = [0]

    def uname(name):
        _uid[0] += 1
        return "s%d_%s" % (_uid[0], name)

    with ExitStack() as st:
        S = Sched(nc, st)

        def sb(name, shape, dt):
            return st.enter_context(nc.sbuf_tensor(uname(name), list(shape), dt))

        H = sb("H", [128, NCH * T], F32)
        Hv = H[:].rearrange("p (c t) -> p c t", c=NCH)
        modT = sb("modT", [128, DEPTH * 72 * 5], F32)
        modv = modT[:].rearrange("p (l m c j) -> p l m c j", l=DEPTH, m=NMOD, c=NCH)
        ps = [st.enter_context(nc.psum_tensor("ps%d" % i, [128, 512], F32)) for i in range(8)]
        pst = [TT(psum=True) for _ in range(8)]
        hT = [[TT() for _ in TILES] for _ in range(NCH)]
        t_mod = TT()
        t_const = TT()
        d_w = [S.dma_sem("dw%d" % i) for i in range(2)]
        d_misc = S.dma_sem("dmisc")
        d_x = S.dma_sem("dx")
        d_out = S.dma_sem("dout")

        d_misc2 = S.dma_sem("dmisc2")
        d_w2 = [S.dma_sem("dw%d" % i) for i in range(2, 4)]
        ones_b = sb("ones_b", [128, 128], BF16)
        S.op("dve", lambda: nc.vector.memset(ones_b[:], 1.0), writes=[t_const])
        ident = sb("ident", [128, 128], BF16)
        bg_s = sb("bg_s", [128, DEPTH * 3 * NCH], F32)
        cw_s = sb("cw_s", [128, DEPTH * 2 * 3], F32)
        S.op("pool", lambda: nc.gpsimd.dma_start(out=ident[:], in_=ident_d), writes=[t_const], dma=d_misc2)
        S.op("sp", lambda: nc.sync.dma_start(out=bg_s[:], in_=bgT), writes=[t_const], dma=d_misc)
        S.op("sp", lambda: nc.sync.dma_start(out=cw_s[:], in_=cwT), writes=[t_const], dma=d_misc)

        with ExitStack() as ps_st:
            def psb(name, shape, dt):
                return ps_st.enter_context(nc.sbuf_tensor(uname(name), list(shape), dt))
            cT_s = psb("cT_s", [128, NCH * 5], F32)
            scT = psb("scT", [128, NCH * 5], BF16)
            bm_s = psb("bm_s", [128, DEPTH * 72], F32)
            ng_s = psb("ng_s", [128, DEPTH * 3 * NCH], F32)
            wm = [psb("wm%d" % i, [128, NCH * 1024], BF16) for i in range(2)]
            t_c, t_sc, t_bm, t_ng = TT(), TT(), TT(), TT()
            t_wm = [TT(), TT()]
            S.op("sp", lambda: nc.sync.dma_start(out=cT_s[:], in_=cT), writes=[t_c], dma=d_misc)
            S.op("sp", lambda: nc.sync.dma_start(out=bm_s[:], in_=bmodT), writes=[t_bm], dma=d_misc)
            S.op("sp", lambda: nc.sync.dma_start(out=ng_s[:], in_=ngT), writes=[t_ng], dma=d_misc)
            S.op("act", lambda: nc.scalar.activation(out=scT[:], in_=cT_s[:], func=AF.Silu), reads=[t_c], writes=[t_sc])
            scv = scT[:].rearrange("p (k j) -> p k j", k=NCH)
            blk = 0
            for l in range(DEPTH):
                wsrc = w_mod[l].rearrange("(k p) n -> p k n", p=128)
                for cb in range(9):
                    s = blk % 2
                    blk += 1
                    wv = wm[s][:].rearrange("p (k n) -> p k n", k=NCH)
                    S.op("pool", (lambda wv=wv, wsrc=wsrc, cb=cb: nc.gpsimd.dma_start(out=wv, in_=wsrc[:, :, cb * 1024:(cb + 1) * 1024])),
                         writes=[t_wm[s]], dma=d_w[s])
                    pt = ps[cb % 2]
                    for oc in range(8):
                        for k in range(NCH):
                            S.op("pe", (lambda pt=pt, wv=wv, oc=oc, k=k: nc.tensor.matmul(
                                pt[:, oc * 5:(oc + 1) * 5], lhsT=wv[:, k, oc * 128:(oc + 1) * 128], rhs=scv[:, k, :],
                                start=(k == 0), stop=(k == NCH - 1))),
                                reads=[t_wm[s], t_sc], writes=[pst[cb % 2]])
                    bsl = bm_s[:, l * 72 + cb * 8: l * 72 + cb * 8 + 8]
                    S.op("dve", (lambda pt=pt, l=l, cb=cb, bsl=bsl: nc.vector.tensor_tensor(
                        out=modv[:, l, cb, :, :], in0=pt[:, 0:40].rearrange("p (c j) -> p c j", c=NCH),
                        in1=bsl.unsqueeze(2).broadcast_to([128, NCH, 5]), op=ALU.add)),
                        reads=[pst[cb % 2], t_bm], writes=[t_mod])
                for i in range(3):
                    ngs = ng_s[:, (l * 3 + i) * NCH:(l * 3 + i + 1) * NCH]
                    S.op("dve", (lambda l=l, i=i, ngs=ngs: nc.vector.scalar_tensor_tensor(
                        out=modv[:, l, 3 * i + 1, :, :], in0=modv[:, l, 3 * i + 1, :, :], scalar=1.0,
                        in1=ngs.unsqueeze(2).broadcast_to([128, NCH, 5]), op0=ALU.add, op1=ALU.mult)),
                        reads=[t_ng, t_mod], writes=[t_mod])
                for i in (0, 2):
                    S.op("dve", (lambda l=l, i=i: nc.vector.tensor_scalar(
                        out=modv[:, l, 3 * i + 2, :, :], in0=modv[:, l, 3 * i + 2, :, :], scalar1=0.5, scalar2=None,
                        op0=ALU.mult)), reads=[t_mod], writes=[t_mod])
            S.barrier()

        def mcol(l, m, c, j):
            return modv[:, l, m, c, j:j + 1]

        def emit_norm(l, i, j_of_tile, tiles, zv, zT, tmp, zdst=None, pbf=None):
            if zdst is None:
                zdst = lambda c, ti: (zv[:, c, TILES[ti][0]:TILES[ti][0] + TILES[ti][1]], zT[c][ti])
            for ti in tiles:
                t0, w, isctx = TILES[ti]
                j = j_of_tile(ti)
                pb = (4 + (ti % 2)) if pbf is None else pbf(ti)
                for c in range(NCH):
                    sq, tsq = tmp["sq"][c % 2]
                    S.op("act", (lambda sq=sq, c=c, t0=t0, w=w: nc.scalar.activation(
                        out=sq[:, 0:w], in_=Hv[:, c, t0:t0 + w], func=AF.Square)), reads=[hT[c][ti]], writes=[tsq])
                    S.op("pe", (lambda sq=sq, c=c, w=w, pb=pb: nc.tensor.matmul(
                        ps[pb][:, 0:w], lhsT=ones_b[:], rhs=sq[:, 0:w], start=(c == 0), stop=(c == NCH - 1))),
                        reads=[tsq, t_const], writes=[pst[pb]])
                sd, tsd = tmp["sd"]
                rs, trs = tmp["rstd"][ti % 2]
                S.op("act", (lambda sd=sd, w=w, pb=pb: nc.scalar.activation(
                    out=sd[:, 0:w], in_=ps[pb][:, 0:w], func=AF.Ln, bias=tmp["eps"][:, 0:1], scale=1.0 / D)),
                    reads=[pst[pb], t_const], writes=[tsd])
                S.op("act", (lambda sd=sd, rs=rs, w=w: nc.scalar.activation(out=rs[:, 0:w], in_=sd[:, 0:w], func=AF.Exp, scale=-0.5)),
                     reads=[tsd], writes=[trs])
                for c in range(NCH):
                    tb, ttb = tmp["t"][c % 2]
                    S.op("dve", (lambda tb=tb, rs=rs, c=c, t0=t0, w=w: nc.vector.tensor_tensor(
                        out=tb[:, 0:w], in0=Hv[:, c, t0:t0 + w], in1=rs[:, 0:w], op=ALU.mult)),
                        reads=[hT[c][ti], trs], writes=[ttb])
                    zap, ztt = zdst(c, ti)
                    S.op("act", (lambda tb=tb, c=c, w=w, j=j, zap=zap: nc.scalar.activation(
                        out=zap, in_=tb[:, 0:w], func=AF.Identity,
                        bias=mcol(l, 3 * i, c, j), scale=mcol(l, 3 * i + 1, c, j))),
                        reads=[ttb, t_mod], writes=[ztt])

        def emit_ffn(l, fi, b, tiles):
            i = 0 if fi == 0 else 2
            with ExitStack() as fs:
                def fsb(name, shape, dt):
                    return fs.enter_context(nc.sbuf_tensor(uname(name), list(shape), dt))
                Z = fsb("Z", [128, NCH * T], BF16)
                zv = Z[:].rearrange("p (c t) -> p c t", c=NCH)
                zT = [[TT() for _ in TILES] for _ in range(NCH)]
                W = [fsb("W%d" % s, [128, 3 * 4096], BF16) for s in range(2)]
                tW = [TT(), TT()]
                G = [[(fsb("g%d_%d" % (p, j), [128, 512], BF16), TT()) for j in range(4)] for p in range(2)]
                SA = [(fsb("sa%d" % k, [128, 512], F32), TT()) for k in range(2)]
                epsb = fsb("epsb", [128, 1], F32)
                tmp = {
                    "sq": [(fsb("sq%d" % k, [128, 512], BF16), TT()) for k in range(2)],
                    "t": [(fsb("tt%d" % k, [128, 512], F32), TT()) for k in range(2)],
                    "sd": (fsb("sd", [128, 512], F32), TT()),
                    "rstd": [(fsb("rstd%d" % k, [128, 512], F32), TT()) for k in range(2)],
                    "eps": epsb,
                }
                S.op("dve", lambda: nc.vector.memset(epsb[:], EPS), writes=[t_const])
                jf = (lambda ti: 4 if TILES[ti][2] else b)
                win = ffn_w_in[l, fi].rearrange("(k p) n -> p k n", p=128)
                wout = ffn_w_out[l, fi].rearrange("(f p) n -> p f n", p=128)
                groups = ffn_groups()

                def load_group(gi):
                    g0, G_ = groups[gi]
                    s = gi % 2
                    n = G_ * 128
                    wa = W[s][:, 0:NCH * n].rearrange("p (k n) -> p k n", k=NCH)
                    wu = W[s][:, 4096:4096 + NCH * n].rearrange("p (k n) -> p k n", k=NCH)
                    wo = W[s][:, 8192:8192 + G_ * 1024].rearrange("p (f n) -> p f n", f=G_)
                    S.op("pool", (lambda: nc.gpsimd.dma_start(out=wa, in_=win[:, :, g0 * 128:g0 * 128 + n])), writes=[tW[s]], dma=d_w[s])
                    S.op("pool", (lambda: nc.gpsimd.dma_start(out=wu, in_=win[:, :, DFF + g0 * 128:DFF + g0 * 128 + n])), writes=[tW[s]], dma=d_w[s])
                    S.op("pool", (lambda: nc.gpsimd.dma_start(out=wo, in_=wout[:, g0:g0 + G_, :])), writes=[tW[s]], dma=d_w[s])
                    return wa, wu, wo

                wviews = {0: load_group(0)}
                emit_norm(l, i, jf, tiles[0:1], zv, zT, tmp)

                def emit_au(gi, ti, par):
                    g0, G_ = groups[gi]
                    s = gi % 2
                    wa, wu, wo = wviews[gi]
                    t0, w, isctx = TILES[ti]
                    for jj in range(G_):
                        pa, pu = jj % 2, 2 + jj % 2
                        for k in range(NCH):
                            S.op("pe", (lambda k=k, jj=jj, pa=pa: nc.tensor.matmul(
                                ps[pa][:, 0:w], lhsT=wa[:, k, jj * 128:(jj + 1) * 128], rhs=zv[:, k, t0:t0 + w],
                                start=(k == 0), stop=(k == NCH - 1))), reads=[tW[s], zT[k][ti]], writes=[pst[pa]])
                        for k in range(NCH):
                            S.op("pe", (lambda k=k, jj=jj, pu=pu: nc.tensor.matmul(
                                ps[pu][:, 0:w], lhsT=wu[:, k, jj * 128:(jj + 1) * 128], rhs=zv[:, k, t0:t0 + w],
                                start=(k == 0), stop=(k == NCH - 1))), reads=[tW[s], zT[k][ti]], writes=[pst[pu]])
                        sa, tsa = SA[jj % 2]
                        gb, tgb = G[par][jj]
                        S.op("act", (lambda sa=sa, pa=pa: nc.scalar.activation(out=sa[:, 0:w], in_=ps[pa][:, 0:w], func=AF.Silu)),
                             reads=[pst[pa]], writes=[tsa])
                        S.op("dve", (lambda sa=sa, gb=gb, pu=pu: nc.vector.tensor_tensor(
                            out=gb[:, 0:w], in0=sa[:, 0:w], in1=ps[pu][:, 0:w], op=ALU.mult)),
                            reads=[tsa, pst[pu]], writes=[tgb])

                def emit_o(gi, ti, par):
                    g0, G_ = groups[gi]
                    s = gi % 2
                    wa, wu, wo = wviews[gi]
                    t0, w, isctx = TILES[ti]
                    j = jf(ti)
                    for c in range(NCH):
                        po = 4 + c % 4
                        for jj in range(G_):
                            gb, tgb = G[par][jj]
                            S.op("pe", (lambda c=c, jj=jj, gb=gb, po=po: nc.tensor.matmul(
                                ps[po][:, 0:w], lhsT=wo[:, jj, c * 128:(c + 1) * 128], rhs=gb[:, 0:w],
                                start=(jj == 0), stop=(jj == G_ - 1))), reads=[tW[s], tgb], writes=[pst[po]])
                        S.op("dve", (lambda c=c, po=po: nc.vector.scalar_tensor_tensor(
                            out=Hv[:, c, t0:t0 + w], in0=ps[po][:, 0:w], scalar=mcol(l, 3 * i + 2, c, j),
                            in1=Hv[:, c, t0:t0 + w], op0=ALU.mult, op1=ALU.add)),
                            reads=[pst[po], hT[c][ti], t_mod], writes=[hT[c][ti]])

                steps = [(gi, ti) for gi in range(len(groups)) for ti in tiles]
                prev = None
                for n, (gi, ti) in enumerate(steps):
                    if gi == 0 and n + 1 < len(tiles):
                        emit_norm(l, i, jf, tiles[n + 1:n + 2], zv, zT, tmp)
                    emit_au(gi, ti, n % 2)
                    if prev is not None:
                        emit_o(prev[0], prev[1], prev[2])
                    if ti == tiles[0] and gi + 1 < len(groups):
                        wviews[gi + 1] = load_group(gi + 1)
                    prev = (gi, ti, n % 2)
                emit_o(prev[0], prev[1], prev[2])
                S.barrier()

        def emit_mixer(l, b, last):
            jf = (lambda ti: 4 if TILES[ti][2] else b)
            tiles1 = [4, 0, 1, 2, 3]
            tiles2 = [0, 1, 2, 3] if last else [4, 0, 1, 2, 3]
            with ExitStack() as ms:
                def msb(name, shape, dt):
                    return ms.enter_context(nc.sbuf_tensor(uname(name), list(shape), dt))
                Y = msb("Y", [128, NCH * T], BF16)
                yv = Y[:].rearrange("p (c t) -> p c t", c=NCH)
                NBLK = T // 128
                yT = [[TT() for _ in range(NBLK)] for _ in range(NCH)]
                epsb = msb("epsb_m", [128, 1], F32)
                S.op("dve", lambda: nc.vector.memset(epsb[:], EPS), writes=[t_const])
                t_lp = TT()

                def ycols(tb):
                    return slice(tb * 128, (tb + 1) * 128)

                with ExitStack() as p1:
                    def sb1(name, shape, dt):
                        return p1.enter_context(nc.sbuf_tensor(uname(name), list(shape), dt))
                    maskP = sb1("maskP", [128, 512], BF16)
                    maskN = sb1("maskN", [128, 512], BF16)
                    t_mask = TT()
                    S.op("pool", lambda: nc.gpsimd.dma_start(out=maskP[:], in_=maskP_d), writes=[t_mask], dma=d_misc2)
                    S.op("pool", lambda: nc.gpsimd.dma_start(out=maskN[:], in_=maskN_d), writes=[t_mask], dma=d_misc2)
                    qkg = sb1("qkg", [128, 640], F32)
                    esk = sb1("esk", [128, 8], F32)
                    S.op("sp", lambda: nc.sync.dma_start(out=qkg[:], in_=qkg_d[l]), writes=[t_lp], dma=d_misc)
                    S.op("sp", lambda: nc.sync.dma_start(out=esk[:], in_=sinkB_d[l]), writes=[t_lp], dma=d_misc)
                    S.op("act", lambda: nc.scalar.activation(out=esk[:], in_=esk[:], func=AF.Exp), reads=[t_lp], writes=[t_lp])
                    qkgv = qkg[:].rearrange("p (h d) -> p h d", h=10)
                    ropeC = sb1("ropeC", [128, 16 * 64], F32)
                    ropeS = sb1("ropeS", [128, 16 * 64], F32)
                    t_rope = TT()
                    S.op("sp", lambda: nc.sync.dma_start(out=ropeC[:], in_=ropeC_d), writes=[t_rope], dma=d_misc)
                    S.op("sp", lambda: nc.sync.dma_start(out=ropeS[:], in_=ropeS_d), writes=[t_rope], dma=d_misc)
                    kT = sb1("kT", [128, T], BF16)
                    Vt = sb1("Vt", [128, NBLK * 128], BF16)
                    Vv = Vt[:].rearrange("p (b n) -> p b n", b=NBLK)
                    kT_t = [TT() for _ in range(NBLK)]
                    V_t = [TT() for _ in range(NBLK)]
                    W2 = sb1("W2", [128, NCH * 768], BF16)
                    W2v = W2[:].rearrange("p (k n) -> p k n", k=NCH)
                    t_W2 = TT()
                    S.op("pool", lambda: nc.gpsimd.dma_start(out=W2v, in_=w_in[l].rearrange("(k p) n -> p k n", p=128)[:, :, OFF_Q:OFF_GATE]),
                         writes=[t_W2], dma=d_w[0])
                    ZT = [sb1("zt%d" % k, [128, NCH * 512], BF16) for k in range(2)]
                    ZTv = [z[:].rearrange("p (c t) -> p c t", c=NCH) for z in ZT]
                    ZT_t = [[TT() for _ in range(NCH)] for _ in range(2)]
                    tmp = {
                        "sq": [(sb1("sq%d" % k, [128, 512], BF16), TT()) for k in range(2)],
                        "t": [(sb1("tt%d" % k, [128, 512], F32), TT()) for k in range(2)],
                        "sd": (sb1("sd", [128, 512], F32), TT()),
                        "rstd": [(sb1("rstd0", [128, 512], F32), TT())] * 2,
                        "eps": epsb,
                    }
                    CH = []
                    for p_ in range(2):
                        CH.append({
                            "qk": (sb1("qk%d" % p_, [128, 640], F32), TT()),
                            "t2": (sb1("t2%d" % p_, [128, 640], F32), TT()),
                            "qrot": (sb1("qrot%d" % p_, [128, 640], BF16), TT()),
                            "ssq": (sb1("ssq%d" % p_, [128, 10], F32), TT()),
                            "sdq": (sb1("sdq%d" % p_, [128, 10], F32), TT()),
                            "rsq": (sb1("rsq%d" % p_, [128, 10], F32), TT()),
                        })
                    QT = [(sb1("qT%d" % k, [128, 512], BF16), TT()) for k in range(4)]
                    PT = [(sb1("PT%d" % k, [128, 512], BF16), TT()) for k in range(10)]
                    DS = [(sb1("dsum%d" % k, [128, 512], F32), TT()) for k in range(2)]
                    RD = [(None, None)]
                    ps2b = ps[2][:, :].bitcast(BF16)

                    CHE = cfg.get("chain_eng", "pool")
                    CHN = nc.vector if CHE == "dve" else nc.gpsimd

                    def proj_block(ti, sbk, tb, need_q, par):
                        zs = tiles1.index(ti) % 2
                        isctx = TILES[ti][2]
                        B_ = CH[par]
                        qk, t_qk = B_["qk"]; qn, t_qn = B_["qk"]; t2, t_t2 = B_["t2"]; qrot, t_qrot = B_["qrot"]
                        ssq, t_ssq = B_["ssq"]; sdq, t_sdq = B_["sdq"]; rsq, t_rsq = B_["rsq"]
                        if need_q:
                            for k in range(NCH):
                                S.op("pe", (lambda k=k: nc.tensor.matmul(ps[0][:, 0:512], lhsT=ZTv[zs][:, k, sbk * 128:(sbk + 1) * 128],
                                                                         rhs=W2v[:, k, 0:512], start=(k == 0), stop=(k == NCH - 1))),
                                     reads=[ZT_t[zs][k], t_W2], writes=[pst[0]])
                        for k in range(NCH):
                            S.op("pe", (lambda k=k: nc.tensor.matmul(ps[1][:, 0:256], lhsT=ZTv[zs][:, k, sbk * 128:(sbk + 1) * 128],
                                                                     rhs=W2v[:, k, 512:768], start=(k == 0), stop=(k == NCH - 1))),
                                 reads=[ZT_t[zs][k], t_W2], writes=[pst[1]])
                        if need_q:
                            S.op("act", lambda: nc.scalar.copy(out=qk[:, 0:512], in_=ps[0][:, 0:512]), reads=[pst[0]], writes=[t_qk])
                        S.op("act", lambda: nc.scalar.copy(out=qk[:, 512:640], in_=ps[1][:, 0:128]), reads=[pst[1]], writes=[t_qk])
                        S.op("act", lambda: nc.scalar.copy(out=Vv[:, tb, :], in_=ps[1][:, 128:256]), reads=[pst[1]], writes=[V_t[tb]])
                        lo = 0 if need_q else 512
                        h0 = lo // 64
                        nh = (640 - lo) // 64
                        S.op("dve", lambda: nc.vector.tensor_tensor(out=t2[:, lo:640], in0=qk[:, lo:640], in1=qk[:, lo:640], op=ALU.mult),
                             reads=[t_qk], writes=[t_t2])
                        S.op("dve", lambda: nc.vector.reduce_sum(out=ssq[:, h0:10], in_=t2[:, lo:640].rearrange("p (h d) -> p h d", d=64),
                                                                 axis=mybir.AxisListType.X), reads=[t_t2], writes=[t_ssq])
                        def part2():
                            _chain2(tb, isctx, need_q, lo, h0, nh, qk, t_qk, qn, t_qn, t2, t_t2, qrot, t_qrot, ssq, t_ssq, sdq, t_sdq, rsq, t_rsq)
                        return part2

                    def _chain2(tb, isctx, need_q, lo, h0, nh, qk, t_qk, qn, t_qn, t2, t_t2, qrot, t_qrot, ssq, t_ssq, sdq, t_sdq, rsq, t_rsq):
                        S.op("act", lambda: nc.scalar.activation(out=sdq[:, h0:10], in_=ssq[:, h0:10], func=AF.Ln, bias=epsb[:, 0:1], scale=1.0 / 64),
                             reads=[t_ssq, t_const], writes=[t_sdq])
                        S.op("act", lambda: nc.scalar.activation(out=rsq[:, h0:10], in_=sdq[:, h0:10], func=AF.Exp, scale=-0.5), reads=[t_sdq], writes=[t_rsq])
                        qkv3 = qk[:, lo:640].rearrange("p (h d) -> p h d", d=64)
                        qn3 = qn[:, lo:640].rearrange("p (h d) -> p h d", d=64)
                        S.op(CHE, lambda: CHN.tensor_tensor(out=qn3, in0=qkv3, in1=rsq[:, h0:10].unsqueeze(2).broadcast_to([128, nh, 64]), op=ALU.mult),
                             reads=[t_qk, t_rsq], writes=[t_qn])
                        S.op(CHE, lambda: CHN.tensor_tensor(out=qn3, in0=qn3, in1=qkgv[:, h0:10, :], op=ALU.mult),
                             reads=[t_qn, t_lp], writes=[t_qn])
                        qr_q = qrot[:, 0:512].rearrange("p (hh kv d) -> p kv hh d", hh=4, kv=2)
                        if isctx:
                            if need_q:
                                S.op(CHE, lambda: CHN.tensor_copy(out=qr_q, in_=qn[:, 0:512].rearrange("p (kv hh d) -> p kv hh d", kv=2, hh=4)),
                                     reads=[t_qn], writes=[t_qrot])
                            S.op(CHE, lambda: CHN.tensor_copy(out=qrot[:, 512:640], in_=qn[:, 512:640]), reads=[t_qn], writes=[t_qrot])
                        else:
                            cosb = ropeC[:, tb * 64:(tb + 1) * 64]
                            sinb = ropeS[:, tb * 64:(tb + 1) * 64]
                            qn4 = qn[:, lo:640].rearrange("p (h s a d) -> p h s a d", s=2, a=2, d=16)
                            t24 = t2[:, lo:640].rearrange("p (h s a d) -> p h s a d", s=2, a=2, d=16)
                            sin4 = sinb.rearrange("p (s a d) -> p s a d", s=2, a=2)
                            for a in range(2):
                                S.op(CHE, (lambda a=a: CHN.tensor_tensor(
                                    out=t24[:, :, :, a, :], in0=qn4[:, :, :, 1 - a, :],
                                    in1=sin4[:, :, a, :].unsqueeze(1).broadcast_to([128, nh, 2, 16]), op=ALU.mult)),
                                    reads=[t_qn, t_rope, t_ssq], writes=[t_t2])
                            S.op(CHE, lambda: CHN.tensor_tensor(out=qn3, in0=qn3, in1=cosb.unsqueeze(1).broadcast_to([128, nh, 64]), op=ALU.mult),
                                 reads=[t_qn, t_rope, t_t2], writes=[t_qn])
                            if need_q:
                                S.op(CHE, lambda: CHN.tensor_tensor(
                                    out=qr_q, in0=qn[:, 0:512].rearrange("p (kv hh d) -> p kv hh d", kv=2, hh=4),
                                    in1=t2[:, 0:512].rearrange("p (kv hh d) -> p kv hh d", kv=2, hh=4), op=ALU.add),
                                    reads=[t_qn, t_t2], writes=[t_qrot])
                            S.op(CHE, lambda: CHN.tensor_tensor(out=qrot[:, 512:640], in0=qn[:, 512:640], in1=t2[:, 512:640], op=ALU.add),
                                 reads=[t_qn, t_t2], writes=[t_qrot])

                    def trans_block(tb, need_q, par):
                        qrot, t_qrot = CH[par]["qrot"]
                        if need_q:
                            for hh in range(4):
                                S.op("pe", (lambda hh=hh: nc.tensor.transpose(out=ps2b[:, hh * 128:(hh + 1) * 128], in_=qrot[:, hh * 128:(hh + 1) * 128], identity=ident[:])),
                                     reads=[t_qrot, t_const], writes=[pst[2]])
                        S.op("pe", lambda: nc.tensor.transpose(out=ps2b[:, 512:640], in_=qrot[:, 512:640], identity=ident[:]),
                             reads=[t_qrot, t_const], writes=[pst[2]])
                        if need_q:
                            qt, tqt = QT[tb % 4]
                            S.op("act", lambda: nc.scalar.copy(out=qt[:], in_=ps2b[:, 0:512]), reads=[pst[2]], writes=[tqt])
                        S.op("act", lambda: nc.scalar.copy(out=kT[:, tb * 128:(tb + 1) * 128], in_=ps2b[:, 512:640]), reads=[pst[2]], writes=[kT_t[tb]])

                    def attn_block(tb, hook=None):
                        isctx = tb >= 16
                        if isctx:
                            keys = [(16, None), (17, None)]
                        else:
                            keys = []
                            if tb > 0:
                                keys.append((tb - 1, maskP))
                            keys.append((tb, None))
                            if tb < 15:
                                keys.append((tb + 1, maskN))
                            keys += [(16, None), (17, None)]
                        nk = len(keys)
                        qt, tqt = QT[tb % 4]

                        SB = [4, 5, 3]

                        def s_exp(kv, idx):
                            rows = slice(kv * 64, (kv + 1) * 64)
                            kb, mask = keys[idx]
                            sp_ = SB[idx % 3]
                            pt, tpt = PT[kv * 5 + idx]
                            S.op("pe", lambda: nc.tensor.matmul(ps[sp_][:, 0:512], lhsT=kT[rows, kb * 128:(kb + 1) * 128], rhs=qt[rows, :], start=True, stop=(mask is None)),
                                 reads=[kT_t[kb], tqt], writes=[pst[sp_]])
                            if mask is not None:
                                S.op("pe", lambda: nc.tensor.matmul(ps[sp_][:, 0:512], lhsT=ident[:], rhs=mask[:], start=False, stop=True),
                                     reads=[t_const, t_mask], writes=[pst[sp_]])
                            S.op("act", lambda: nc.scalar.activation(out=pt[:], in_=ps[sp_][:, 0:512], func=AF.Exp, scale=0.125), reads=[pst[sp_]], writes=[tpt])

                        def den_mm(kv, idx):
                            pt, tpt = PT[kv * 5 + idx]
                            S.op("pe", lambda: nc.tensor.matmul(ps[6][:, 0:512], lhsT=ones_b[:], rhs=pt[:], start=(idx == 0), stop=(idx == nk - 1)),
                                 reads=[tpt, t_const], writes=[pst[6]])

                        def o_mm(kv):
                            for hh in range(4):
                                j = 2 * kv + hh // 2
                                half = slice((hh % 2) * 64, (hh % 2) * 64 + 64)
                                for idx, (kb, mask) in enumerate(keys):
                                    pt, tpt = PT[kv * 5 + idx]
                                    S.op("pe", (lambda kb=kb, pt=pt, idx=idx, j=j, half=half, hh=hh: nc.tensor.matmul(
                                        ps[7][half, j * 128:(j + 1) * 128], lhsT=Vv[:, kb, kv * 64:(kv + 1) * 64], rhs=pt[:, hh * 128:(hh + 1) * 128],
                                        start=(idx == 0), stop=(idx == nk - 1))),
                                        reads=[V_t[kb], tpt], writes=[pst[7]])

                        def den_fin(kv):
                            ds, tds = DS[kv]
                            rd, trd = RD[0]
                            S.op("dve", lambda: nc.vector.tensor_tensor(
                                out=ds[:].rearrange("p (h q) -> p h q", h=4), in0=ps[6][:, 0:512].rearrange("p (h q) -> p h q", h=4),
                                in1=esk[:, 4 * kv:4 * kv + 4].unsqueeze(2).broadcast_to([128, 4, 128]), op=ALU.add),
                                reads=[pst[6], t_lp], writes=[tds])
                            S.op("act", lambda: nc.scalar.activation(out=ds[:], in_=ds[:], func=AF.Ln), reads=[tds], writes=[tds])
                            S.op("act", lambda: nc.scalar.activation(out=ds[:], in_=ds[:], func=AF.Exp, scale=-1.0), reads=[tds], writes=[tds])

                        def o_fin(kv):
                            ds, tds = DS[kv]
                            rd4 = ds[:].rearrange("p (a b q) -> p a b q", a=2, b=2)
                            for hb in range(2):
                                half = slice(hb * 64, hb * 64 + 64)
                                S.op("dve", (lambda hb=hb, half=half: nc.vector.tensor_tensor(
                                    out=yv[half, 4 + 2 * kv:4 + 2 * kv + 2, tb * 128:(tb + 1) * 128],
                                    in0=ps[7][half, 2 * kv * 128:(2 * kv + 2) * 128].rearrange("p (a q) -> p a q", a=2),
                                    in1=rd4[half, :, hb, :], op=ALU.mult)),
                                    reads=[pst[7], tds], writes=[yT[4 + 2 * kv][tb], yT[4 + 2 * kv + 1][tb]])

                        def s_phase(kv, first, last_):
                            for idx in range(first, last_):
                                s_exp(kv, idx)

                        LA = 2
                        for idx in range(min(LA, nk)):
                            s_exp(0, idx)
                        for idx in range(nk):
                            if idx + LA < nk:
                                s_exp(0, idx + LA)
                            den_mm(0, idx)
                        den_fin(0)
                        if hook is not None:
                            hook()
                        for idx in range(min(LA, nk)):
                            s_exp(1, idx)
                        o_mm(0)
                        o_fin(0)
                        for idx in range(nk):
                            if idx + LA < nk:
                                s_exp(1, idx + LA)
                            den_mm(1, idx)
                        den_fin(1)
                        o_mm(1)
                        o_fin(1)

                    seq = []
                    for ti in tiles1:
                        t0, w, isctx = TILES[ti]
                        for sbk in range(w // 128):
                            seq.append((ti, sbk, t0 // 128 + sbk))
                    normed = set()

                    def do_norm(ti):
                        if ti in normed:
                            return
                        normed.add(ti)
                        zs = tiles1.index(ti) % 2
                        w = TILES[ti][1]
                        emit_norm(l, 1, jf, [ti], None, None, tmp,
                                  zdst=(lambda c, ti_, zs=zs, w=w: (ZTv[zs][:, c, 0:w], ZT_t[zs][c])), pbf=(lambda ti_: 3))

                    projected = set()
                    done_attn = set()
                    attn_wanted = set(range(16)) | (set() if last else {16, 17})

                    def ready_blocks():
                        out = []
                        for tb in sorted(attn_wanted - done_attn, key=lambda x: (x < 16, x)):
                            need = {16, 17} if tb >= 16 else ({tb, 16, 17} | ({tb - 1} if tb > 0 else set()) | ({tb + 1} if tb < 15 else set()))
                            if need <= projected:
                                out.append(tb)
                        return out

                    for n, (ti, sbk, tb) in enumerate(seq):
                        do_norm(ti)
                        if sbk == 1 or TILES[ti][1] == 256:
                            nxt = tiles1.index(ti) + 1
                            if nxt < len(tiles1) and sbk >= (1 if TILES[ti][1] > 256 else 1):
                                do_norm(tiles1[nxt])
                        need_q = not (TILES[ti][2] and last)
                        part2 = proj_block(ti, sbk, tb, need_q, n % 2)
                        rbs = ready_blocks()[:2]
                        if not rbs:
                            part2()
                        for i_, rb in enumerate(rbs):
                            attn_block(rb, hook=(part2 if i_ == 0 else None))
                            done_attn.add(rb)
                        trans_block(tb, need_q, n % 2)
                        projected.add(tb)
                    for rb in ready_blocks():
                        attn_block(rb)
                        done_attn.add(rb)
                    assert done_attn == attn_wanted
                    S.barrier()

                with ExitStack() as p2:
                    cur2 = [p2]

                    def sb2(name, shape, dt):
                        return cur2[0].enter_context(nc.sbuf_tensor(uname(name), list(shape), dt))
                    Z = sb2("Zm", [128, NCH * T], BF16)
                    zv = Z[:].rearrange("p (c t) -> p c t", c=NCH)
                    zT = [[TT() for _ in TILES] for _ in range(NCH)]
                    pw = ExitStack()
                    cur2[0] = pw
                    lnB = sb2("lnB", [128, 512], F32)
                    wsT = sb2("wsT", [128, 512], BF16)
                    bsB = sb2("bsB", [128, 256], F32)
                    t_lp2 = TT()
                    S.op("sp", lambda: nc.sync.dma_start(out=lnB[:], in_=lnB_d[l]), writes=[t_lp2], dma=d_misc)
                    S.op("pool", lambda: nc.gpsimd.dma_start(out=wsT[:], in_=wsT_d[l]), writes=[t_lp2], dma=d_misc2)
                    S.op("sp", lambda: nc.sync.dma_start(out=bsB[:], in_=bsB_d[l]), writes=[t_lp2], dma=d_misc)
                    W1 = sb2("W1", [128, NCH * 1280], BF16)
                    W1v = W1[:].rearrange("p (k n) -> p k n", k=NCH)
                    t_W1 = TT()
                    S.op("pool", lambda: nc.gpsimd.dma_start(out=W1v, in_=w_in[l].rearrange("(k p) n -> p k n", p=128)[:, :, 0:1280]),
                         writes=[t_W1], dma=d_w[1])
                    pn = ExitStack()
                    cur2[0] = pw
                    tmp = {
                        "sq": [(sb2("sq%d" % k, [128, 512], BF16), TT()) for k in range(2)],
                        "t": [(sb2("tt%d" % k, [128, 512], F32), TT()) for k in range(2)],
                        "sd": (sb2("sd", [128, 512], F32), TT()),
                        "rstd": [(sb2("rstd0", [128, 512], F32), TT())] * 2,
                        "eps": epsb,
                    }
                    UU = [(sb2("U%d" % k, [128, 2 * 512], BF16), TT()) for k in range(2)]
                    GB = []
                    for p_ in range(2):
                        GB.append({
                            "gv": (sb2("gv%d" % p_, [128, 256], F32), TT()),
                            "bst": (sb2("bst%d" % p_, [128, 6], F32), TT()),
                            "mv": (sb2("mvv%d" % p_, [128, 2], F32), TT()),
                            "sdl": (sb2("sdl%d" % p_, [128, 1], F32), TT()),
                            "rsl": (sb2("rsl%d" % p_, [128, 1], F32), TT()),
                            "vn": (sb2("vn%d" % p_, [128, 256], F32), TT()),
                            "vnb": (sb2("vnb%d" % p_, [128, 256], BF16), TT()),
                            "stmp": (sb2("stmp%d" % p_, [128, 256], F32), TT()),
                        })
                    hal = sb2("hal", [128, 8], F32); t_hal = TT()
                    tbuf = sb2("tbuf", [128, 2 * 514], F32); tbv = tbuf[:].rearrange("p (c t) -> p c t", c=2); t_tb = TT()
                    acc = sb2("acc", [128, 512], F32); t_acc = TT()
                    cwv = cw_s[:].rearrange("p (l c k) -> p l c k", l=DEPTH, c=2)

                    def gu_tile(ti, up):
                        t0, w, isctx = TILES[ti]
                        U, t_U = UU[up]
                        Uv = U[:].rearrange("p (c t) -> p c t", c=2)
                        for cc in range(2):
                            for k in range(NCH):
                                S.op("pe", (lambda k=k, cc=cc: nc.tensor.matmul(ps[cc][:, 0:w], lhsT=W1v[:, k, OFF_GU + cc * 128:OFF_GU + (cc + 1) * 128],
                                                                               rhs=zv[:, k, t0:t0 + w], start=(k == 0), stop=(k == NCH - 1))),
                                     reads=[t_W1, zT[k][ti]], writes=[pst[cc]])
                            S.op("act", (lambda cc=cc: nc.scalar.activation(out=Uv[:, cc, 0:w], in_=ps[cc][:, 0:w], func=AF.Gelu_apprx_tanh)),
                                 reads=[pst[cc]], writes=[t_U])

                    def stage1(ti, sbk, par):
                        t0, w, isctx = TILES[ti]
                        tok = slice(t0 + sbk * 128, t0 + (sbk + 1) * 128)
                        B_ = GB[par]
                        gv, t_gv = B_["gv"]; bst, t_bst = B_["bst"]; mvv, t_mv = B_["mv"]; sdl, t_sdl = B_["sdl"]; rsl, t_rsl = B_["rsl"]
                        vn, t_vn = B_["vn"]; vnb, t_vnb = B_["vnb"]
                        for k in range(NCH):
                            S.op("pe", (lambda k=k: nc.tensor.matmul(ps[2][:, 0:256], lhsT=zv[:, k, tok], rhs=W1v[:, k, OFF_GV:OFF_GV + 256],
                                                                     start=(k == 0), stop=(k == NCH - 1))),
                                 reads=[t_W1, zT[k][ti]], writes=[pst[2]])
                        S.op("act", lambda: nc.scalar.activation(out=gv[:], in_=ps[2][:, 0:256], func=AF.Gelu_apprx_tanh), reads=[pst[2]], writes=[t_gv])
                        S.op("dve", lambda: nc.vector.bn_stats(out=bst[:], in_=gv[:]), reads=[t_gv], writes=[t_bst])
                        S.op("dve", lambda: nc.vector.bn_aggr(out=mvv[:], in_=bst[:]), reads=[t_bst], writes=[t_mv])
                        S.op("act", lambda: nc.scalar.activation(out=sdl[:], in_=mvv[:, 1:2], func=AF.Ln, bias=epsb[:, 0:1], scale=1.0),
                             reads=[t_mv, t_const], writes=[t_sdl])
                        S.op("act", lambda: nc.scalar.activation(out=rsl[:], in_=sdl[:], func=AF.Exp, scale=-0.5), reads=[t_sdl], writes=[t_rsl])
                        S.op("dve", lambda: nc.vector.tensor_scalar(out=vn[:], in0=gv[:], scalar1=mvv[:, 0:1], scalar2=rsl[:, 0:1], op0=ALU.subtract, op1=ALU.mult),
                             reads=[t_gv, t_mv, t_rsl], writes=[t_vn])
                        S.op("pool", lambda: nc.gpsimd.tensor_tensor(out=vn[:], in0=vn[:], in1=lnB[:, 0:256], op=ALU.mult), reads=[t_vn, t_lp2], writes=[t_vn])
                        S.op("pool", lambda: nc.gpsimd.tensor_tensor(out=vnb[:], in0=vn[:], in1=lnB[:, 256:512], op=ALU.add), reads=[t_vn, t_lp2], writes=[t_vnb])

                    def stage2(ti, sbk, par, up):
                        t0, w, isctx = TILES[ti]
                        tb = t0 // 128 + sbk
                        tok = slice(t0 + sbk * 128, t0 + (sbk + 1) * 128)
                        B_ = GB[par]
                        vnb, t_vnb = B_["vnb"]; stmp, t_stmp = B_["stmp"]
                        U, t_U = UU[up]
                        Uv = U[:].rearrange("p (c t) -> p c t", c=2)
                        for g in range(4):
                            half = slice((g % 2) * 64, (g % 2) * 64 + 64)
                            S.op("pe", (lambda g=g, half=half: nc.tensor.matmul(ps[3][half, (g // 2) * 128:(g // 2 + 1) * 128], lhsT=vnb[:, g * 64:(g + 1) * 64],
                                                                                 rhs=wsT[:, g * 128:(g + 1) * 128], start=True, stop=True)),
                                 reads=[t_vnb, t_lp2], writes=[pst[3]])
                        S.op("dve", lambda: nc.vector.tensor_tensor(out=stmp[:], in0=ps[3][:, 0:256], in1=bsB[:], op=ALU.add), reads=[pst[3], t_lp2], writes=[t_stmp])
                        S.op("dve", lambda: nc.vector.tensor_tensor(
                            out=yv[:, 2:4, tok], in0=stmp[:].rearrange("p (c t) -> p c t", c=2), in1=Uv[:, :, sbk * 128:(sbk + 1) * 128], op=ALU.mult),
                            reads=[t_stmp, t_U], writes=[yT[2][tb], yT[3][tb]])

                    def conv_mm1(ti):
                        t0, w, isctx = TILES[ti]
                        has_l = t0 not in (0, SEQ)
                        has_r = (t0 + w) not in (SEQ, T)
                        for cc in range(2):
                            for k in range(NCH):
                                S.op("pe", (lambda k=k, cc=cc: nc.tensor.matmul(ps[4 + cc][:, 0:w], lhsT=W1v[:, k, OFF_CC + cc * 128:OFF_CC + (cc + 1) * 128],
                                                                               rhs=zv[:, k, t0:t0 + w], start=(k == 0), stop=(k == NCH - 1))),
                                     reads=[t_W1, zT[k][ti]], writes=[pst[4 + cc]])
                            S.op("act", (lambda cc=cc: nc.scalar.copy(out=tbv[:, cc, 1:w + 1], in_=ps[4 + cc][:, 0:w])), reads=[pst[4 + cc]], writes=[t_tb])
                        for cc in range(2):
                            for k in range(NCH):
                                S.op("pe", (lambda k=k, cc=cc: nc.tensor.matmul(ps[6 + cc][:, 0:w], lhsT=W1v[:, k, OFF_CH + cc * 128:OFF_CH + (cc + 1) * 128],
                                                                               rhs=zv[:, k, t0:t0 + w], start=(k == 0), stop=(k == NCH - 1))),
                                     reads=[t_W1, zT[k][ti]], writes=[pst[6 + cc]])
                            S.op("dve", (lambda cc=cc: nc.vector.tensor_tensor(out=tbv[:, cc, 1:w + 1], in0=tbv[:, cc, 1:w + 1], in1=ps[6 + cc][:, 0:w], op=ALU.mult)),
                                 reads=[t_tb, pst[6 + cc]], writes=[t_tb])
                        sides = []
                        if has_l:
                            sides.append((0, t0 - 1, ti - 1))
                        if has_r:
                            sides.append((1, t0 + w, ti + 1))
                        for side, col, nti in sides:
                            for which, off in ((0, OFF_CC), (1, OFF_CH)):
                                for cc in range(2):
                                    pc = side * 4 + which * 2 + cc
                                    for k in range(NCH):
                                        S.op("pe", (lambda k=k, cc=cc, off=off, col=col, pc=pc: nc.tensor.matmul(
                                            ps[2][:, 256 + pc:256 + pc + 1], lhsT=W1v[:, k, off + cc * 128:off + (cc + 1) * 128], rhs=zv[:, k, col:col + 1],
                                            start=(k == 0), stop=(k == NCH - 1))),
                                            reads=[t_W1, zT[k][nti]], writes=[pst[2]])
                        if sides:
                            S.op("act", lambda: nc.scalar.copy(out=hal[:], in_=ps[2][:, 256:264]), reads=[pst[2]], writes=[t_hal])
                        for side in (0, 1):
                            present = any(s_[0] == side for s_ in sides)
                            colt = 0 if side == 0 else w + 1
                            if present:
                                S.op("dve", (lambda side=side, colt=colt: nc.vector.tensor_tensor(
                                    out=tbv[:, :, colt], in0=hal[:, side * 4:side * 4 + 2], in1=hal[:, side * 4 + 2:side * 4 + 4], op=ALU.mult)),
                                    reads=[t_hal], writes=[t_tb])
                            else:
                                S.op("dve", (lambda colt=colt: nc.vector.memset(tbv[:, :, colt:colt + 1], 0.0)), writes=[t_tb])

                    def conv_mm2(ti):
                        t0, w, isctx = TILES[ti]
                        for cc in range(2):
                            for k in range(NCH):
                                S.op("pe", (lambda k=k, cc=cc: nc.tensor.matmul(ps[4 + cc][:, 0:w], lhsT=W1v[:, k, OFF_CB + cc * 128:OFF_CB + (cc + 1) * 128],
                                                                               rhs=zv[:, k, t0:t0 + w], start=(k == 0), stop=(k == NCH - 1))),
                                     reads=[t_W1, zT[k][ti]], writes=[pst[4 + cc]])
                            S.op("dve", (lambda cc=cc: nc.vector.tensor_scalar(out=acc[:, 0:w], in0=tbv[:, cc, 1:w + 1], scalar1=cwv[:, l, cc, 1:2], scalar2=None, op0=ALU.mult)),
                                 reads=[t_tb, t_const], writes=[t_acc])
                            S.op("dve", (lambda cc=cc: nc.vector.scalar_tensor_tensor(out=acc[:, 0:w], in0=tbv[:, cc, 0:w], scalar=cwv[:, l, cc, 0:1], in1=acc[:, 0:w],
                                                                                      op0=ALU.mult, op1=ALU.add)), reads=[t_tb, t_acc, t_const], writes=[t_acc])
                            S.op("dve", (lambda cc=cc: nc.vector.scalar_tensor_tensor(out=acc[:, 0:w], in0=tbv[:, cc, 2:w + 2], scalar=cwv[:, l, cc, 2:3], in1=acc[:, 0:w],
                                                                                      op0=ALU.mult, op1=ALU.add)), reads=[t_tb, t_acc, t_const], writes=[t_acc])
                            S.op("dve", (lambda cc=cc: nc.vector.tensor_tensor(out=yv[:, cc, t0:t0 + w], in0=acc[:, 0:w], in1=ps[4 + cc][:, 0:w], op=ALU.mult)),
                                 reads=[t_acc, pst[4 + cc]], writes=[yT[cc][tb_] for tb_ in range(t0 // 128, (t0 + w) // 128)])

                    norm_done = []

                    def norm_tile(idx_):
                        if idx_ < len(tiles2) and idx_ not in norm_done:
                            norm_done.append(idx_)
                            emit_norm(l, 1, jf, [tiles2[idx_]], zv, zT, tmp, pbf=(lambda ti_: 3))

                    nblk = 0
                    for pos, ti in enumerate(tiles2):
                        norm_tile(pos)
                        norm_tile(pos + 1)
                        for nb_ in (ti - 1, ti + 1):
                            if nb_ in tiles2:
                                norm_tile(tiles2.index(nb_))
                        up = pos % 2
                        gu_tile(ti, up)
                        nb = TILES[ti][1] // 128
                        pend = None
                        conv_steps = [conv_mm1, conv_mm2]
                        for sbk in range(nb):
                            stage1(ti, sbk, nblk % 2)
                            if pend is not None:
                                stage2(*pend)
                            if sbk < len(conv_steps):
                                conv_steps[sbk](ti)
                            pend = (ti, sbk, nblk % 2, up)
                            nblk += 1
                        stage2(*pend)
                    if cfg.get("dump_y") and not dumped:
                        dumped.append(1)
                        S.op("pool", lambda: nc.gpsimd.dma_start(out=dbgY, in_=Y[:]), reads=[t for row in yT for t in row], dma=d_out)
                        S.op("pool", lambda: nc.gpsimd.dma_start(out=dbgZ, in_=Z[:]), reads=[t for row in zT for t in row], dma=d_out)
                    S.barrier()
                    pw.close()
                    cur2[0] = p2
                    WM = [sb2("WM%d" % s_, [128, 5120], BF16) for s_ in range(4)]
                    t_WM = [TT(), TT(), TT(), TT()]
                    d_wm = d_w + d_w2
                    SG = [(sb2("sg%d" % k, [128, 512], F32), TT()) for k in range(3)]
                    M1 = sb2("m1", [128, 512], F32); t_m1 = TT()
                    M2 = sb2("m2", [128, 512], F32); t_m2 = TT()
                    M3 = sb2("m3", [128, 512], F32); t_m3 = TT()
                    MG = [(sb2("mg%d" % k, [128, 512], BF16), TT()) for k in range(4)]
                    bgv = bg_s[:].rearrange("p (l r c) -> p l r c", l=DEPTH, r=3)
                    win_v = w_in[l].rearrange("(k p) n -> p k n", p=128)

                    def load_c(c):
                        s_ = c % 4
                        wg = WM[s_][:, 0:3072].rearrange("p (k r n) -> p k r n", k=NCH, r=3)
                        wb = WM[s_][:, 3072:4096].rearrange("p (k n) -> p k n", k=8)
                        wo = WM[s_][:, 4096:5120]
                        for r in range(3):
                            S.op("pool", (lambda r=r: nc.gpsimd.dma_start(out=wg[:, :, r, :], in_=win_v[:, :, OFF_GATE + r * D + c * 128:OFF_GATE + r * D + (c + 1) * 128])),
                                 writes=[t_WM[s_]], dma=d_wm[s_])
                        S.op("pool", lambda: nc.gpsimd.dma_start(out=wb[:, 0:2, :], in_=w_bc[l].rearrange("(k p) n -> p k n", p=128)[:, :, c * 128:(c + 1) * 128]),
                             writes=[t_WM[s_]], dma=d_wm[s_])
                        S.op("pool", lambda: nc.gpsimd.dma_start(out=wb[:, 2:4, :], in_=w_bg[l].rearrange("(k p) n -> p k n", p=128)[:, :, c * 128:(c + 1) * 128]),
                             writes=[t_WM[s_]], dma=d_wm[s_])
                        S.op("pool", lambda: nc.gpsimd.dma_start(out=wb[:, 4:8, :], in_=w_ba[l].rearrange("(k p) n -> p k n", p=128)[:, :, c * 128:(c + 1) * 128]),
                             writes=[t_WM[s_]], dma=d_wm[s_])
                        S.op("pool", lambda: nc.gpsimd.dma_start(out=wo, in_=w_o[l][c * 128:(c + 1) * 128, :]), writes=[t_WM[s_]], dma=d_wm[s_])
                        return wg, wb, wo

                    wv_ = {0: load_c(0), 1: load_c(1)}

                    def emit_gate(c, ti, par):
                        s_ = c % 4
                        wg, wb, wo = wv_[c]
                        t0, w, isctx = TILES[ti]
                        blks = range(t0 // 128, (t0 + w) // 128)
                        for r in range(3):
                            for k in range(NCH):
                                S.op("pe", (lambda r=r, k=k: nc.tensor.matmul(ps[r % 2][:, 0:w], lhsT=wg[:, k, r, :], rhs=zv[:, k, t0:t0 + w], start=(k == 0), stop=(k == NCH - 1))),
                                     reads=[t_WM[s_], zT[k][ti]], writes=[pst[r % 2]])
                            sg, tsg = SG[r]
                            S.op("act", (lambda r=r, sg=sg: nc.scalar.activation(out=sg[:, 0:w], in_=ps[r % 2][:, 0:w], func=AF.Sigmoid, bias=bgv[:, l, r, c:c + 1], scale=1.0)),
                                 reads=[pst[r % 2], t_const], writes=[tsg])

                    def emit_branch(c, ti, par):
                        s_ = c % 4
                        wg, wb, wo = wv_[c]
                        t0, w, isctx = TILES[ti]
                        blks = range(t0 // 128, (t0 + w) // 128)
                        kr = [(0, 2), (2, 4), (4, 8)]
                        MM = [(M1, t_m1), (M2, t_m2), (M3, t_m3)]
                        for r in range(3):
                            k0, k1 = kr[r]
                            pb_ = 2 + r % 2
                            for kk in range(k0, k1):
                                S.op("pe", (lambda kk=kk, k0=k0, k1=k1, pb_=pb_: nc.tensor.matmul(ps[pb_][:, 0:w], lhsT=wb[:, kk, :], rhs=yv[:, kk, t0:t0 + w],
                                                                                             start=(kk == k0), stop=(kk == k1 - 1))),
                                     reads=[t_WM[s_]] + [yT[kk][tb_] for tb_ in blks], writes=[pst[pb_]])
                            mm_, tmm_ = MM[r]
                            S.op("dve", (lambda r=r, pb_=pb_, mm_=mm_: nc.vector.tensor_tensor(out=mm_[:, 0:w], in0=SG[r][0][:, 0:w], in1=ps[pb_][:, 0:w], op=ALU.mult)),
                                 reads=[SG[r][1], pst[pb_]], writes=[tmm_])
                        mg, tmg = MG[par * 2 + c % 2]
                        S.op("pool", lambda: nc.gpsimd.tensor_tensor(out=M1[:, 0:w], in0=M1[:, 0:w], in1=M2[:, 0:w], op=ALU.add), reads=[t_m1, t_m2], writes=[t_m1])
                        S.op("pool", lambda: nc.gpsimd.tensor_tensor(out=mg[:, 0:w], in0=M1[:, 0:w], in1=M3[:, 0:w], op=ALU.add), reads=[t_m1, t_m3], writes=[tmg])

                    def emit_out(c0, ti, par):
                        t0, w, isctx = TILES[ti]
                        j = jf(ti)
                        for c2 in range(NCH):
                            po = 4 + c2 % 4
                            for cc in range(2):
                                c = c0 + cc
                                s_ = c % 4
                                wo = wv_[c][2]
                                mg, tmg = MG[par * 2 + cc]
                                S.op("pe", (lambda c2=c2, po=po, wo=wo, mg=mg, cc=cc: nc.tensor.matmul(ps[po][:, 0:w], lhsT=wo[:, c2 * 128:(c2 + 1) * 128], rhs=mg[:, 0:w],
                                                                                               start=(cc == 0), stop=(cc == 1))),
                                     reads=[t_WM[s_], tmg], writes=[pst[po]])
                            S.op("dve", (lambda c2=c2, po=po: nc.vector.scalar_tensor_tensor(
                                out=Hv[:, c2, t0:t0 + w], in0=ps[po][:, 0:w], scalar=mcol(l, 5, c2, j), in1=Hv[:, c2, t0:t0 + w], op0=ALU.mult, op1=ALU.add)),
                                reads=[pst[po], hT[c2][ti], t_mod], writes=[hT[c2][ti]])

                    steps = [(c0, ti) for c0 in range(0, NCH, 2) for ti in tiles2]
                    prev = None
                    for n, (c0, ti) in enumerate(steps):
                        emit_gate(c0, ti, n % 2)
                        if prev is not None:
                            emit_out(*prev)
                        emit_branch(c0, ti, n % 2)
                        if ti == tiles2[0] and c0 + 2 < NCH:
                            wv_[c0 + 2] = load_c(c0 + 2)
                            wv_[c0 + 3] = load_c(c0 + 3)
                        emit_gate(c0 + 1, ti, n % 2)
                        emit_branch(c0 + 1, ti, n % 2)
                        prev = (c0, ti, n % 2)
                    emit_out(*prev)
                    S.barrier()

        for b in range(nb):
            for c in range(NCH):
                S.op("sp", (lambda c=c, b=b: nc.sync.dma_start(out=Hv[:, c, 0:SEQ], in_=xT[b, c * 128:(c + 1) * 128, :])),
                     writes=[hT[c][ti] for ti in range(4)], dma=d_x)
                S.op("sp", (lambda c=c, b=b: nc.sync.dma_start(out=Hv[:, c, SEQ:T], in_=ctxT[b, c * 128:(c + 1) * 128, :])),
                     writes=[hT[c][4]], dma=d_x)
            for l in layers:
                last = (l == DEPTH - 1)
                if "ffn1" in subs:
                    emit_ffn(l, 0, b, [4, 0, 1, 2, 3])
                if "mix" in subs:
                    emit_mixer(l, b, last)
                if "ffn2" in subs:
                    emit_ffn(l, 1, b, [0, 1, 2, 3] if last else [4, 0, 1, 2, 3])
            for c in range(NCH):
                S.op("sp", (lambda c=c, b=b: nc.sync.dma_start(out=yT[b, c * 128:(c + 1) * 128, :], in_=Hv[:, c, 0:SEQ])),
                     reads=[hT[c][ti] for ti in range(4)], dma=d_out)
            S.barrier()
        S.barrier()
        print("instructions:", S.n_instr)
    return nc


def prep_inputs(inp, nb=NB, ncores=NCORES):
    f = np.float32
    x = np.asarray(inp["x"], f)
    ctx = np.asarray(inp["ctx"], f)
    c = np.asarray(inp["c"], f)
    c_ctx = np.asarray(inp["c_ctx"], f)
    shared = {
        "bmodT": np.ascontiguousarray(np.asarray(inp["b_mod"], f).reshape(DEPTH, 72, 128).transpose(2, 0, 1).reshape(128, DEPTH * 72)),
        "ngT": np.ascontiguousarray(np.asarray(inp["norm_g"], f).reshape(DEPTH, 3, NCH, 128).transpose(3, 0, 1, 2).reshape(128, -1)),
        "w_mod": np.ascontiguousarray(np.asarray(inp["w_mod"], f)),
        "ffn_w_in": np.ascontiguousarray(np.asarray(inp["ffn_w_in"], f)),
        "ffn_w_out": np.ascontiguousarray(np.asarray(inp["ffn_w_out"], f)),
    }
    L = DEPTH
    shared["w_in"] = np.ascontiguousarray(np.asarray(inp["w_in"], f))
    shared["w_bc"] = np.ascontiguousarray(np.asarray(inp["w_branch_conv"], f))
    shared["w_bg"] = np.ascontiguousarray(np.asarray(inp["w_branch_gmlp"], f))
    shared["w_ba"] = np.ascontiguousarray(np.asarray(inp["w_branch_attn"], f))
    shared["w_o"] = np.ascontiguousarray(np.asarray(inp["w_out"], f))
    shared["bgT"] = np.ascontiguousarray(np.asarray(inp["b_gate"], f).reshape(L, 3, NCH, 128).transpose(3, 0, 1, 2).reshape(128, -1))
    shared["cwT"] = np.ascontiguousarray(np.asarray(inp["conv_w"], f).reshape(L, 3, 2, 128).transpose(3, 0, 2, 1).reshape(128, -1))
    lng = np.asarray(inp["gmlp_ln_g"], f)
    lnb = np.asarray(inp["gmlp_ln_b"], f)
    shared["lnB"] = np.ascontiguousarray(np.broadcast_to(np.concatenate([lng, lnb], axis=1)[:, None, :], (L, 128, 512)))
    ws = np.asarray(inp["gmlp_ws"], f)
    shared["wsT"] = np.ascontiguousarray(ws.transpose(0, 3, 1, 2).reshape(L, 128, 512))
    bs = np.asarray(inp["gmlp_bs"], f)
    shared["bsB"] = np.ascontiguousarray(np.repeat(bs.reshape(L, 2, 2, 1, 128), 64, axis=3).transpose(0, 2, 3, 1, 4).reshape(L, 128, 256))
    qg = np.asarray(inp["q_norm_g"], f)
    kg = np.asarray(inp["k_norm_g"], f)
    qkg = np.concatenate([np.tile(qg, (1, 8)), np.tile(kg, (1, 2))], axis=1)
    shared["qkg"] = np.ascontiguousarray(np.broadcast_to(qkg[:, None, :], (L, 128, 640)))
    shared["sinkB"] = np.ascontiguousarray(np.broadcast_to(np.asarray(inp["attn_sink"], f)[:, None, :], (L, 128, 8)))
    shared.update(_const_tables())
    maps = []
    for core in range(ncores):
        bs = slice(core * nb, (core + 1) * nb)
        cc = np.concatenate([c[bs], c_ctx[None, :]], axis=0)
        if nb < 4:
            cc = np.concatenate([c[bs], np.zeros((4 - nb, D), f), c_ctx[None, :]], axis=0)
        cT = np.ascontiguousarray(cc.reshape(5, NCH, 128).transpose(2, 1, 0).reshape(128, NCH * 5))
        m = dict(shared)
        m["xT"] = np.ascontiguousarray(x[bs].transpose(0, 2, 1))
        m["ctxT"] = np.ascontiguousarray(ctx[bs].transpose(0, 2, 1))
        m["cT"] = cT
        maps.append(m)
    return maps


def _const_tables():
    f = np.float32
    pos = np.arange(SEQ)
    r = (pos // 64).astype(f)
    col = (pos % 64).astype(f)
    half = 32
    inv = (np.float32(10000.0) ** (-np.arange(0, half, 2, dtype=f) / half)).astype(f)
    ang_r = r[:, None] * inv[None, :]
    ang_c = col[:, None] * inv[None, :]
    ang = np.concatenate([ang_r, ang_r, ang_c, ang_c], axis=-1).astype(f)
    cos = np.cos(ang).astype(f)
    sin = np.sin(ang).astype(f)
    sgn = np.tile(np.concatenate([-np.ones(16, f), np.ones(16, f)]), 2)
    sin_s = sin * sgn[None, :]
    ropeC = cos.reshape(16, 128, 64).transpose(1, 0, 2).reshape(128, 16 * 64)
    ropeS = sin_s.reshape(16, 128, 64).transpose(1, 0, 2).reshape(128, 16 * 64)
    j = np.arange(128)[:, None]
    i = np.arange(128)[None, :]
    return {
        "ropeC": np.ascontiguousarray(ropeC), "ropeS": np.ascontiguousarray(ropeS),
        "maskP": np.tile(np.where(j >= i, 0.0, -30000.0).astype(f), (1, 4)), "maskN": np.tile(np.where(j <= i, 0.0, -30000.0).astype(f), (1, 4)),
        "ident": np.eye(128, dtype=f),
    }


_NC_CACHE = {}


def kernel(**inputs):
    cfg = {"nb": NB}
    key = "full"
    if key not in _NC_CACHE:
        _NC_CACHE[key] = build_nc(cfg)
    nc = _NC_CACHE[key]
    maps = prep_inputs(inputs)
    res = run_bass_kernel_spmd(nc, maps, core_ids=list(range(NCORES)))
    outs = [np.asarray(r["yT"]).transpose(0, 2, 1) for r in res.results]
    return np.ascontiguousarray(np.concatenate(outs, axis=0).astype(np.float32))
```
